# Optimizing a Trainium2 kernel written in Bass

```python
import jax, jax.numpy as jnp
from jax import lax
import numpy as np

D_MODEL = 1024
BATCH = 16
SEQ = 4096
DEPTH = 2

N_MIXERS = 2
N_A = (DEPTH + 1) // 2
N_B = DEPTH // 2

CHUNK = 128
SGU_WIDTH = D_MODEL
SGU_GROUPS = 8
SGU_GROUP_DIM = SGU_WIDTH // SGU_GROUPS

RWKV_HEAD = 64
RWKV_HEADS = D_MODEL // RWKV_HEAD
DECAY_LORA = 64
AAA_LORA = 64
GATE_LORA = 160

D_FF = 4 * D_MODEL
N_MOD = 6

RMS_EPS = 1e-6
LN_EPS = 1e-5
GN_EPS = RWKV_HEAD * 1e-5
L2_EPS = 1e-12

kernel_name = "hybrid_sgu_rwkv7_adaln_trunk"


def rms_norm(x):
    xf = x.astype(jnp.float32)
    y = xf * lax.rsqrt(jnp.mean(xf * xf, axis=-1, keepdims=True) + RMS_EPS)
    return y.astype(x.dtype)


def layer_norm(x, g, b):
    xf = x.astype(jnp.float32)
    mu = jnp.mean(xf, axis=-1, keepdims=True)
    var = jnp.mean(jnp.square(xf - mu), axis=-1, keepdims=True)
    y = (xf - mu) * lax.rsqrt(var + LN_EPS)
    return y.astype(x.dtype) * g + b


def modulate(h, shift, scale):
    return h * (1 + scale[:, None, :]) + shift[:, None, :]


def token_shift(x):
    return jnp.pad(x[:, :-1], ((0, 0), (1, 0), (0, 0)))


def sgu_mixer(h, w_in, ln_g, ln_b, w_s, b_s, w_out):
    B, T, _ = h.shape
    uv = jax.nn.gelu(h @ w_in, approximate=False)
    u, v = jnp.split(uv, 2, axis=-1)
    v = layer_norm(v, ln_g, ln_b)
    vc = v.reshape(B, T // CHUNK, CHUNK, SGU_GROUPS, SGU_GROUP_DIM)
    mask = jnp.tril(jnp.ones((CHUNK, CHUNK), dtype=w_s.dtype))
    sv = jnp.einsum('gts,bcsgd->bctgd', w_s * mask, vc) + b_s.T[:, :, None]
    return (u * sv.reshape(B, T, SGU_WIDTH)) @ w_out


def wkv7_scan(r, w, k, v, a_, b_):
    B, T, H, N = r.shape
    seq = tuple(jnp.swapaxes(z.astype(jnp.float32), 0, 1) for z in (r, w, k, v, a_, b_))

    def step(S, inp):
        r_t, w_t, k_t, v_t, a_t, b_t = inp
        sa = jnp.einsum('bhij,bhj->bhi', S, a_t)
        S = S * w_t[:, :, None, :] + sa[..., None] * b_t[:, :, None, :] + v_t[..., None] * k_t[:, :, None, :]
        y = jnp.einsum('bhij,bhj->bhi', S, r_t)
        return S, y

    S0 = jnp.zeros((B, H, N, N), jnp.float32)
    _, ys = lax.scan(step, S0, seq)
    return jnp.swapaxes(ys, 0, 1).astype(r.dtype)


def rwkv7_mixer(h, mu, w_in, w0, w1, w2, a0, a1, a2, g1, g2, k_k, k_a, r_k, ln_g, ln_b, w_out):
    B, T, D = h.shape
    H, N = RWKV_HEADS, RWKV_HEAD
    xx = token_shift(h) - h
    xr, xw, xk, xv, xa, xg = [h + xx * mu[i] for i in range(6)]
    rkv = jnp.einsum('nbtd,dne->nbte', jnp.stack([xr, xk, xv]), w_in.reshape(D, 3, D))
    r, k, v = rkv[0], rkv[1], rkv[2]
    w_log = -jax.nn.softplus(-(w0 + jnp.tanh(xw @ w1) @ w2)) - 0.5
    decay = jnp.exp(-jnp.exp(w_log))
    a = jax.nn.sigmoid(a0 + (xa @ a1) @ a2)
    g = jax.nn.sigmoid(xg @ g1) @ g2
    kk = (k * k_k).reshape(B, T, H, N)
    kkf = kk.astype(jnp.float32)
    kk = (kkf / jnp.maximum(jnp.sqrt(jnp.sum(kkf * kkf, axis=-1, keepdims=True)), L2_EPS)).astype(h.dtype)
    k = k * (1 + (a - 1) * k_a)
    rh = r.reshape(B, T, H, N)
    kh = k.reshape(B, T, H, N)
    vh = v.reshape(B, T, H, N)
    ah = a.reshape(B, T, H, N)
    y = wkv7_scan(rh, decay.reshape(B, T, H, N), kh, vh, -kk, kk * ah)
    yf = y.astype(jnp.float32)
    m = jnp.mean(yf, axis=-1, keepdims=True)
    var = jnp.mean(jnp.square(yf - m), axis=-1, keepdims=True)
    y = ((yf - m) * lax.rsqrt(var + GN_EPS)).astype(h.dtype).reshape(B, T, D) * ln_g + ln_b
    bonus = jnp.sum(rh * kh * r_k, axis=-1, keepdims=True) * vh
    y = y + bonus.reshape(B, T, D)
    return (y * g) @ w_out


def setup_inputs(seed: int = 0) -> dict:
    key = jax.random.key(seed)
    ks = iter(jax.random.split(key, 48))
    D = D_MODEL

    def nrm(shape, scale):
        return jax.random.normal(next(ks), shape, jnp.float32) * scale

    def unif(shape, lo, hi):
        return jax.random.uniform(next(ks), shape, jnp.float32, lo, hi)

    return {
        "x": nrm((BATCH, SEQ, D), 1.0),
        "c": nrm((BATCH, D), 1.0),
        "ada_w": nrm((DEPTH, D, N_MOD * D), 0.3 * D ** -0.5),
        "ada_b": nrm((DEPTH, N_MOD * D), 0.02),
        "mlp_w1": nrm((DEPTH, D, D_FF), D ** -0.5),
        "mlp_w2": nrm((DEPTH, D_FF, D), D_FF ** -0.5),
        "a_w_in": nrm((N_A, D, 2 * SGU_WIDTH), D ** -0.5),
        "a_ln_g": 1.0 + nrm((N_A, SGU_WIDTH), 0.02),
        "a_ln_b": nrm((N_A, SGU_WIDTH), 0.02),
        "a_w_s": nrm((N_A, SGU_GROUPS, CHUNK, CHUNK), CHUNK ** -0.5),
        "a_b_s": 1.0 + nrm((N_A, SGU_GROUPS, CHUNK), 0.02),
        "a_w_out": nrm((N_A, SGU_WIDTH, D), SGU_WIDTH ** -0.5),
        "b_mu": unif((N_B, 6, D), 0.0, 1.0),
        "b_w_in": nrm((N_B, D, 3 * D), D ** -0.5),
        "b_w0": unif((N_B, D), -7.0, -2.0),
        "b_w1": nrm((N_B, D, DECAY_LORA), D ** -0.5),
        "b_w2": nrm((N_B, DECAY_LORA, D), 0.1 * DECAY_LORA ** -0.5),
        "b_a0": nrm((N_B, D), 0.5),
        "b_a1": nrm((N_B, D, AAA_LORA), D ** -0.5),
        "b_a2": nrm((N_B, AAA_LORA, D), 0.5 * AAA_LORA ** -0.5),
        "b_g1": nrm((N_B, D, GATE_LORA), D ** -0.5),
        "b_g2": nrm((N_B, GATE_LORA, D), GATE_LORA ** -0.5),
        "b_k_k": 0.85 + nrm((N_B, D), 0.02),
        "b_k_a": 1.0 + nrm((N_B, D), 0.02),
        "b_r_k": -0.04 + nrm((N_B, RWKV_HEADS, RWKV_HEAD), 0.02),
        "b_ln_g": 1.0 + nrm((N_B, D), 0.02),
        "b_ln_b": nrm((N_B, D), 0.02),
        "b_w_out": nrm((N_B, D, D), D ** -0.5),
        "final_g": 1.0 + nrm((D,), 0.02),
    }


def reference(x, c, ada_w, ada_b, mlp_w1, mlp_w2,
              a_w_in, a_ln_g, a_ln_b, a_w_s, a_b_s, a_w_out,
              b_mu, b_w_in, b_w0, b_w1, b_w2, b_a0, b_a1, b_a2, b_g1, b_g2,
              b_k_k, b_k_a, b_r_k, b_ln_g, b_ln_b, b_w_out, final_g):
    cond = jax.nn.silu(c)
    for i in range(DEPTH):
        mod = cond @ ada_w[i] + ada_b[i]
        shift1, scale1, gate1, shift2, scale2, gate2 = jnp.split(mod, N_MOD, axis=-1)
        h = modulate(rms_norm(x), shift1, scale1)
        j = i // N_MIXERS
        if i % N_MIXERS == 0:
            mix = sgu_mixer(h, a_w_in[j], a_ln_g[j], a_ln_b[j], a_w_s[j], a_b_s[j], a_w_out[j])
        else:
            mix = rwkv7_mixer(h, b_mu[j], b_w_in[j], b_w0[j], b_w1[j], b_w2[j],
                              b_a0[j], b_a1[j], b_a2[j], b_g1[j], b_g2[j],
                              b_k_k[j], b_k_a[j], b_r_k[j], b_ln_g[j], b_ln_b[j], b_w_out[j])
        x = x + gate1[:, None, :] * mix
        h = modulate(rms_norm(x), shift2, scale2)
        ff = jnp.square(jax.nn.relu(h @ mlp_w1[i])) @ mlp_w2[i]
        x = x + gate2[:, None, :] * ff
    return rms_norm(x) * final_g
```

```python
import os
import numpy as np
from contextlib import ExitStack
import concourse.bass as bass
import concourse.mybir as mybir
from concourse.bass_utils import run_bass_kernel_spmd

F32 = mybir.dt.float32
BF16 = mybir.dt.bfloat16
AF = mybir.ActivationFunctionType
ALU = mybir.AluOpType
AX = mybir.AxisListType

D = 1024
DC = 8
FF = 4096
TT = 512
CH = 128
NH = 16
DECAY_C = 0.6065306597126334


class Buf:
    __slots__ = ("name", "w", "r")

    def __init__(self, name):
        self.name = name
        self.w = None
        self.r = {}


class Sched:
    NDMA = 24

    def __init__(self, nc, stack):
        self.nc = nc
        self.eng = {"pe": nc.tensor, "dve": nc.vector, "act": nc.scalar,
                    "pool": nc.gpsimd, "sp": nc.sync}
        self.sem = {}
        for k in self.eng:
            self.sem[k] = stack.enter_context(nc.semaphore("s_" + k))
        self.cnt = {k: 0 for k in self.eng}
        self.waited = {k: {} for k in self.eng}
        for j in range(self.NDMA):
            key = "d%d" % j
            self.sem[key] = stack.enter_context(nc.semaphore("s_" + key))
            self.cnt[key] = 0
        self.dma_i = 0
        self.sw_i = 0

    def _wait(self, e, deps):
        eng = self.eng[e]
        wd = self.waited[e]
        for key, val in deps.items():
            if key == "pe" and e == "pe":
                continue
            if wd.get(key, 0) < val:
                eng.wait_ge(self.sem[key], val)
                wd[key] = val

    @staticmethod
    def _add(deps, tok):
        if tok is None:
            return
        k, v = tok
        if deps.get(k, 0) < v:
            deps[k] = v

    def _collect(self, reads, writes):
        deps = {}
        for b in reads:
            self._add(deps, b.w)
        for b in writes:
            self._add(deps, b.w)
            for k, v in b.r.items():
                self._add(deps, (k, v))
        return deps

    def _commit(self, tok, reads, writes):
        k, v = tok
        for b in reads:
            if b.r.get(k, 0) < v:
                b.r[k] = v
        for b in writes:
            b.w = tok
            b.r = {}

    def op(self, e, fn, reads=(), writes=()):
        deps = self._collect(reads, writes)
        self._wait(e, deps)
        ins = fn(self.eng[e])
        self.cnt[e] += 1
        ins.then_inc(self.sem[e], 1)
        self._commit((e, self.cnt[e]), reads, writes)
        return ins

    def dma(self, e, out, in_, reads=(), writes=(), **kw):
        half = self.NDMA // 2
        if e == "pool":
            key = "d%d" % (self.sw_i % half)
            self.sw_i += 1
        else:
            key = "d%d" % (half + self.dma_i % half)
            self.dma_i += 1
        deps = self._collect(reads, writes)
        if self.cnt[key] > 0:
            self._add(deps, (key, self.cnt[key]))
        self._wait(e, deps)
        ins = self.eng[e].dma_start(out=out, in_=in_, **kw)
        self.cnt[key] += 16
        ins.then_inc(self.sem[key], 16)
        self._commit((key, self.cnt[key]), reads, writes)
        return ins

    def finish(self, e):
        for j in range(self.NDMA):
            k = "d%d" % j
            if self.cnt[k]:
                self._wait(e, {k: self.cnt[k]})


class _Stop(Exception):
    pass


def build(T=4096, NB=2, stop=None):
    nc = bass.Bass("TRN2", target_bir_lowering=False)
    NT = T // TT

    def din(name, shape):
        return nc.dram_tensor(name, list(shape), F32, kind="ExternalInput").ap()

    x_d = din("x", [NB, T, D])
    c_d = din("c", [NB, D])
    ada_w = din("ada_w", [2, D, 6 * D])
    ada_b = din("ada_b", [2, 6 * D])
    mlp_w1 = din("mlp_w1", [2, D, FF])
    mlp_w2 = din("mlp_w2", [2, FF, D])
    a_w_in = din("a_w_in", [D, 2 * D])
    a_ln_g = din("a_ln_g", [D])
    a_ln_b = din("a_ln_b", [D])
    a_w_s = din("a_w_s", [8, CH, CH])
    a_b_s = din("a_b_s", [8 * CH])
    a_w_out = din("a_w_out", [D, D])
    b_mu = din("b_mu", [6, D])
    b_w_in = din("b_w_in", [D, 3 * D])
    b_w0 = din("b_w0", [D])
    b_w1 = din("b_w1", [D, 64])
    b_w2 = din("b_w2", [64, D])
    b_a0 = din("b_a0", [D])
    b_a1 = din("b_a1", [D, 64])
    b_a2 = din("b_a2", [64, D])
    b_g1 = din("b_g1", [D, 160])
    b_g2 = din("b_g2", [160, D])
    b_k_k = din("b_k_k", [D])
    b_k_a = din("b_k_a", [D])
    b_r_k = din("b_r_k", [D])
    b_ln_g = din("b_ln_g", [D])
    b_ln_b = din("b_ln_b", [D])
    b_w_out = din("b_w_out", [D, D])
    final_g = din("final_g", [D])
    out_d = nc.dram_tensor("out", [NB, T, D], F32, kind="ExternalOutput").ap()

    def scratch(name, nsl):
        return nc.dram_tensor(name, [nsl, 128, 2048], BF16, kind="Internal").ap()

    sc_w1 = [scratch("sc_w1_%d" % l, 16) for l in range(2)]
    sc_w2 = [scratch("sc_w2_%d" % l, 16) for l in range(2)]
    sc_awin = scratch("sc_awin", 8)
    sc_awout = scratch("sc_awout", 4)
    sc_bwin = scratch("sc_bwin", 12)
    sc_bwout = scratch("sc_bwout", 4)

    with ExitStack() as st:
      S = Sched(nc, st)
      dbg_d = nc.dram_tensor("dbg", [128, 4096], F32, kind="ExternalOutput").ap() if stop else None

      def chk(name, dump=None):
          if stop == name:
              if dump is not None:
                  ap, bf, ncol = dump
                  S.dma("sp", dbg_d[:, 0:ncol], ap, reads=[bf])
              raise _Stop()
      try:

            def tile(name, shape, dt):
                return st.enter_context(nc.sbuf_tensor(name, list(shape), dt)), Buf(name)

            ps = st.enter_context(nc.psum_tensor("ps", [128, 8, 512], F32))
            bps = [Buf("ps%d" % i) for i in range(8)]
            pstate = {"i": 0}

            held = set()

            def pb1():
                while True:
                    k = pstate["i"] % 8
                    pstate["i"] += 1
                    if k not in held:
                        return k

            def pb2():
                while True:
                    if pstate["i"] % 2:
                        pstate["i"] += 1
                    k = pstate["i"] % 8
                    pstate["i"] += 2
                    if k not in held and (k + 1) not in held:
                        return k

            identf, b_identf = tile("identf", [128, 128], F32)
            identb, b_identb = tile("identb", [128, 128], BF16)
            onesb, b_onesb = tile("onesb", [128, 128], BF16)
            onesf, b_onesf = tile("onesf", [128, 128], F32)
            tri, b_tri = tile("tri", [128, 3, 128], F32)
            negc, b_negc = tile("negc", [128, 2], F32)
            mask4, b_mask4 = tile("mask4", [128, 4, 128], BF16)
            maskL, b_maskL = tile("maskL", [128, 4, 128], BF16)
            lnb_rms, b_lnbr = tile("lnb_rms", [128, 4], F32)
            cst = [b_identf, b_identb, b_onesb, b_onesf, b_tri, b_negc, b_mask4, b_maskL, b_lnbr]

            def sel(out, in_, cm, pat, base):
                S.op("pool", lambda e: e.affine_select(out=out, in_=in_, compare_op=ALU.is_ge, fill=0.0,
                                                        base=base, pattern=[[pat, 128]], channel_multiplier=cm),
                     writes=cst, reads=cst)

            S.op("pool", lambda e: e.memset(identf[:], 0.0), writes=cst)
            S.op("pool", lambda e: e.affine_select(out=identf[:], in_=identf[:], compare_op=ALU.not_equal, fill=1.0,
                                                    base=0, pattern=[[-1, 128]], channel_multiplier=1), writes=cst)
            S.op("pool", lambda e: e.tensor_copy(identb[:], identf[:]), writes=cst)
            S.op("pool", lambda e: e.memset(onesb[:], 1.0), writes=cst)
            S.op("pool", lambda e: e.memset(onesf[:], 1.0), writes=cst)
            S.op("pool", lambda e: e.memset(negc[:], -DECAY_C), writes=cst)
            for i_, v_ in enumerate((D * 1e-6, 1e-5, 1e-24, 64e-5)):
                S.op("pool", lambda e: e.memset(lnb_rms[:, i_:i_ + 1], v_), writes=cst)
            S.op("pool", lambda e: e.memset(tri[:], -DECAY_C), writes=cst)
            sel(tri[:, 0, :], tri[:, 0, :], -1, 1, 0)
            sel(tri[:, 1, :], tri[:, 1, :], -1, 1, -1)
            sel(tri[:, 2, :], tri[:, 2, :], 1, -1, -1)
            S.op("pool", lambda e: e.memset(mask4[:], 1.0), writes=cst)
            S.op("pool", lambda e: e.memset(maskL[:], 1.0), writes=cst)
            for k in range(4):
                sel(mask4[:, k, :], mask4[:, k, :], -1, 1, -1 if k % 2 == 0 else 0)
            for k in range(4):
                sel(maskL[:, k, :], maskL[:, k, :], 1, -1, -1)

            xres, b_x = tile("xres", [128, DC, TT], F32)
            hb, b_h = tile("hb", [128, DC, TT], BF16)
            carry, b_carry = tile("carry", [128, DC, 2], BF16)
            rstd, b_rstd = tile("rstd", [128, TT], F32)
            ntmp = [tile("ntmp%d" % i, [128, TT], F32) for i in range(2)]
            slab = [tile("slab%d" % i, [128, DC, TT], BF16) for i in range(2)]
            sl1f = slab[1][0][:].rearrange("p c t -> p (c t)")
            sl1 = [sl1f[:, k * 1024:(k + 1) * 1024] for k in range(4)]
            b_sl1 = [Buf("sl1_%d" % k) for k in range(4)]
            NTMF = 6
            tmf = [tile("tmf%d" % i, [128, D], F32) for i in range(NTMF)]
            NTMB = 5
            tmb = [tile("tmb%d" % i, [128, D], BF16) for i in range(NTMB)]
            NWS = 4
            wslot = [tile("wslot%d" % i, [128, 2048], BF16) for i in range(NWS)]
            wstate = {"i": 0}
            rstate = {"i": 0}

            scrb = {}

            def sbuf_of(scr, s):
                key = (scr.name if hasattr(scr, "name") else id(scr), s)
                if key not in scrb:
                    scrb[key] = Buf("scr")
                return scrb[key]

            def wload(src):
                scr, s = src
                k = wstate["i"] % NWS
                wstate["i"] += 1
                t, b = wslot[k]
                S.dma("sp", t[:], scr[s], writes=[b], reads=[sbuf_of(scr, s)])
                pump(2)
                return t, b

            st16 = [tile("st16_%d" % i, [128, 16], F32) for i in range(6)]
            modf, b_mod = tile("modf", [128, 2, 6, DC, NB], F32)
            scf, b_scf = tile("scf", [128, 2, 2, DC, NB], F32)
            adab, b_adab = tile("adab", [128, 2, 6, DC], F32)
            condT, b_cond = tile("condT", [128, DC, NB], BF16)
            cfm, b_cfm = tile("cfm", [128, DC, NB], F32)
            muf, b_muf = tile("muf", [128, 6, DC], F32)
            fgf, b_fgf = tile("fgf", [128, DC], F32)
            alng, b_alng = tile("alng", [128, DC], F32)
            alnb, b_alnb = tile("alnb", [128, DC], F32)
            wsT, b_wsT = tile("wsT", [128, 8, CH], BF16)
            csgu, b_csgu = tile("csgu", [128, 8, CH], F32)
            rows, b_rows = tile("rows", [128, 5, D], BF16)
            w0a0, b_w0a0 = tile("w0a0", [1, 2, D], F32)
            lw1, b_lw1 = tile("lw1", [128, DC, 64], BF16)
            la1, b_la1 = tile("la1", [128, DC, 64], BF16)
            lg1, b_lg1 = tile("lg1", [128, DC, 160], BF16)
            lw2, b_lw2 = tile("lw2", [64, D], BF16)
            la2, b_la2 = tile("la2", [64, D], BF16)
            lg2a, b_lg2a = tile("lg2a", [128, D], BF16)
            lg2b, b_lg2b = tile("lg2b", [32, D], BF16)
            xi = [tile("xi%d" % i, [128, DC, CH], BF16) for i in range(4)]
            xstate = {"i": 0}
            xxt, b_xx = tile("xxt", [128, DC, CH], BF16)
            lor, b_lor = tile("lor", [128, 4, CH], BF16)
            b_lorw, b_lora, b_lorg = Buf("lorw"), Buf("lora"), Buf("lorg")
            arfm, b_arfm = tile("arfm", [128, 8, 2, CH], BF16)
            kbfm, b_kbfm = tile("kbfm", [128, 8, 2, CH], BF16)
            MKB, b_MKB = tile("MKB", [128, NH, 4, CH], BF16)
            Q0, b_Q0 = tile("Q0", [128, NH, CH], BF16)
            Qt_, _ = tile("Qt", [128, NH, CH], BF16)
            PX = [tile("PX%d" % i, [128, NH, 2, CH], BF16)[0] for i in range(2)]
            bPX = [[Buf("PX%d_%d" % (i, g)) for g in range(4)] for i in range(2)]
            bQb = [[Buf("Q%d_%d" % (i, g)) for g in range(4)] for i in range(2)]
            Hf = [tile("Hf%d" % i, [128, 8, 64], F32) for i in range(NB)]
            Hbd = [tile("Hbd%d" % i, [128, 8, 128], BF16) for i in range(NB)]
            elc2 = [tile("elc%d" % i, [128, 8], F32) for i in range(2)]
            sbon2 = [tile("sbon%d" % i, [128, 16], F32) for i in range(2)]

            if stop:
                print('SBUF remaining', nc.sbuf_bytes_remaining)
            def fmvec(dst, bdst, src_ap):
                with nc.allow_non_contiguous_dma(reason="small per-feature vector to FM"):
                    S.dma("sp", dst, src_ap, writes=[bdst])

            for b in range(NB):
                fmvec(cfm[:, :, b], b_cfm, c_d[b].rearrange("(c p) -> p c", p=128))
            for m in range(6):
                fmvec(muf[:, m, :], b_muf, b_mu[m].rearrange("(c p) -> p c", p=128))
                for l in range(2):
                    fmvec(adab[:, l, m, :], b_adab, ada_b[l, m * D:(m + 1) * D].rearrange("(c p) -> p c", p=128))
            fmvec(fgf[:], b_fgf, final_g.rearrange("(c p) -> p c", p=128))
            fmvec(alng[:], b_alng, a_ln_g.rearrange("(c p) -> p c", p=128))
            fmvec(alnb[:], b_alnb, a_ln_b.rearrange("(c p) -> p c", p=128))
            S.op("dve", lambda e: e.tensor_scalar(fgf[:], fgf[:], 32.0, None, ALU.mult), reads=[b_fgf], writes=[b_fgf])
            S.dma("sp", w0a0[:, 0, :], b_w0.rearrange("(o n) -> o n", o=1), writes=[b_w0a0])
            S.dma("sp", w0a0[:, 1, :], b_a0.rearrange("(o n) -> o n", o=1), writes=[b_w0a0])
            for i, src in enumerate([b_k_k, b_k_a, b_r_k, b_ln_g, b_ln_b]):
                t, b = tmf[i % 2]
                S.dma("sp", t[:], src.partition_broadcast(128), writes=[b])
                S.op("act", lambda e: e.copy(rows[:, i, :], t[:]), reads=[b], writes=[b_rows])
            S.dma("pool", lw1[:], b_w1.rearrange("(c p) n -> p c n", p=128), writes=[b_lw1])
            S.dma("pool", la1[:], b_a1.rearrange("(c p) n -> p c n", p=128), writes=[b_la1])
            S.dma("pool", lg1[:], b_g1.rearrange("(c p) n -> p c n", p=128), writes=[b_lg1])
            S.dma("pool", lw2[:], b_w2[:, :], writes=[b_lw2])
            S.dma("pool", la2[:], b_a2[:, :], writes=[b_la2])
            S.dma("pool", lg2a[:], b_g2[0:128, :], writes=[b_lg2a])
            S.dma("pool", lg2b[:], b_g2[128:160, :], writes=[b_lg2b])

            chk('const', (rows[:, 0:2, :].rearrange('p a n -> p (a n)'), b_rows, 2048)) if False else chk('const')
            def prep_kmajor(dst, src, ncols):
                v = src.rearrange("(c p) (s n) -> s p c n", p=128, n=256)
                for s in range(ncols // 256):
                    k = wstate["i"] % NWS
                    wstate["i"] += 1
                    wt, wb_ = wslot[k]
                    S.dma("pool", wt[:].rearrange("p (c n) -> p c n", n=256), v[s], writes=[wb_])
                    S.dma("sp", dst[s], wt[:], reads=[wb_], writes=[sbuf_of(dst, s)])
                    yield

            def prep_w2(dst, src):
                v = src.rearrange("(pa s j p) (eh n) -> pa eh s p j n", pa=2, s=4, j=4, p=128, n=512)
                for pa in range(2):
                    for eh in range(2):
                        for s4 in range(4):
                            k = wstate["i"] % NWS
                            wstate["i"] += 1
                            wt, wb_ = wslot[k]
                            S.dma("pool", wt[:].rearrange("p (j n) -> p j n", n=512), v[pa, eh, s4], writes=[wb_])
                            S.dma("sp", dst[(pa * 2 + eh) * 4 + s4], wt[:], reads=[wb_],
                                  writes=[sbuf_of(dst, (pa * 2 + eh) * 4 + s4)])
                            yield

            for _ in prep_kmajor(sc_awin, a_w_in, 2048):
                pass
            for _ in prep_kmajor(sc_awout, a_w_out, 1024):
                pass

            def _pending():
                yield from prep_kmajor(sc_w1[0], mlp_w1[0], FF)
                yield from prep_w2(sc_w2[0], mlp_w2[0])
                yield from prep_kmajor(sc_bwin, b_w_in, 3072)
                yield from prep_kmajor(sc_bwout, b_w_out, 1024)
                yield from prep_kmajor(sc_w1[1], mlp_w1[1], FF)
                yield from prep_w2(sc_w2[1], mlp_w2[1])
            pend = {"g": _pending(), "alive": True}

            def pump(n):
                for _ in range(n):
                    if not pend["alive"]:
                        return
                    try:
                        next(pend["g"])
                    except StopIteration:
                        pend["alive"] = False

            chk('prologue')
            S.op("act", lambda e: e.activation(condT[:], cfm[:], AF.Silu), reads=[b_cfm], writes=[b_cond])
            for l in range(2):
                av = ada_w[l].rearrange("(c p) (s n) -> s p c n", p=128, n=256)
                for s in range(24):
                    k = wstate["i"] % NWS
                    wstate["i"] += 1
                    wt, wb_ = wslot[k]
                    S.dma("pool", wt[:].rearrange("p (c n) -> p c n", n=256), av[s], writes=[wb_])
                    bk = pb1()
                    for j in range(2):
                        for c in range(DC):
                            S.op("pe", lambda e: e.matmul(ps[:, bk, j * NB:(j + 1) * NB],
                                                          wt[:, c * 256 + j * 128: c * 256 + (j + 1) * 128],
                                                          condT[:, c, :], start=(c == 0), stop=(c == DC - 1)),
                                 reads=[wb_, b_cond], writes=[bps[bk]])
                    ec = s * 2
                    m, cc = ec // 8, ec % 8
                    S.op("dve", lambda e: e.tensor_tensor(
                        modf[:, l, m, cc:cc + 2, :], ps[:, bk, 0:2 * NB].rearrange("p (j b) -> p j b", b=NB),
                        adab[:, l, m, cc:cc + 2].unsqueeze(2).to_broadcast([128, 2, NB]), ALU.add),
                        reads=[bps[bk], b_adab], writes=[b_mod])
            for l in range(2):
                for k2 in range(2):
                    S.op("dve", lambda e: e.tensor_scalar(scf[:, l, k2], modf[:, l, 1 + 3 * k2], 1.0, 32.0, ALU.add, ALU.mult),
                         reads=[b_mod], writes=[b_scf])

            chk('ada', (modf[:].rearrange('p l m c b -> p (l m c b)'), b_mod, 2 * 6 * DC * NB))
            lnbs_t, b_lnbs = tmf[2]
            S.dma("sp", lnbs_t[:], a_b_s.partition_broadcast(128), writes=[b_lnbs])
            wtmp, b_wtmp = tmf[3]
            S.dma("sp", wtmp[:].rearrange("p (g s) -> p g s", s=CH), a_w_s.rearrange("g t s -> t g s"), writes=[b_wtmp])
            wm32, b_wm32 = tmf[4]
            for g in range(8):
                bk = pb1()
                S.op("pe", lambda e: e.transpose(ps[:, bk, 0:CH], wtmp[:, g * CH:(g + 1) * CH], identf[:]),
                     reads=[b_wtmp] + cst, writes=[bps[bk]])
                S.op("dve", lambda e: e.tensor_tensor(wm32[:, g * CH:(g + 1) * CH], ps[:, bk, 0:CH], mask4[:, 1, :], ALU.mult),
                     reads=[bps[bk]] + cst, writes=[b_wm32])
                S.op("act", lambda e: e.copy(wsT[:, g, :], wm32[:, g * CH:(g + 1) * CH]), reads=[b_wm32], writes=[b_wsT])
                bk2 = pb1()
                S.op("pe", lambda e: e.matmul(ps[:, bk2, 0:CH], onesf[:], wm32[:, g * CH:(g + 1) * CH], start=True, stop=True),
                     reads=[b_wm32] + cst, writes=[bps[bk2]])
                S.op("dve", lambda e: e.scalar_tensor_tensor(csgu[:, g, :], ps[:, bk2, 0:CH], alnb[:, g:g + 1],
                                                             lnbs_t[:, g * CH:(g + 1) * CH], ALU.mult, ALU.add),
                     reads=[bps[bk2], b_alnb, b_lnbs], writes=[b_csgu])

            chk('sguconst', (csgu[:].rearrange('p g t -> p (g t)'), b_csgu, 1024))
            def rms_mod(l, k2, b):
                sq, b_sq = slab[1]
                S.op("act", lambda e: e.activation(sq[:], xres[:], AF.Square), reads=[b_x], writes=[b_sq] + b_sl1)
                bk = pb1()
                for c in range(DC):
                    S.op("pe", lambda e: e.matmul(ps[:, bk, :], onesb[:], sq[:, c, :], start=(c == 0), stop=(c == DC - 1)),
                         reads=[b_sq] + b_sl1 + cst, writes=[bps[bk]])
                S.op("act", lambda e: e.activation(rstd[:], ps[:, bk, :], AF.Ln, bias=lnb_rms[:, 0:1], scale=1.0),
                     reads=[bps[bk]] + cst, writes=[b_rstd])
                S.op("act", lambda e: e.activation(rstd[:], rstd[:], AF.Exp, scale=-0.5), reads=[b_rstd], writes=[b_rstd])
                if l is None:
                    return
                for c in range(DC):
                    nt, b_nt = ntmp[c % 2]
                    S.op("dve", lambda e: e.scalar_tensor_tensor(nt[:], xres[:, c, :], scf[:, l, k2, c, b:b + 1], rstd[:],
                                                                 ALU.mult, ALU.mult),
                         reads=[b_x, b_scf, b_rstd], writes=[b_nt])
                    S.op("act", lambda e: e.activation(hb[:, c, :], nt[:], AF.Identity,
                                                       bias=modf[:, l, 3 * k2, c, b:b + 1], scale=1.0),
                         reads=[b_nt, b_mod], writes=[b_h])

            def resid_add(e_chunk, bk, l, k2, b, cols=slice(0, TT)):
                gate = modf[:, l, 2 + 3 * k2, e_chunk, b:b + 1]
                S.op("dve", lambda e: e.scalar_tensor_tensor(xres[:, e_chunk, cols], ps[:, bk, cols], gate,
                                                             xres[:, e_chunk, cols], ALU.mult, ALU.add),
                     reads=[bps[bk], b_mod, b_x], writes=[b_x])

            def proj_fm_out(scr, nsl, rhs_tile, b_rhs, sink):
                for s in range(nsl):
                    wt, wb_ = wload((scr, s))
                    for j in range(2):
                        bk = pb1()
                        for c in range(DC):
                            S.op("pe", lambda e: e.matmul(ps[:, bk, :], wt[:, c * 256 + j * 128: c * 256 + (j + 1) * 128],
                                                          rhs_tile[:, c, :], start=(c == 0), stop=(c == DC - 1)),
                                 reads=[wb_, b_rhs], writes=[bps[bk]])
                        sink(s * 2 + j, bk)

            def mlp(l, b):
                rms_mod(l, 1, b)
                hid, b_hid = slab[0]
                hid2, b_hid2 = slab[1]

                def hv(f):
                    return (hid if f < 8 else hid2)[:, f % 8, :], (b_hid if f < 8 else b_hid2)
                for pa in range(2):
                    for s in range(8):
                        wt, wb_ = wload((sc_w1[l], pa * 8 + s))
                        for j in range(2):
                            f = s * 2 + j
                            bk = pb1()
                            for c in range(DC):
                                S.op("pe", lambda e: e.matmul(ps[:, bk, :], wt[:, c * 256 + j * 128: c * 256 + (j + 1) * 128],
                                                              hb[:, c, :], start=(c == 0), stop=(c == DC - 1)),
                                     reads=[wb_, b_h], writes=[bps[bk]])
                            hap, hbuf = hv(f)
                            S.op("act", lambda e: e.activation(hap, ps[:, bk, :], AF.Relu), reads=[bps[bk]], writes=[hbuf])
                            S.op("pool", lambda e: e.tensor_tensor(hap, hap, hap, ALU.mult), reads=[hbuf], writes=[hbuf])
                    for eh in range(2):
                        bks = [pb1() for _ in range(4)]
                        for s4 in range(4):
                            wt, wb_ = wload((sc_w2[l], (pa * 2 + eh) * 4 + s4))
                            for j in range(4):
                                f = s4 * 4 + j
                                hap, hbuf = hv(f)
                                for e4 in range(4):
                                    S.op("pe", lambda e: e.matmul(ps[:, bks[e4], :],
                                                                  wt[:, j * 512 + e4 * 128: j * 512 + (e4 + 1) * 128],
                                                                  hap, start=(f == 0), stop=(f == 15)),
                                         reads=[wb_, hbuf], writes=[bps[bks[e4]]])
                        for e4 in range(4):
                            resid_add(eh * 4 + e4, bks[e4], l, 1, b)

            def sgu(b):
                l = 0
                rms_mod(l, 0, b)
                u, b_u = slab[0]
                def sink_u(ec, bk):
                    S.op("act", lambda e: e.activation(u[:, ec, :], ps[:, bk, :], AF.Gelu), reads=[bps[bk]], writes=[b_u])
                s1, b_s1 = st16[0]
                s2, b_s2 = st16[1]
                S.op("pool", lambda e: e.memset(s1[:], 0.0), writes=[b_s1])
                S.op("pool", lambda e: e.memset(s2[:], 0.0), writes=[b_s2])
                for s in range(4):
                    wt, wb_ = wload((sc_awin, 4 + s))
                    for q in range(4):
                        bk = pb1()
                        for c in range(DC):
                            S.op("pe", lambda e: e.matmul(ps[:, bk, 0:256], hb[:, c, q * CH:(q + 1) * CH],
                                                          wt[:, c * 256:(c + 1) * 256], start=(c == 0), stop=(c == DC - 1)),
                                 reads=[wb_, b_h], writes=[bps[bk]])
                        vt, b_vt = tmf[q]
                        S.op("act", lambda e: e.activation(vt[:, s * 256:(s + 1) * 256], ps[:, bk, 0:256], AF.Gelu,
                                                           accum_out=s1[:, q * 4 + s: q * 4 + s + 1]),
                             reads=[bps[bk]], writes=[b_vt, b_s1])
                        jt, b_jt = ntmp[0]
                        S.op("act", lambda e: e.activation(jt[:, 0:256], vt[:, s * 256:(s + 1) * 256], AF.Square,
                                                           accum_out=s2[:, q * 4 + s: q * 4 + s + 1]),
                             reads=[b_vt], writes=[b_jt, b_s2])
                proj_fm_out(sc_awin, 4, hb, b_h, sink_u)
                mean, b_mean = st16[2]
                ex2, b_ex2 = st16[3]
                rs, b_rs = st16[4]
                nb_, b_nb = st16[5]
                S.op("dve", lambda e: e.reduce_sum(mean[:, 0:4], s1[:].rearrange("p (q s) -> p q s", s=4), axis=AX.X),
                     reads=[b_s1], writes=[b_mean])
                S.op("dve", lambda e: e.reduce_sum(ex2[:, 0:4], s2[:].rearrange("p (q s) -> p q s", s=4), axis=AX.X),
                     reads=[b_s2], writes=[b_ex2])
                S.op("dve", lambda e: e.tensor_scalar(mean[:, 0:4], mean[:, 0:4], 1.0 / D, None, ALU.mult), reads=[b_mean], writes=[b_mean])
                S.op("dve", lambda e: e.tensor_tensor(rs[:, 0:4], mean[:, 0:4], mean[:, 0:4], ALU.mult), reads=[b_mean], writes=[b_rs])
                S.op("dve", lambda e: e.scalar_tensor_tensor(rs[:, 0:4], ex2[:, 0:4], 1.0 / D, rs[:, 0:4], ALU.mult, ALU.subtract),
                     reads=[b_ex2, b_rs], writes=[b_rs])
                S.op("act", lambda e: e.activation(rs[:, 0:4], rs[:, 0:4], AF.Ln, bias=lnb_rms[:, 1:2], scale=1.0), reads=[b_rs] + cst, writes=[b_rs])
                S.op("act", lambda e: e.activation(rs[:, 0:4], rs[:, 0:4], AF.Exp, scale=-0.5), reads=[b_rs], writes=[b_rs])
                S.op("dve", lambda e: e.scalar_tensor_tensor(nb_[:, 0:4], mean[:, 0:4], -1.0, rs[:, 0:4], ALU.mult, ALU.mult),
                     reads=[b_mean, b_rs], writes=[b_nb])
                for q in range(4):
                    vt, b_vt = tmf[q]
                    vn, b_vn = tmb[q % 2]
                    S.op("act", lambda e: e.activation(vn[:], vt[:], AF.Identity, bias=nb_[:, q:q + 1], scale=rs[:, q:q + 1]),
                         reads=[b_vt, b_rs, b_nb], writes=[b_vn])
                    for half in range(2):
                        bk = pb1()
                        for g4 in range(4):
                            g = half * 4 + g4
                            S.op("pe", lambda e: e.matmul(ps[:, bk, g4 * CH:(g4 + 1) * CH], vn[:, g * CH:(g + 1) * CH],
                                                          wsT[:, g, :], start=True, stop=True),
                                 reads=[b_vn, b_wsT], writes=[bps[bk]])
                        tt_, b_tt = ntmp[half]
                        for g4 in range(4):
                            g = half * 4 + g4
                            S.op("dve", lambda e: e.scalar_tensor_tensor(tt_[:, g4 * CH:(g4 + 1) * CH], ps[:, bk, g4 * CH:(g4 + 1) * CH],
                                                                         alng[:, g:g + 1], csgu[:, g, :], ALU.mult, ALU.add),
                                 reads=[bps[bk], b_alng, b_csgu], writes=[b_tt])
                        uv = u[:, half * 4:(half + 1) * 4, q * CH:(q + 1) * CH]
                        S.op("dve", lambda e: e.tensor_tensor(uv, tt_[:].rearrange("p (g t) -> p g t", t=CH), uv, ALU.mult),
                             reads=[b_tt, b_u], writes=[b_u])
                proj_fm_out(sc_awout, 4, u, b_u, lambda ec, bk: resid_add(ec, bk, l, 0, b))

            def rwkv(b, first_tile):
                l = 1
                rms_mod(l, 0, b)
                yg, b_yg = slab[0]
                hf_t, b_hf = Hf[b]
                hbd_t, b_hbd = Hbd[b]
                A, b_A = tmf[0]
                Fa, b_F = tmf[1]
                G, b_G = tmf[2]
                E1, b_E1 = tmf[3]
                E2, b_E2 = tmf[4]
                SW, b_SW = tmf[5]
                Kp, b_Kp = E2, b_E2
                khat, b_khat = tmb[0]
                bhat, b_bhat = tmb[1]
                vbf, b_vbf = tmb[2]
                tA, b_tA = tmb[3]
                tB, b_tB = tmb[4]
                ubf, b_ubf = tB, b_tB
                rhsbf, b_rhsbf = tA, b_tA
                ss, b_ss = st16[0]
                rn, b_rn = st16[1]
                g1s, b_g1s = st16[3]
                g2s, b_g2s = st16[4]
                g3s, b_g3s = st16[5]

                px0 = PX[0][:].rearrange("p h k t -> p (h k t)")
                px1 = PX[1][:].rearrange("p h k t -> p (h k t)")
                ring = [(wslot[i][0][:], [wslot[i][1]]) for i in range(NWS)] + [
                    (px0[:, 0:2048], [bPX[0][0], bPX[0][1]]), (px0[:, 2048:4096], [bPX[0][2], bPX[0][3]]),
                    (Q0[:].rearrange("p h t -> p (h t)"), [b_Q0] + bQb[0]),
                    (Qt_[:].rearrange("p h t -> p (h t)"), list(bQb[1]))]

                def h3(ap):
                    return ap.rearrange("p (h j) -> p h j", j=64)

                def bc16(ap16):
                    return ap16.unsqueeze(2).to_broadcast([128, NH, 64])

                def mix(i, q):
                    slot = {1: 0, 4: 1, 5: 2, 2: 3, 0: 4, 3: 5}[i]
                    if slot < 4:
                        t, bt = xi[slot]
                        tv = t[:]
                    else:
                        tv, bt = sl1[slot - 4].rearrange("p (c t) -> p c t", t=CH), b_sl1[slot - 4]
                    S.op("pool", lambda e: e.tensor_tensor(tv, xxt[:], muf[:, i, :].unsqueeze(2).to_broadcast([128, DC, CH]), ALU.mult),
                         reads=[b_xx, b_muf], writes=[bt])
                    S.op("dve", lambda e: e.tensor_tensor(tv, tv, hb[:, :, q * CH:(q + 1) * CH], ALU.add),
                         reads=[bt, b_h], writes=[bt])
                    return tv, bt

                def proj_tm(xt_, bxt_, sl0, bk2, extra=None):
                    for s in range(4):
                        if extra == "noring":
                            wt_, wb1 = wload((sc_bwin, sl0 + s))
                            wt, wbl = wt_[:], [wb1]
                        else:
                            wt, wbl = ring[rstate["i"] % len(ring)]
                            rstate["i"] += 1
                            S.dma("sp", wt, sc_bwin[sl0 + s], writes=wbl, reads=[sbuf_of(sc_bwin, sl0 + s)])
                        k = bk2 + s // 2
                        cols = slice((s % 2) * 256, (s % 2) * 256 + 256)
                        for c in range(DC):
                            S.op("pe", lambda e: e.matmul(ps[:, k, cols], xt_[:, c, :], wt[:, c * 256:(c + 1) * 256],
                                                          start=(c == 0), stop=(c == DC - 1)),
                                 reads=wbl + [bxt_], writes=[bps[k]])

                def transpose8(src, bsrc, dst_fn, bdst):
                    bk = pb1()
                    psb = ps[:, bk, :].bitcast(BF16)
                    for c in range(8):
                        S.op("pe", lambda e: e.transpose(psb[:, c * CH:(c + 1) * CH], src[:, c * CH:(c + 1) * CH], identb[:]),
                             reads=[bsrc] + cst, writes=[bps[bk]])
                    S.op("act", lambda e: e.copy(dst_fn, psb.rearrange("p (c t) -> p c t", t=CH)), reads=[bps[bk]], writes=[bdst])

                rbf, b_rbf = sl1[2], b_sl1[2]
                gbf, b_gbf = sl1[3], b_sl1[3]

                def genA(q):
                    q0 = q * CH
                    elc, b_elc = elc2[q % 2]
                    sbon, b_sbon = sbon2[q % 2]
                    if q == 0:
                        if first_tile:
                            S.op("pool", lambda e: e.memset(carry[:], 0.0), writes=[b_carry])
                        S.op("dve", lambda e: e.tensor_tensor(xxt[:, :, 0:1], carry[:, :, 0:1], hb[:, :, 0:1], ALU.subtract),
                             reads=[b_carry, b_h], writes=[b_xx])
                        S.op("dve", lambda e: e.tensor_tensor(xxt[:, :, 1:CH], hb[:, :, 0:CH - 1], hb[:, :, 1:CH], ALU.subtract),
                             reads=[b_h], writes=[b_xx])
                    else:
                        S.op("dve", lambda e: e.tensor_tensor(xxt[:], hb[:, :, q0 - 1:q0 + CH - 1], hb[:, :, q0:q0 + CH], ALU.subtract),
                             reads=[b_h], writes=[b_xx])
                    if q == 3:
                        S.op("pool", lambda e: e.tensor_copy(carry[:, :, 0:1], hb[:, :, TT - 1:TT]), reads=[b_h, b_xx], writes=[b_carry])
                    xk, bxk = mix(2, q)
                    xa, bxa = mix(4, q)
                    b_lw, b_la, b_lg = b_lorw, b_lora, b_lorg
                    bkk = pb2()
                    proj_tm(xk, bxk, 4, bkk)
                    kps = ps[:, bkk:bkk + 2, :].rearrange("p a n -> p (a n)")
                    bkps = [bps[bkk], bps[bkk + 1]]
                    held.update((bkk, bkk + 1))
                    S.op("dve", lambda e: e.tensor_tensor(G[:], kps, rows[:, 0, :], ALU.mult), reads=bkps + [b_rows], writes=[b_G])
                    S.op("act", lambda e: e.activation(SW[:], G[:], AF.Square), reads=[b_G], writes=[b_SW])
                    S.op("dve", lambda e: e.reduce_sum(ss[:], h3(SW[:]), axis=AX.X), reads=[b_SW], writes=[b_ss])
                    S.op("act", lambda e: e.activation(rn[:], ss[:], AF.Ln, bias=lnb_rms[:, 2:3], scale=1.0), reads=[b_ss] + cst, writes=[b_rn])
                    S.op("act", lambda e: e.activation(rn[:], rn[:], AF.Exp, scale=-0.5), reads=[b_rn], writes=[b_rn])
                    S.op("dve", lambda e: e.tensor_tensor(h3(G[:]), h3(G[:]), bc16(rn[:]), ALU.mult), reads=[b_G, b_rn], writes=[b_G])
                    yield
                    xw, bxw = mix(1, q)
                    xr, bxr = mix(0, q)

                    def zproj(which, dst, bdst, l2, blor):
                        bk2 = pb2()
                        for hlf in range(2):
                            cs = slice(hlf * 512, (hlf + 1) * 512)
                            S.op("pe", lambda e: e.matmul(ps[:, bk2 + hlf, :], onesf[0:1, :], w0a0[0:1, which, cs], start=True, stop=False),
                                 reads=[b_w0a0] + cst, writes=[bps[bk2 + hlf]])
                            S.op("pe", lambda e: e.matmul(ps[:, bk2 + hlf, :], lor[0:64, which, :], l2[:, cs], start=False, stop=True),
                                 reads=[blor, b_lw2, b_la2], writes=[bps[bk2 + hlf]])
                        S.op("act", lambda e: e.activation(dst[:], ps[:, bk2:bk2 + 2, :].rearrange("p a n -> p (a n)"), AF.Sigmoid),
                             reads=[bps[bk2], bps[bk2 + 1]], writes=[bdst])
                    bkl = pb1()
                    for c in range(DC):
                        S.op("pe", lambda e: e.matmul(ps[0:64, bkl, 0:CH], la1[:, c, :], xa[:, c, :], start=(c == 0), stop=(c == DC - 1)),
                             reads=[b_la1, bxa], writes=[bps[bkl]])
                    S.op("act", lambda e: e.copy(lor[0:64, 1, :], ps[0:64, bkl, 0:CH]), reads=[bps[bkl]], writes=[b_la])
                    zproj(1, Fa, b_F, la2, b_la)
                    S.op("dve", lambda e: e.scalar_tensor_tensor(A[:], Fa[:], -1.0, rows[:, 1, :], ALU.add, ALU.mult), reads=[b_F, b_rows], writes=[b_A])
                    S.op("dve", lambda e: e.scalar_tensor_tensor(A[:], A[:], 1.0, kps, ALU.add, ALU.mult), reads=bkps + [b_A], writes=[b_A])
                    held.difference_update((bkk, bkk + 1))
                    S.op("dve", lambda e: e.tensor_tensor(Fa[:], G[:], Fa[:], ALU.mult), reads=[b_G, b_F], writes=[b_F])
                    bkl = pb1()
                    for c in range(DC):
                        S.op("pe", lambda e: e.matmul(ps[0:64, bkl, 0:CH], lw1[:, c, :], xw[:, c, :], start=(c == 0), stop=(c == DC - 1)),
                             reads=[b_lw1, bxw], writes=[bps[bkl]])
                    S.op("act", lambda e: e.activation(lor[0:64, 0, :], ps[0:64, bkl, 0:CH], AF.Tanh), reads=[bps[bkl]], writes=[b_lw])
                    zproj(0, SW, b_SW, lw2, b_lw)
                    chk('r2')
                    yield
                    def lmat(kind):
                        bk2 = pb2()
                        for hlf in range(2):
                            S.op("pe", lambda e: e.matmul(ps[:, bk2 + hlf, :], tri[:, kind, :], SW[:, hlf * 512:(hlf + 1) * 512], start=True, stop=True),
                                 reads=[b_SW] + cst, writes=[bps[bk2 + hlf]])
                        return bk2, ps[:, bk2:bk2 + 2, :].rearrange("p a n -> p (a n)")
                    bke = pb1()
                    for p_ in range(8):
                        S.op("pe", lambda e: e.matmul(ps[:, bke, 2 * p_:2 * p_ + 2], SW[:, p_ * CH:(p_ + 1) * CH], negc[:, 0:2], start=True, stop=True),
                             reads=[b_SW] + cst, writes=[bps[bke]])
                    S.op("act", lambda e: e.activation(elc[:], ps[:, bke, 0:16].rearrange("p (a two) -> p a two", two=2)[:, :, 0], AF.Exp),
                         reads=[bps[bke]], writes=[b_elc])
                    chk('r3')
                    yield
                    bkr = pb2()
                    proj_tm(xr, bxr, 0, bkr)
                    S.op("act", lambda e: e.copy(rbf, ps[:, bkr:bkr + 2, :].rearrange("p a n -> p (a n)")),
                         reads=[bps[bkr], bps[bkr + 1]], writes=[b_rbf])
                    chk('r4')
                    yield "barrier"
                    bkL, Lps = lmat(2)
                    S.op("act", lambda e: e.activation(E1[:], Lps, AF.Exp), reads=[bps[bkL], bps[bkL + 1]], writes=[b_E1])
                    S.op("dve", lambda e: e.tensor_tensor(khat[:], A[:], E1[:], ALU.mult), reads=[b_A, b_E1], writes=[b_khat])
                    S.op("pool", lambda e: e.tensor_tensor(bhat[:], Fa[:], E1[:], ALU.mult), reads=[b_F, b_E1], writes=[b_bhat])
                    bkL, Lps = lmat(0)
                    S.op("act", lambda e: e.activation(E2[:], Lps, AF.Exp, scale=-1.0), reads=[bps[bkL], bps[bkL + 1]], writes=[b_E2])
                    S.op("act", lambda e: e.activation(E1[:], Lps, AF.Exp), reads=[bps[bkL], bps[bkL + 1]], writes=[b_E1])
                    S.op("dve", lambda e: e.tensor_tensor(tA[:], A[:], E2[:], ALU.mult), reads=[b_A, b_E2], writes=[b_tA])
                    transpose8(tA, b_tA, kbfm[:, :, 0, :], b_kbfm)
                    S.op("dve", lambda e: e.tensor_tensor(tB[:], Fa[:], E2[:], ALU.mult), reads=[b_F, b_E2], writes=[b_tB])
                    transpose8(tB, b_tB, kbfm[:, :, 1, :], b_kbfm)
                    chk('r5')
                    yield
                    S.op("dve", lambda e: e.tensor_tensor(tA[:], rbf, E1[:], ALU.mult), reads=[b_rbf, b_E1], writes=[b_tA])
                    transpose8(tA, b_tA, arfm[:, :, 1, :], b_arfm)
                    S.op("dve", lambda e: e.tensor_tensor(Kp[:], rbf, A[:], ALU.mult), reads=[b_rbf, b_A], writes=[b_Kp])
                    S.op("pool", lambda e: e.tensor_tensor(Kp[:], Kp[:], rows[:, 2, :], ALU.mult), reads=[b_Kp, b_rows], writes=[b_Kp])
                    S.op("dve", lambda e: e.reduce_sum(sbon[:], h3(Kp[:]), axis=AX.X), reads=[b_Kp], writes=[b_sbon])
                    bkL, Lps = lmat(1)
                    S.op("act", lambda e: e.activation(E2[:], Lps, AF.Exp), reads=[bps[bkL], bps[bkL + 1]], writes=[b_E2])
                    S.op("dve", lambda e: e.scalar_tensor_tensor(tB[:], G[:], -1.0, E2[:], ALU.mult, ALU.mult), reads=[b_G, b_E2], writes=[b_tB])
                    transpose8(tB, b_tB, arfm[:, :, 0, :], b_arfm)
                    chk('r6')
                    yield
                    for p_ in range(8):
                        bkA = [pb1(), pb1()]
                        for hh in range(2):
                            pr = slice(hh * 64, hh * 64 + 64)
                            rhs_ar = arfm[pr, p_, :, :].rearrange("p a t -> p (a t)")
                            S.op("pe", lambda e: e.matmul(ps[:, bkA[hh], 0:256], kbfm[pr, p_, 0, :], rhs_ar, start=True, stop=True),
                                 reads=[b_kbfm, b_arfm], writes=[bps[bkA[hh]]])
                        for hh in range(2):
                            pr = slice(hh * 64, hh * 64 + 64)
                            rhs_ar = arfm[pr, p_, :, :].rearrange("p a t -> p (a t)")
                            S.op("pe", lambda e: e.matmul(ps[:, bkA[hh], 256:512], kbfm[pr, p_, 1, :], rhs_ar, start=True, stop=True),
                                 reads=[b_kbfm, b_arfm], writes=[bps[bkA[hh]]])
                        for hh in range(2):
                            h = 2 * p_ + hh
                            S.op("dve", lambda e: e.tensor_tensor(MKB[:, h], ps[:, bkA[hh], :].rearrange("p (a t) -> p a t", t=CH),
                                                                  mask4[:], ALU.mult),
                                 reads=[bps[bkA[hh]]] + cst, writes=[b_MKB])
                        yield
                    Q0v = Q0[:].rearrange("p (c two) t -> p c two t", two=2)
                    for pg in range(2):
                        bkC = [pb1(), pb1()]
                        for i4 in range(4):
                            p_ = pg * 4 + i4
                            for hh in range(2):
                                pr = slice(hh * 64, hh * 64 + 64)
                                S.op("pe", lambda e: e.matmul(ps[:, bkC[hh], i4 * CH:(i4 + 1) * CH], arfm[pr, p_, 0, :], kbfm[pr, p_, 1, :],
                                                              start=True, stop=True),
                                     reads=[b_kbfm, b_arfm], writes=[bps[bkC[hh]]])
                        for hh in range(2):
                            S.op("dve", lambda e: e.tensor_tensor(Q0v[:, pg * 4:(pg + 1) * 4, hh, :],
                                                                  ps[:, bkC[hh], :].rearrange("p (a t) -> p a t", t=CH), maskL[:], ALU.mult),
                                 reads=[bps[bkC[hh]]] + cst, writes=[b_Q0] + bQb[0])
                        yield
                    xg, bxg = mix(5, q)
                    xv_, bxv = mix(3, q)
                    bkv = pb2()
                    held.update((bkv, bkv + 1))

                    def v_slice(s_):
                        wt_, wb1 = wload((sc_bwin, 8 + s_))
                        k = bkv + s_ // 2
                        cols = slice((s_ % 2) * 256, (s_ % 2) * 256 + 256)
                        for c in range(DC):
                            S.op("pe", lambda e: e.matmul(ps[:, k, cols], xv_[:, c, :], wt_[:, c * 256:(c + 1) * 256],
                                                          start=(c == 0), stop=(c == DC - 1)),
                                 reads=[wb1, bxv], writes=[bps[k]])

                    def v_fin():
                        S.op("act", lambda e: e.copy(vbf[:], ps[:, bkv:bkv + 2, :].rearrange("p a n -> p (a n)")),
                             reads=[bps[bkv], bps[bkv + 1]], writes=[b_vbf])
                        held.difference_update((bkv, bkv + 1))

                    def g_all():
                        bkl = pb1()
                        for c in range(DC):
                            S.op("pe", lambda e: e.matmul(ps[:, bkl, 0:CH], lg1[:, c, 0:128], xg[:, c, :], start=(c == 0), stop=(c == DC - 1)),
                                 reads=[b_lg1, bxg], writes=[bps[bkl]])
                        for c in range(DC):
                            S.op("pe", lambda e: e.matmul(ps[0:32, bkl, CH:2 * CH], lg1[:, c, 128:160], xg[:, c, :], start=(c == 0), stop=(c == DC - 1)),
                                 reads=[b_lg1, bxg], writes=[bps[bkl]])
                        S.op("act", lambda e: e.activation(lor[:, 2, :], ps[:, bkl, 0:CH], AF.Sigmoid), reads=[bps[bkl]], writes=[b_lg])
                        S.op("act", lambda e: e.activation(lor[0:32, 3, :], ps[0:32, bkl, CH:2 * CH], AF.Sigmoid), reads=[bps[bkl]], writes=[b_lg])
                        bkg = pb2()
                        for hlf in range(2):
                            cs = slice(hlf * 512, (hlf + 1) * 512)
                            S.op("pe", lambda e: e.matmul(ps[:, bkg + hlf, :], lor[:, 2, :], lg2a[:, cs], start=True, stop=False),
                                 reads=[b_lg, b_lg2a], writes=[bps[bkg + hlf]])
                            S.op("pe", lambda e: e.matmul(ps[:, bkg + hlf, :], lor[0:32, 3, :], lg2b[:, cs], start=False, stop=True),
                                 reads=[b_lg, b_lg2b], writes=[bps[bkg + hlf]])
                        S.op("act", lambda e: e.copy(gbf, ps[:, bkg:bkg + 2, :].rearrange("p a n -> p (a n)")),
                             reads=[bps[bkg], bps[bkg + 1]], writes=[b_gbf])
                    chk('r7')
                    yield
                    Qbuf = [lambda h: Q0[:, h, :], lambda h: Qt_[:, h, :]]
                    QbufG = [lambda hs: Q0[:, hs, :], lambda hs: Qt_[:, hs, :]]

                    def v4(bk, n):
                        return ps[:, bk, 0:4 * n].rearrange("p (a t) -> p a t", t=n)
                    evq = 0
                    for gi in range(4):
                        hs = slice(gi * 4, gi * 4 + 4)
                        S.op("dve", lambda e: e.tensor_tensor(PX[1][:, hs, 1, :], MKB[:, hs, 2, :],
                                                               identb[:].unsqueeze(1).to_broadcast([128, 4, CH]), ALU.add),
                             reads=[b_MKB] + cst, writes=[bPX[1][gi]])
                        bkp = pb1()
                        bkq = pb1()
                        for j in range(4):
                            h = gi * 4 + j
                            S.op("pe", lambda e: e.matmul(ps[:, bkp, j * CH:(j + 1) * CH], Q0[:, h, :], MKB[:, h, 2, :], start=True, stop=True),
                                 reads=[b_MKB, b_Q0, bQb[0][gi]], writes=[bps[bkp]])
                        for j in range(4):
                            h = gi * 4 + j
                            S.op("pe", lambda e: e.matmul(ps[:, bkq, j * CH:(j + 1) * CH], MKB[:, h, 2, :], Q0[:, h, :], start=True, stop=True),
                                 reads=[b_MKB, b_Q0, bQb[0][gi]], writes=[bps[bkq]])
                        S.op("act", lambda e: e.copy(PX[1][:, hs, 0, :], v4(bkp, CH)), reads=[bps[bkp]], writes=[bPX[1][gi]])
                        S.op("dve", lambda e: e.tensor_copy(Qt_[:, hs, :], v4(bkq, CH)), reads=[bps[bkq]], writes=[bQb[1][gi]])
                        yield
                    for t_ in range(2, 8):
                        si, di = (t_ - 1) % 2, t_ % 2
                        if t_ <= 5:
                            v_slice(t_ - 2)
                        elif t_ == 6:
                            v_fin()
                            g_all()
                        for gi in range(4):
                            hs = slice(gi * 4, gi * 4 + 4)
                            rdp = [bPX[si][gi], bQb[si][gi]]
                            if t_ <= 5:
                                bk2 = pb2()
                                for j in range(4):
                                    h = gi * 4 + j
                                    S.op("pe", lambda e: e.matmul(ps[:, bk2 + j // 2, (j % 2) * 256:(j % 2) * 256 + 256], Qbuf[si](h),
                                                                  PX[si][:, h, :, :].rearrange("p a t -> p (a t)"), start=True, stop=True),
                                         reads=rdp, writes=[bps[bk2], bps[bk2 + 1]])
                                pv = ps[:, bk2:bk2 + 2, :].rearrange("p a (h k t) -> p (a h) k t", k=2, t=CH)
                                S.op("act", lambda e: e.copy(PX[di][:, hs, 0, :], pv[:, :, 0, :]), reads=[bps[bk2], bps[bk2 + 1]], writes=[bPX[di][gi]])
                                S.op("dve", lambda e: e.tensor_tensor(PX[di][:, hs, 1, :], pv[:, :, 1, :], PX[si][:, hs, 1, :], ALU.add),
                                     reads=[bps[bk2], bps[bk2 + 1], bPX[si][gi]], writes=[bPX[di][gi]])
                            else:
                                bkx = pb1()
                                for j in range(4):
                                    h = gi * 4 + j
                                    S.op("pe", lambda e: e.matmul(ps[:, bkx, j * CH:(j + 1) * CH], Qbuf[si](h), PX[si][:, h, 1, :], start=True, stop=True),
                                         reads=rdp, writes=[bps[bkx]])
                                S.op("dve", lambda e: e.tensor_tensor(PX[di][:, hs, 1, :], v4(bkx, CH), PX[si][:, hs, 1, :], ALU.add),
                                     reads=[bps[bkx], bPX[si][gi]], writes=[bPX[di][gi]])
                            if t_ <= 6:
                                bkq = pb1()
                                for j in range(4):
                                    h = gi * 4 + j
                                    S.op("pe", lambda e: e.matmul(ps[:, bkq, j * CH:(j + 1) * CH], PX[si][:, h, 0, :], Qbuf[si](h), start=True, stop=True),
                                         reads=rdp, writes=[bps[bkq]])
                                evq += 1
                                S.op("act" if evq % 2 else "dve",
                                     (lambda e: e.copy(QbufG[di](hs), v4(bkq, CH))) if evq % 2 else (lambda e: e.tensor_copy(QbufG[di](hs), v4(bkq, CH))),
                                     reads=[bps[bkq]], writes=[bQb[di][gi]])
                            yield
                    chk('r8')
                    yield

                def genB(q):
                    q0 = q * CH
                    elc, b_elc = elc2[q % 2]
                    sbon, b_sbon = sbon2[q % 2]
                    bkR = pb2()
                    for h in range(NH):
                        k = bkR + h // 8
                        cs = slice((h % 8) * 64, (h % 8) * 64 + 64)
                        S.op("pe", lambda e: e.matmul(ps[:, k, cs], MKB[:, h, 0, :], vbf[:, h * 64:(h + 1) * 64], start=True, stop=False),
                             reads=[b_MKB, b_vbf], writes=[bps[k]])
                        S.op("pe", lambda e: e.matmul(ps[:, k, cs], arfm[:, h // 2, 0, :], hbd_t[:, h // 2, (h % 2) * 64:(h % 2) * 64 + 64],
                                                      start=False, stop=True),
                             reads=[b_arfm, b_hbd], writes=[bps[k]])
                    S.op("act", lambda e: e.copy(rhsbf[:], ps[:, bkR:bkR + 2, :].rearrange("p a n -> p (a n)")),
                         reads=[bps[bkR], bps[bkR + 1]], writes=[b_rhsbf])
                    yield
                    bkU = pb2()
                    for h in range(NH):
                        k = bkU + h // 8
                        cs = slice((h % 8) * 64, (h % 8) * 64 + 64)
                        S.op("pe", lambda e: e.matmul(ps[:, k, cs], PX[1][:, h, 1, :], rhsbf[:, h * 64:(h + 1) * 64], start=True, stop=True),
                             reads=bPX[1] + [b_rhsbf], writes=[bps[k]])
                    S.op("dve", lambda e: e.tensor_copy(ubf[:], ps[:, bkU:bkU + 2, :].rearrange("p a n -> p (a n)")),
                         reads=[bps[bkU], bps[bkU + 1]], writes=[b_ubf])
                    yield
                    bkY = pb2()
                    for h in range(NH):
                        k = bkY + h // 8
                        cs = slice((h % 8) * 64, (h % 8) * 64 + 64)
                        S.op("pe", lambda e: e.matmul(ps[:, k, cs], MKB[:, h, 3, :], ubf[:, h * 64:(h + 1) * 64], start=True, stop=False),
                             reads=[b_MKB, b_ubf], writes=[bps[k]])
                        S.op("pe", lambda e: e.matmul(ps[:, k, cs], MKB[:, h, 1, :], vbf[:, h * 64:(h + 1) * 64], start=False, stop=False),
                             reads=[b_MKB, b_vbf], writes=[bps[k]])
                        S.op("pe", lambda e: e.matmul(ps[:, k, cs], arfm[:, h // 2, 1, :], hbd_t[:, h // 2, (h % 2) * 64:(h % 2) * 64 + 64],
                                                      start=False, stop=True),
                             reads=[b_arfm, b_hbd], writes=[bps[k]])
                    yps = ps[:, bkY:bkY + 2, :].rearrange("p a n -> p (a n)")
                    bY = [bps[bkY], bps[bkY + 1]]
                    held.update((bkY, bkY + 1))
                    yield
                    bkH = pb2()
                    for p_ in range(8):
                        k = bkH + p_ // 4
                        cs = slice((p_ % 4) * CH, (p_ % 4) * CH + CH)
                        S.op("pe", lambda e: e.matmul(ps[:, k, cs], khat[:, p_ * CH:(p_ + 1) * CH], vbf[:, p_ * CH:(p_ + 1) * CH], start=True, stop=False),
                             reads=[b_khat, b_vbf], writes=[bps[k]])
                        S.op("pe", lambda e: e.matmul(ps[:, k, cs], bhat[:, p_ * CH:(p_ + 1) * CH], ubf[:, p_ * CH:(p_ + 1) * CH], start=False, stop=True),
                             reads=[b_bhat, b_ubf], writes=[bps[k]])
                    S.op("dve", lambda e: e.tensor_tensor(hf_t[:], hf_t[:], elc[:].unsqueeze(2).to_broadcast([128, 8, 64]), ALU.mult),
                         reads=[b_hf, b_elc], writes=[b_hf])
                    dH = ps[:, bkH:bkH + 2, :].rearrange("p a (c n) -> p (a c) n", n=CH)
                    for hh in range(2):
                        pr = slice(hh * 64, hh * 64 + 64)
                        S.op("dve", lambda e: e.tensor_tensor(hf_t[pr], hf_t[pr], dH[pr, :, hh * 64:hh * 64 + 64], ALU.add),
                             reads=[b_hf, bps[bkH], bps[bkH + 1]], writes=[b_hf])
                        S.op("act", lambda e: e.copy(hbd_t[pr, :, hh * 64:hh * 64 + 64], hf_t[pr]), reads=[b_hf], writes=[b_hbd])
                    chk('r9')
                    yield
                    S.op("dve", lambda e: e.reduce_sum(g1s[:], h3(yps), axis=AX.X), reads=bY, writes=[b_g1s])
                    S.op("act", lambda e: e.activation(E1[:], yps, AF.Square), reads=bY, writes=[b_E1])
                    S.op("dve", lambda e: e.reduce_sum(g2s[:], h3(E1[:]), axis=AX.X), reads=[b_E1], writes=[b_g2s])
                    S.op("dve", lambda e: e.tensor_scalar(g1s[:], g1s[:], 1.0 / 64, None, ALU.mult), reads=[b_g1s], writes=[b_g1s])
                    S.op("dve", lambda e: e.tensor_tensor(g3s[:], g1s[:], g1s[:], ALU.mult), reads=[b_g1s], writes=[b_g3s])
                    S.op("dve", lambda e: e.scalar_tensor_tensor(g3s[:], g2s[:], 1.0 / 64, g3s[:], ALU.mult, ALU.subtract),
                         reads=[b_g2s, b_g3s], writes=[b_g3s])
                    S.op("act", lambda e: e.activation(g3s[:], g3s[:], AF.Ln, bias=lnb_rms[:, 3:4], scale=1.0), reads=[b_g3s] + cst, writes=[b_g3s])
                    S.op("act", lambda e: e.activation(g3s[:], g3s[:], AF.Exp, scale=-0.5), reads=[b_g3s], writes=[b_g3s])
                    S.op("dve", lambda e: e.tensor_tensor(h3(E1[:]), h3(yps), bc16(g1s[:]), ALU.subtract), reads=bY + [b_g1s], writes=[b_E1])
                    held.difference_update((bkY, bkY + 1))
                    yield
                    S.op("dve", lambda e: e.tensor_tensor(h3(E1[:]), h3(E1[:]), bc16(g3s[:]), ALU.mult), reads=[b_E1, b_g3s], writes=[b_E1])
                    S.op("dve", lambda e: e.tensor_tensor(E1[:], E1[:], rows[:, 3, :], ALU.mult), reads=[b_E1, b_rows], writes=[b_E1])
                    S.op("dve", lambda e: e.tensor_tensor(E1[:], E1[:], rows[:, 4, :], ALU.add), reads=[b_E1, b_rows], writes=[b_E1])
                    S.op("dve", lambda e: e.tensor_tensor(h3(tA[:]), h3(vbf[:]), bc16(sbon[:]), ALU.mult), reads=[b_vbf, b_sbon], writes=[b_tA])
                    S.op("dve", lambda e: e.tensor_tensor(E1[:], E1[:], tA[:], ALU.add), reads=[b_E1, b_tA], writes=[b_E1])
                    S.op("dve", lambda e: e.tensor_tensor(tA[:], E1[:], gbf, ALU.mult), reads=[b_E1, b_gbf], writes=[b_tA])
                    transpose8(tA, b_tA, yg[:, :, q0:q0 + CH], b_yg)
                def run(g):
                    for _ in g:
                        pass

                def interleave(ga, gb):
                    alive_a, alive_b = True, True
                    while alive_a or alive_b:
                        if alive_b:
                            try:
                                next(gb)
                            except StopIteration:
                                alive_b = False
                        if alive_a:
                            try:
                                if next(ga) == "barrier" and alive_b:
                                    run(gb)
                                    alive_b = False
                            except StopIteration:
                                alive_a = False
                run(genA(0))
                for q in range(4):
                    if q < 3 and os.environ.get("KNOIL") != "1":
                        interleave(genA(q + 1), genB(q))
                    else:
                        run(genB(q))
                        if q < 3:
                            run(genA(q + 1))
                proj_fm_out(sc_bwout, 4, yg, b_yg, lambda ec, bk: resid_add(ec, bk, l, 0, b))

            for b in range(NB):
                hf_t, b_hf = Hf[b]
                hbd_t, b_hbd = Hbd[b]
                S.op("pool", lambda e: e.memset(hf_t[:], 0.0), writes=[b_hf])
                S.op("pool", lambda e: e.memset(hbd_t[:], 0.0), writes=[b_hbd])
                for ti in range(NT):
                    t0 = ti * TT
                    for q in range(4):
                        xt_, b_xt = tmf[q]
                        S.dma("sp", xt_[:], x_d[b, t0 + q * CH: t0 + (q + 1) * CH, :], writes=[b_xt])
                    for half in range(2):
                        for c4 in range(4):
                            c = half * 4 + c4
                            bk = pb1()
                            for q in range(4):
                                xt_, b_xt = tmf[q]
                                S.op("pe", lambda e: e.transpose(ps[:, bk, q * CH:(q + 1) * CH], xt_[:, c * CH:(c + 1) * CH], identf[:]),
                                     reads=[b_xt] + cst, writes=[bps[bk]])
                            S.op("act" if c % 2 else "dve",
                                 (lambda e: e.copy(xres[:, c, :], ps[:, bk, :])) if c % 2 else (lambda e: e.tensor_copy(xres[:, c, :], ps[:, bk, :])),
                                 reads=[bps[bk]], writes=[b_x])
                    chk('load', (xres[:].rearrange('p c t -> p (c t)'), b_x, 4096))
                    sgu(b)
                    chk('sgu', (xres[:].rearrange('p c t -> p (c t)'), b_x, 4096))
                    mlp(0, b)
                    chk('mlp0', (xres[:].rearrange('p c t -> p (c t)'), b_x, 4096))
                    pump(1000)
                    rwkv(b, ti == 0)
                    chk('rwkv', (xres[:].rearrange('p c t -> p (c t)'), b_x, 4096))
                    mlp(1, b)
                    rms_mod(None, 0, b)
                    yf, b_yf = slab[0]
                    yff = yf[:].rearrange("p c t -> p (c t)").bitcast(F32).rearrange("p (c t) -> p c t", t=TT // 2)
                    for hlf in range(2):
                        tsl = slice(hlf * 256, (hlf + 1) * 256)
                        for c in range(DC):
                            S.op("dve", lambda e: e.scalar_tensor_tensor(yff[:, c, :], xres[:, c, tsl], fgf[:, c:c + 1], rstd[:, tsl],
                                                                         ALU.mult, ALU.mult),
                                 reads=[b_x, b_fgf, b_rstd], writes=[b_yf])
                        for q2 in range(2):
                            q = hlf * 2 + q2
                            ot, b_ot = tmf[4 + q % 2]
                            for c4 in range(2):
                                bk = pb1()
                                for cc in range(4):
                                    c = c4 * 4 + cc
                                    S.op("pe", lambda e: e.transpose(ps[:, bk, cc * CH:(cc + 1) * CH], yff[:, c, q2 * CH:(q2 + 1) * CH], identf[:]),
                                         reads=[b_yf] + cst, writes=[bps[bk]])
                                S.op("act" if c4 else "dve",
                                     (lambda e: e.copy(ot[:, c4 * 512:(c4 + 1) * 512], ps[:, bk, :])) if c4 else
                                     (lambda e: e.tensor_copy(ot[:, c4 * 512:(c4 + 1) * 512], ps[:, bk, :])),
                                     reads=[bps[bk]], writes=[b_ot])
                            S.dma("pool", out_d[b, t0 + q * CH: t0 + (q + 1) * CH, :], ot[:], reads=[b_ot])
      except _Stop:
          pass
      for e_ in ("sp", "pool"):
          S.finish(e_)
    return nc


b_outd = Buf("out_dram")

_NC_CACHE = {}


def kernel(**inputs):
    n = 8
    x = np.ascontiguousarray(inputs["x"], dtype=np.float32)
    B, T, _ = x.shape
    NB = B // n
    key = (T, NB)
    if key not in _NC_CACHE:
        _NC_CACHE[key] = build(T, NB)
    nc = _NC_CACHE[key]
    shared = {}
    for k, v in inputs.items():
        if k in ("x", "c"):
            continue
        a = np.ascontiguousarray(v, dtype=np.float32)
        if k.startswith("a_") or k.startswith("b_"):
            a = a[0]
        if k in ("a_b_s", "b_r_k"):
            a = a.reshape(-1)
        shared[k] = np.ascontiguousarray(a)
    c = np.ascontiguousarray(inputs["c"], dtype=np.float32)
    in_maps = []
    for i in range(n):
        m = dict(shared)
        m["x"] = np.ascontiguousarray(x[i * NB:(i + 1) * NB])
        m["c"] = np.ascontiguousarray(c[i * NB:(i + 1) * NB])
        in_maps.append(m)
    res = run_bass_kernel_spmd(nc, in_maps, core_ids=list(range(n)))
    return np.concatenate([r["out"] for r in res.results], axis=0).astype(np.float32)
```

```python
import os
import numpy as np
from contextlib import ExitStack
import concourse.bass as bass
import concourse.mybir as mybir
from concourse.bass_utils import run_bass_kernel_spmd

F32 = mybir.dt.float32
BF16 = mybir.dt.bfloat16
AF = mybir.ActivationFunctionType
ALU = mybir.AluOpType
AX = mybir.AxisListType

D = 1024
DC = 8
FF = 4096
TT = 512
CH = 128
NH = 16
DECAY_C = 0.6065306597126334


class Buf:
    __slots__ = ("name", "w", "r")

    def __init__(self, name):
        self.name = name
        self.w = None
        self.r = {}


class Sched:
    NDMA = 24

    def __init__(self, nc, stack):
        self.nc = nc
        self.eng = {"pe": nc.tensor, "dve": nc.vector, "act": nc.scalar,
                    "pool": nc.gpsimd, "sp": nc.sync}
        self.sem = {}
        for k in self.eng:
            self.sem[k] = stack.enter_context(nc.semaphore("s_" + k))
        self.cnt = {k: 0 for k in self.eng}
        self.waited = {k: {} for k in self.eng}
        for j in range(self.NDMA):
            key = "d%d" % j
            self.sem[key] = stack.enter_context(nc.semaphore("s_" + key))
            self.cnt[key] = 0
        self.dma_i = 0
        self.sw_i = 0

    def _wait(self, e, deps):
        eng = self.eng[e]
        wd = self.waited[e]
        for key, val in deps.items():
            if key == "pe" and e == "pe":
                continue
            if wd.get(key, 0) < val:
                eng.wait_ge(self.sem[key], val)
                wd[key] = val

    @staticmethod
    def _add(deps, tok):
        if tok is None:
            return
        k, v = tok
        if deps.get(k, 0) < v:
            deps[k] = v

    def _collect(self, reads, writes):
        deps = {}
        for b in reads:
            self._add(deps, b.w)
        for b in writes:
            self._add(deps, b.w)
            for k, v in b.r.items():
                self._add(deps, (k, v))
        return deps

    def _commit(self, tok, reads, writes):
        k, v = tok
        for b in reads:
            if b.r.get(k, 0) < v:
                b.r[k] = v
        for b in writes:
            b.w = tok
            b.r = {}

    def op(self, e, fn, reads=(), writes=()):
        deps = self._collect(reads, writes)
        self._wait(e, deps)
        ins = fn(self.eng[e])
        self.cnt[e] += 1
        ins.then_inc(self.sem[e], 1)
        self._commit((e, self.cnt[e]), reads, writes)
        return ins

    def dma(self, e, out, in_, reads=(), writes=(), **kw):
        half = self.NDMA // 2
        if e == "pool":
            key = "d%d" % (self.sw_i % half)
            self.sw_i += 1
        else:
            key = "d%d" % (half + self.dma_i % half)
            self.dma_i += 1
        deps = self._collect(reads, writes)
        if self.cnt[key] > 0:
            self._add(deps, (key, self.cnt[key]))
        self._wait(e, deps)
        ins = self.eng[e].dma_start(out=out, in_=in_, **kw)
        self.cnt[key] += 16
        ins.then_inc(self.sem[key], 16)
        self._commit((key, self.cnt[key]), reads, writes)
        return ins

    def finish(self, e):
        for j in range(self.NDMA):
            k = "d%d" % j
            if self.cnt[k]:
                self._wait(e, {k: self.cnt[k]})


class _Stop(Exception):
    pass


def build(T=4096, NB=2, stop=None):
    nc = bass.Bass("TRN2", target_bir_lowering=False)
    NT = T // TT

    def din(name, shape):
        return nc.dram_tensor(name, list(shape), F32, kind="ExternalInput").ap()

    x_d = din("x", [NB, T, D])
    c_d = din("c", [NB, D])
    ada_w = din("ada_w", [2, D, 6 * D])
    ada_b = din("ada_b", [2, 6 * D])
    mlp_w1 = din("mlp_w1", [2, D, FF])
    mlp_w2 = din("mlp_w2", [2, FF, D])
    a_w_in = din("a_w_in", [D, 2 * D])
    a_ln_g = din("a_ln_g", [D])
    a_ln_b = din("a_ln_b", [D])
    a_w_s = din("a_w_s", [8, CH, CH])
    a_b_s = din("a_b_s", [8 * CH])
    a_w_out = din("a_w_out", [D, D])
    b_mu = din("b_mu", [6, D])
    b_w_in = din("b_w_in", [D, 3 * D])
    b_w0 = din("b_w0", [D])
    b_w1 = din("b_w1", [D, 64])
    b_w2 = din("b_w2", [64, D])
    b_a0 = din("b_a0", [D])
    b_a1 = din("b_a1", [D, 64])
    b_a2 = din("b_a2", [64, D])
    b_g1 = din("b_g1", [D, 160])
    b_g2 = din("b_g2", [160, D])
    b_k_k = din("b_k_k", [D])
    b_k_a = din("b_k_a", [D])
    b_r_k = din("b_r_k", [D])
    b_ln_g = din("b_ln_g", [D])
    b_ln_b = din("b_ln_b", [D])
    b_w_out = din("b_w_out", [D, D])
    final_g = din("final_g", [D])
    out_d = nc.dram_tensor("out", [NB, T, D], F32, kind="ExternalOutput").ap()

    def scratch(name, nsl):
        return nc.dram_tensor(name, [nsl, 128, 2048], BF16, kind="Internal").ap()

    sc_w1 = [scratch("sc_w1_%d" % l, 16) for l in range(2)]
    sc_w2 = [scratch("sc_w2_%d" % l, 16) for l in range(2)]
    sc_awin = scratch("sc_awin", 8)
    sc_awout = scratch("sc_awout", 4)
    sc_bwin = scratch("sc_bwin", 12)
    sc_bwout = scratch("sc_bwout", 4)

    with ExitStack() as st:
      S = Sched(nc, st)
      dbg_d = nc.dram_tensor("dbg", [128, 4096], F32, kind="ExternalOutput").ap() if stop else None

      def chk(name, dump=None):
          if stop == name:
              if dump is not None:
                  ap, bf, ncol = dump
                  S.dma("sp", dbg_d[:, 0:ncol], ap, reads=[bf])
              raise _Stop()
      try:

            def tile(name, shape, dt):
                return st.enter_context(nc.sbuf_tensor(name, list(shape), dt)), Buf(name)

            ps = st.enter_context(nc.psum_tensor("ps", [128, 8, 512], F32))
            bps = [Buf("ps%d" % i) for i in range(8)]
            pstate = {"i": 0}

            held = set()

            def pb1():
                while True:
                    k = pstate["i"] % 8
                    pstate["i"] += 1
                    if k not in held:
                        return k

            def pb2():
                while True:
                    if pstate["i"] % 2:
                        pstate["i"] += 1
                    k = pstate["i"] % 8
                    pstate["i"] += 2
                    if k not in held and (k + 1) not in held:
                        return k

            identf, b_identf = tile("identf", [128, 128], F32)
            identb, b_identb = tile("identb", [128, 128], BF16)
            onesb, b_onesb = tile("onesb", [128, 128], BF16)
            onesf, b_onesf = tile("onesf", [128, 128], F32)
            tri, b_tri = tile("tri", [128, 3, 128], F32)
            negc, b_negc = tile("negc", [128, 2], F32)
            mask4, b_mask4 = tile("mask4", [128, 4, 128], BF16)
            maskL, b_maskL = tile("maskL", [128, 4, 128], BF16)
            lnb_rms, b_lnbr = tile("lnb_rms", [128, 4], F32)
            cst = [b_identf, b_identb, b_onesb, b_onesf, b_tri, b_negc, b_mask4, b_maskL, b_lnbr]

            def sel(out, in_, cm, pat, base):
                S.op("pool", lambda e: e.affine_select(out=out, in_=in_, compare_op=ALU.is_ge, fill=0.0,
                                                        base=base, pattern=[[pat, 128]], channel_multiplier=cm),
                     writes=cst, reads=cst)

            S.op("pool", lambda e: e.memset(identf[:], 0.0), writes=cst)
            S.op("pool", lambda e: e.affine_select(out=identf[:], in_=identf[:], compare_op=ALU.not_equal, fill=1.0,
                                                    base=0, pattern=[[-1, 128]], channel_multiplier=1), writes=cst)
            S.op("pool", lambda e: e.tensor_copy(identb[:], identf[:]), writes=cst)
            S.op("pool", lambda e: e.memset(onesb[:], 1.0), writes=cst)
            S.op("pool", lambda e: e.memset(onesf[:], 1.0), writes=cst)
            S.op("pool", lambda e: e.memset(negc[:], -DECAY_C), writes=cst)
            for i_, v_ in enumerate((D * 1e-6, 1e-5, 1e-24, 64e-5)):
                S.op("pool", lambda e: e.memset(lnb_rms[:, i_:i_ + 1], v_), writes=cst)
            S.op("pool", lambda e: e.memset(tri[:], -DECAY_C), writes=cst)
            sel(tri[:, 0, :], tri[:, 0, :], -1, 1, 0)
            sel(tri[:, 1, :], tri[:, 1, :], -1, 1, -1)
            sel(tri[:, 2, :], tri[:, 2, :], 1, -1, -1)
            S.op("pool", lambda e: e.memset(mask4[:], 1.0), writes=cst)
            S.op("pool", lambda e: e.memset(maskL[:], 1.0), writes=cst)
            for k in range(4):
                sel(mask4[:, k, :], mask4[:, k, :], -1, 1, -1 if k % 2 == 0 else 0)
            for k in range(4):
                sel(maskL[:, k, :], maskL[:, k, :], 1, -1, -1)

            xres, b_x = tile("xres", [128, DC, TT], F32)
            hb, b_h = tile("hb", [128, DC, TT], BF16)
            carry, b_carry = tile("carry", [128, DC, 2], BF16)
            rstd, b_rstd = tile("rstd", [128, TT], F32)
            ntmp = [tile("ntmp%d" % i, [128, TT], F32) for i in range(2)]
            slab = [tile("slab%d" % i, [128, DC, TT], BF16) for i in range(2)]
            sl1f = slab[1][0][:].rearrange("p c t -> p (c t)")
            sl1 = [sl1f[:, k * 1024:(k + 1) * 1024] for k in range(4)]
            b_sl1 = [Buf("sl1_%d" % k) for k in range(4)]
            NTMF = 6
            tmf = [tile("tmf%d" % i, [128, D], F32) for i in range(NTMF)]
            NTMB = 5
            tmb = [tile("tmb%d" % i, [128, D], BF16) for i in range(NTMB)]
            NWS = 4
            wslot = [tile("wslot%d" % i, [128, 2048], BF16) for i in range(NWS)]
            wstate = {"i": 0}
            rstate = {"i": 0}

            scrb = {}

            def sbuf_of(scr, s):
                key = (scr.name if hasattr(scr, "name") else id(scr), s)
                if key not in scrb:
                    scrb[key] = Buf("scr")
                return scrb[key]

            def wload(src):
                scr, s = src
                k = wstate["i"] % NWS
                wstate["i"] += 1
                t, b = wslot[k]
                S.dma("sp", t[:], scr[s], writes=[b], reads=[sbuf_of(scr, s)])
                pump(2)
                return t, b

            st16 = [tile("st16_%d" % i, [128, 16], F32) for i in range(6)]
            modf, b_mod = tile("modf", [128, 2, 6, DC, NB], F32)
            scf, b_scf = tile("scf", [128, 2, 2, DC, NB], F32)
            adab, b_adab = tile("adab", [128, 2, 6, DC], F32)
            condT, b_cond = tile("condT", [128, DC, NB], BF16)
            cfm, b_cfm = tile("cfm", [128, DC, NB], F32)
            muf, b_muf = tile("muf", [128, 6, DC], F32)
            fgf, b_fgf = tile("fgf", [128, DC], F32)
            alng, b_alng = tile("alng", [128, DC], F32)
            alnb, b_alnb = tile("alnb", [128, DC], F32)
            wsT, b_wsT = tile("wsT", [128, 8, CH], BF16)
            csgu, b_csgu = tile("csgu", [128, 8, CH], F32)
            rows, b_rows = tile("rows", [128, 5, D], BF16)
            w0a0, b_w0a0 = tile("w0a0", [1, 2, D], F32)
            lw1, b_lw1 = tile("lw1", [128, DC, 64], BF16)
            la1, b_la1 = tile("la1", [128, DC, 64], BF16)
            lg1, b_lg1 = tile("lg1", [128, DC, 160], BF16)
            lw2, b_lw2 = tile("lw2", [64, D], BF16)
            la2, b_la2 = tile("la2", [64, D], BF16)
            lg2a, b_lg2a = tile("lg2a", [128, D], BF16)
            lg2b, b_lg2b = tile("lg2b", [32, D], BF16)
            xi = [tile("xi%d" % i, [128, DC, CH], BF16) for i in range(4)]
            xstate = {"i": 0}
            xxt, b_xx = tile("xxt", [128, DC, CH], BF16)
            lor, b_lor = tile("lor", [128, 4, CH], BF16)
            b_lorw, b_lora, b_lorg = Buf("lorw"), Buf("lora"), Buf("lorg")
            arfm, b_arfm = tile("arfm", [128, 8, 2, CH], BF16)
            kbfm, b_kbfm = tile("kbfm", [128, 8, 2, CH], BF16)
            MKB, b_MKB = tile("MKB", [128, NH, 4, CH], BF16)
            Q0, b_Q0 = tile("Q0", [128, NH, CH], BF16)
            Qt_, _ = tile("Qt", [128, NH, CH], BF16)
            PX = [tile("PX%d" % i, [128, NH, 2, CH], BF16)[0] for i in range(2)]
            bPX = [[Buf("PX%d_%d" % (i, g)) for g in range(4)] for i in range(2)]
            bQb = [[Buf("Q%d_%d" % (i, g)) for g in range(4)] for i in range(2)]
            Hf = [tile("Hf%d" % i, [128, 8, 64], F32) for i in range(NB)]
            Hbd = [tile("Hbd%d" % i, [128, 8, 128], BF16) for i in range(NB)]
            elc2 = [tile("elc%d" % i, [128, 8], F32) for i in range(2)]
            sbon2 = [tile("sbon%d" % i, [128, 16], F32) for i in range(2)]

            if stop:
                print('SBUF remaining', nc.sbuf_bytes_remaining)
            def fmvec(dst, bdst, src_ap):
                with nc.allow_non_contiguous_dma(reason="small per-feature vector to FM"):
                    S.dma("sp", dst, src_ap, writes=[bdst])

            for b in range(NB):
                fmvec(cfm[:, :, b], b_cfm, c_d[b].rearrange("(c p) -> p c", p=128))
            for m in range(6):
                fmvec(muf[:, m, :], b_muf, b_mu[m].rearrange("(c p) -> p c", p=128))
                for l in range(2):
                    fmvec(adab[:, l, m, :], b_adab, ada_b[l, m * D:(m + 1) * D].rearrange("(c p) -> p c", p=128))
            fmvec(fgf[:], b_fgf, final_g.rearrange("(c p) -> p c", p=128))
            fmvec(alng[:], b_alng, a_ln_g.rearrange("(c p) -> p c", p=128))
            fmvec(alnb[:], b_alnb, a_ln_b.rearrange("(c p) -> p c", p=128))
            S.op("dve", lambda e: e.tensor_scalar(fgf[:], fgf[:], 32.0, None, ALU.mult), reads=[b_fgf], writes=[b_fgf])
            S.dma("sp", w0a0[:, 0, :], b_w0.rearrange("(o n) -> o n", o=1), writes=[b_w0a0])
            S.dma("sp", w0a0[:, 1, :], b_a0.rearrange("(o n) -> o n", o=1), writes=[b_w0a0])
            for i, src in enumerate([b_k_k, b_k_a, b_r_k, b_ln_g, b_ln_b]):
                t, b = tmf[i % 2]
                S.dma("sp", t[:], src.partition_broadcast(128), writes=[b])
                S.op("act", lambda e: e.copy(rows[:, i, :], t[:]), reads=[b], writes=[b_rows])
            S.dma("pool", lw1[:], b_w1.rearrange("(c p) n -> p c n", p=128), writes=[b_lw1])
            S.dma("pool", la1[:], b_a1.rearrange("(c p) n -> p c n", p=128), writes=[b_la1])
            S.dma("pool", lg1[:], b_g1.rearrange("(c p) n -> p c n", p=128), writes=[b_lg1])
            S.dma("pool", lw2[:], b_w2[:, :], writes=[b_lw2])
            S.dma("pool", la2[:], b_a2[:, :], writes=[b_la2])
            S.dma("pool", lg2a[:], b_g2[0:128, :], writes=[b_lg2a])
            S.dma("pool", lg2b[:], b_g2[128:160, :], writes=[b_lg2b])

            chk('const', (rows[:, 0:2, :].rearrange('p a n -> p (a n)'), b_rows, 2048)) if False else chk('const')
            def prep_kmajor(dst, src, ncols):
                v = src.rearrange("(c p) (s n) -> s p c n", p=128, n=256)
                for s in range(ncols // 256):
                    k = wstate["i"] % NWS
                    wstate["i"] += 1
                    wt, wb_ = wslot[k]
                    S.dma("pool", wt[:].rearrange("p (c n) -> p c n", n=256), v[s], writes=[wb_])
                    S.dma("sp", dst[s], wt[:], reads=[wb_], writes=[sbuf_of(dst, s)])
                    yield

            def prep_w2(dst, src):
                v = src.rearrange("(pa s j p) (eh n) -> pa eh s p j n", pa=2, s=4, j=4, p=128, n=512)
                for pa in range(2):
                    for eh in range(2):
                        for s4 in range(4):
                            k = wstate["i"] % NWS
                            wstate["i"] += 1
                            wt, wb_ = wslot[k]
                            S.dma("pool", wt[:].rearrange("p (j n) -> p j n", n=512), v[pa, eh, s4], writes=[wb_])
                            S.dma("sp", dst[(pa * 2 + eh) * 4 + s4], wt[:], reads=[wb_],
                                  writes=[sbuf_of(dst, (pa * 2 + eh) * 4 + s4)])
                            yield

            for _ in prep_kmajor(sc_awin, a_w_in, 2048):
                pass
            for _ in prep_kmajor(sc_awout, a_w_out, 1024):
                pass

            def _pending():
                yield from prep_kmajor(sc_w1[0], mlp_w1[0], FF)
                yield from prep_w2(sc_w2[0], mlp_w2[0])
                yield from prep_kmajor(sc_bwin, b_w_in, 3072)
                yield from prep_kmajor(sc_bwout, b_w_out, 1024)
                yield from prep_kmajor(sc_w1[1], mlp_w1[1], FF)
                yield from prep_w2(sc_w2[1], mlp_w2[1])
            pend = {"g": _pending(), "alive": True}

            def pump(n):
                for _ in range(n):
                    if not pend["alive"]:
                        return
                    try:
                        next(pend["g"])
                    except StopIteration:
                        pend["alive"] = False

            chk('prologue')
            S.op("act", lambda e: e.activation(condT[:], cfm[:], AF.Silu), reads=[b_cfm], writes=[b_cond])
            for l in range(2):
                av = ada_w[l].rearrange("(c p) (s n) -> s p c n", p=128, n=256)
                for s in range(24):
                    k = wstate["i"] % NWS
                    wstate["i"] += 1
                    wt, wb_ = wslot[k]
                    S.dma("pool", wt[:].rearrange("p (c n) -> p c n", n=256), av[s], writes=[wb_])
                    bk = pb1()
                    for j in range(2):
                        for c in range(DC):
                            S.op("pe", lambda e: e.matmul(ps[:, bk, j * NB:(j + 1) * NB],
                                                          wt[:, c * 256 + j * 128: c * 256 + (j + 1) * 128],
                                                          condT[:, c, :], start=(c == 0), stop=(c == DC - 1)),
                                 reads=[wb_, b_cond], writes=[bps[bk]])
                    ec = s * 2
                    m, cc = ec // 8, ec % 8
                    S.op("dve", lambda e: e.tensor_tensor(
                        modf[:, l, m, cc:cc + 2, :], ps[:, bk, 0:2 * NB].rearrange("p (j b) -> p j b", b=NB),
                        adab[:, l, m, cc:cc + 2].unsqueeze(2).to_broadcast([128, 2, NB]), ALU.add),
                        reads=[bps[bk], b_adab], writes=[b_mod])
            for l in range(2):
                for k2 in range(2):
                    S.op("dve", lambda e: e.tensor_scalar(scf[:, l, k2], modf[:, l, 1 + 3 * k2], 1.0, 32.0, ALU.add, ALU.mult),
                         reads=[b_mod], writes=[b_scf])

            chk('ada', (modf[:].rearrange('p l m c b -> p (l m c b)'), b_mod, 2 * 6 * DC * NB))
            lnbs_t, b_lnbs = tmf[2]
            S.dma("sp", lnbs_t[:], a_b_s.partition_broadcast(128), writes=[b_lnbs])
            wtmp, b_wtmp = tmf[3]
            S.dma("sp", wtmp[:].rearrange("p (g s) -> p g s", s=CH), a_w_s.rearrange("g t s -> t g s"), writes=[b_wtmp])
            wm32, b_wm32 = tmf[4]
            for g in range(8):
                bk = pb1()
                S.op("pe", lambda e: e.transpose(ps[:, bk, 0:CH], wtmp[:, g * CH:(g + 1) * CH], identf[:]),
                     reads=[b_wtmp] + cst, writes=[bps[bk]])
                S.op("dve", lambda e: e.tensor_tensor(wm32[:, g * CH:(g + 1) * CH], ps[:, bk, 0:CH], mask4[:, 1, :], ALU.mult),
                     reads=[bps[bk]] + cst, writes=[b_wm32])
                S.op("act", lambda e: e.copy(wsT[:, g, :], wm32[:, g * CH:(g + 1) * CH]), reads=[b_wm32], writes=[b_wsT])
                bk2 = pb1()
                S.op("pe", lambda e: e.matmul(ps[:, bk2, 0:CH], onesf[:], wm32[:, g * CH:(g + 1) * CH], start=True, stop=True),
                     reads=[b_wm32] + cst, writes=[bps[bk2]])
                S.op("dve", lambda e: e.scalar_tensor_tensor(csgu[:, g, :], ps[:, bk2, 0:CH], alnb[:, g:g + 1],
                                                             lnbs_t[:, g * CH:(g + 1) * CH], ALU.mult, ALU.add),
                     reads=[bps[bk2], b_alnb, b_lnbs], writes=[b_csgu])

            chk('sguconst', (csgu[:].rearrange('p g t -> p (g t)'), b_csgu, 1024))
            def rms_mod(l, k2, b):
                sq, b_sq = slab[1]
                S.op("act", lambda e: e.activation(sq[:], xres[:], AF.Square), reads=[b_x], writes=[b_sq] + b_sl1)
                bk = pb1()
                for c in range(DC):
                    S.op("pe", lambda e: e.matmul(ps[:, bk, :], onesb[:], sq[:, c, :], start=(c == 0), stop=(c == DC - 1)),
                         reads=[b_sq] + b_sl1 + cst, writes=[bps[bk]])
                S.op("act", lambda e: e.activation(rstd[:], ps[:, bk, :], AF.Ln, bias=lnb_rms[:, 0:1], scale=1.0),
                     reads=[bps[bk]] + cst, writes=[b_rstd])
                S.op("act", lambda e: e.activation(rstd[:], rstd[:], AF.Exp, scale=-0.5), reads=[b_rstd], writes=[b_rstd])
                if l is None:
                    return
                for c in range(DC):
                    nt, b_nt = ntmp[c % 2]
                    S.op("dve", lambda e: e.scalar_tensor_tensor(nt[:], xres[:, c, :], scf[:, l, k2, c, b:b + 1], rstd[:],
                                                                 ALU.mult, ALU.mult),
                         reads=[b_x, b_scf, b_rstd], writes=[b_nt])
                    S.op("act", lambda e: e.activation(hb[:, c, :], nt[:], AF.Identity,
                                                       bias=modf[:, l, 3 * k2, c, b:b + 1], scale=1.0),
                         reads=[b_nt, b_mod], writes=[b_h])

            def resid_add(e_chunk, bk, l, k2, b, cols=slice(0, TT)):
                gate = modf[:, l, 2 + 3 * k2, e_chunk, b:b + 1]
                S.op("dve", lambda e: e.scalar_tensor_tensor(xres[:, e_chunk, cols], ps[:, bk, cols], gate,
                                                             xres[:, e_chunk, cols], ALU.mult, ALU.add),
                     reads=[bps[bk], b_mod, b_x], writes=[b_x])

            def proj_fm_out(scr, nsl, rhs_tile, b_rhs, sink):
                for s in range(nsl):
                    wt, wb_ = wload((scr, s))
                    for j in range(2):
                        bk = pb1()
                        for c in range(DC):
                            S.op("pe", lambda e: e.matmul(ps[:, bk, :], wt[:, c * 256 + j * 128: c * 256 + (j + 1) * 128],
                                                          rhs_tile[:, c, :], start=(c == 0), stop=(c == DC - 1)),
                                 reads=[wb_, b_rhs], writes=[bps[bk]])
                        sink(s * 2 + j, bk)

            def mlp(l, b):
                rms_mod(l, 1, b)
                hid, b_hid = slab[0]
                hid2, b_hid2 = slab[1]

                def hv(f):
                    return (hid if f < 8 else hid2)[:, f % 8, :], (b_hid if f < 8 else b_hid2)
                for pa in range(2):
                    for s in range(8):
                        wt, wb_ = wload((sc_w1[l], pa * 8 + s))
                        for j in range(2):
                            f = s * 2 + j
                            bk = pb1()
                            for c in range(DC):
                                S.op("pe", lambda e: e.matmul(ps[:, bk, :], wt[:, c * 256 + j * 128: c * 256 + (j + 1) * 128],
                                                              hb[:, c, :], start=(c == 0), stop=(c == DC - 1)),
                                     reads=[wb_, b_h], writes=[bps[bk]])
                            hap, hbuf = hv(f)
                            S.op("act", lambda e: e.activation(hap, ps[:, bk, :], AF.Relu), reads=[bps[bk]], writes=[hbuf])
                            S.op("pool", lambda e: e.tensor_tensor(hap, hap, hap, ALU.mult), reads=[hbuf], writes=[hbuf])
                    for eh in range(2):
                        bks = [pb1() for _ in range(4)]
                        for s4 in range(4):
                            wt, wb_ = wload((sc_w2[l], (pa * 2 + eh) * 4 + s4))
                            for j in range(4):
                                f = s4 * 4 + j
                                hap, hbuf = hv(f)
                                for e4 in range(4):
                                    S.op("pe", lambda e: e.matmul(ps[:, bks[e4], :],
                                                                  wt[:, j * 512 + e4 * 128: j * 512 + (e4 + 1) * 128],
                                                                  hap, start=(f == 0), stop=(f == 15)),
                                         reads=[wb_, hbuf], writes=[bps[bks[e4]]])
                        for e4 in range(4):
                            resid_add(eh * 4 + e4, bks[e4], l, 1, b)

            def sgu(b):
                l = 0
                rms_mod(l, 0, b)
                u, b_u = slab[0]
                def sink_u(ec, bk):
                    S.op("act", lambda e: e.activation(u[:, ec, :], ps[:, bk, :], AF.Gelu), reads=[bps[bk]], writes=[b_u])
                proj_fm_out(sc_awin, 4, hb, b_h, sink_u)
                s1, b_s1 = st16[0]
                s2, b_s2 = st16[1]
                S.op("pool", lambda e: e.memset(s1[:], 0.0), writes=[b_s1])
                S.op("pool", lambda e: e.memset(s2[:], 0.0), writes=[b_s2])
                for s in range(4):
                    wt, wb_ = wload((sc_awin, 4 + s))
                    for q in range(4):
                        bk = pb1()
                        for c in range(DC):
                            S.op("pe", lambda e: e.matmul(ps[:, bk, 0:256], hb[:, c, q * CH:(q + 1) * CH],
                                                          wt[:, c * 256:(c + 1) * 256], start=(c == 0), stop=(c == DC - 1)),
                                 reads=[wb_, b_h], writes=[bps[bk]])
                        vt, b_vt = tmf[q]
                        S.op("act", lambda e: e.activation(vt[:, s * 256:(s + 1) * 256], ps[:, bk, 0:256], AF.Gelu,
                                                           accum_out=s1[:, q * 4 + s: q * 4 + s + 1]),
                             reads=[bps[bk]], writes=[b_vt, b_s1])
                        jt, b_jt = ntmp[0]
                        S.op("act", lambda e: e.activation(jt[:, 0:256], vt[:, s * 256:(s + 1) * 256], AF.Square,
                                                           accum_out=s2[:, q * 4 + s: q * 4 + s + 1]),
                             reads=[b_vt], writes=[b_jt, b_s2])
                mean, b_mean = st16[2]
                ex2, b_ex2 = st16[3]
                rs, b_rs = st16[4]
                nb_, b_nb = st16[5]
                S.op("dve", lambda e: e.reduce_sum(mean[:, 0:4], s1[:].rearrange("p (q s) -> p q s", s=4), axis=AX.X),
                     reads=[b_s1], writes=[b_mean])
                S.op("dve", lambda e: e.reduce_sum(ex2[:, 0:4], s2[:].rearrange("p (q s) -> p q s", s=4), axis=AX.X),
                     reads=[b_s2], writes=[b_ex2])
                S.op("dve", lambda e: e.tensor_scalar(mean[:, 0:4], mean[:, 0:4], 1.0 / D, None, ALU.mult), reads=[b_mean], writes=[b_mean])
                S.op("dve", lambda e: e.tensor_tensor(rs[:, 0:4], mean[:, 0:4], mean[:, 0:4], ALU.mult), reads=[b_mean], writes=[b_rs])
                S.op("dve", lambda e: e.scalar_tensor_tensor(rs[:, 0:4], ex2[:, 0:4], 1.0 / D, rs[:, 0:4], ALU.mult, ALU.subtract),
                     reads=[b_ex2, b_rs], writes=[b_rs])
                S.op("act", lambda e: e.activation(rs[:, 0:4], rs[:, 0:4], AF.Ln, bias=lnb_rms[:, 1:2], scale=1.0), reads=[b_rs] + cst, writes=[b_rs])
                S.op("act", lambda e: e.activation(rs[:, 0:4], rs[:, 0:4], AF.Exp, scale=-0.5), reads=[b_rs], writes=[b_rs])
                S.op("dve", lambda e: e.scalar_tensor_tensor(nb_[:, 0:4], mean[:, 0:4], -1.0, rs[:, 0:4], ALU.mult, ALU.mult),
                     reads=[b_mean, b_rs], writes=[b_nb])
                for q in range(4):
                    vt, b_vt = tmf[q]
                    vn, b_vn = tmb[q % 2]
                    S.op("act", lambda e: e.activation(vn[:], vt[:], AF.Identity, bias=nb_[:, q:q + 1], scale=rs[:, q:q + 1]),
                         reads=[b_vt, b_rs, b_nb], writes=[b_vn])
                    for half in range(2):
                        bk = pb1()
                        for g4 in range(4):
                            g = half * 4 + g4
                            S.op("pe", lambda e: e.matmul(ps[:, bk, g4 * CH:(g4 + 1) * CH], vn[:, g * CH:(g + 1) * CH],
                                                          wsT[:, g, :], start=True, stop=True),
                                 reads=[b_vn, b_wsT], writes=[bps[bk]])
                        tt_, b_tt = ntmp[half]
                        for g4 in range(4):
                            g = half * 4 + g4
                            S.op("dve", lambda e: e.scalar_tensor_tensor(tt_[:, g4 * CH:(g4 + 1) * CH], ps[:, bk, g4 * CH:(g4 + 1) * CH],
                                                                         alng[:, g:g + 1], csgu[:, g, :], ALU.mult, ALU.add),
                                 reads=[bps[bk], b_alng, b_csgu], writes=[b_tt])
                        uv = u[:, half * 4:(half + 1) * 4, q * CH:(q + 1) * CH]
                        S.op("dve", lambda e: e.tensor_tensor(uv, tt_[:].rearrange("p (g t) -> p g t", t=CH), uv, ALU.mult),
                             reads=[b_tt, b_u], writes=[b_u])
                proj_fm_out(sc_awout, 4, u, b_u, lambda ec, bk: resid_add(ec, bk, l, 0, b))

            def rwkv(b, first_tile):
                l = 1
                rms_mod(l, 0, b)
                yg, b_yg = slab[0]
                hf_t, b_hf = Hf[b]
                hbd_t, b_hbd = Hbd[b]
                A, b_A = tmf[0]
                Fa, b_F = tmf[1]
                G, b_G = tmf[2]
                E1, b_E1 = tmf[3]
                E2, b_E2 = tmf[4]
                SW, b_SW = tmf[5]
                Kp, b_Kp = E2, b_E2
                khat, b_khat = tmb[0]
                bhat, b_bhat = tmb[1]
                vbf, b_vbf = tmb[2]
                tA, b_tA = tmb[3]
                tB, b_tB = tmb[4]
                ubf, b_ubf = tB, b_tB
                rhsbf, b_rhsbf = tA, b_tA
                ss, b_ss = st16[0]
                rn, b_rn = st16[1]
                g1s, b_g1s = st16[3]
                g2s, b_g2s = st16[4]
                g3s, b_g3s = st16[5]

                px0 = PX[0][:].rearrange("p h k t -> p (h k t)")
                px1 = PX[1][:].rearrange("p h k t -> p (h k t)")
                ring = [(wslot[i][0][:], [wslot[i][1]]) for i in range(NWS)] + [
                    (px0[:, 0:2048], [bPX[0][0], bPX[0][1]]), (px0[:, 2048:4096], [bPX[0][2], bPX[0][3]]),
                    (Q0[:].rearrange("p h t -> p (h t)"), [b_Q0] + bQb[0]),
                    (Qt_[:].rearrange("p h t -> p (h t)"), list(bQb[1]))]

                def h3(ap):
                    return ap.rearrange("p (h j) -> p h j", j=64)

                def bc16(ap16):
                    return ap16.unsqueeze(2).to_broadcast([128, NH, 64])

                def mix(i, q):
                    slot = {1: 0, 4: 1, 5: 2, 2: 3, 0: 4, 3: 5}[i]
                    if slot < 4:
                        t, bt = xi[slot]
                        tv = t[:]
                    else:
                        tv, bt = sl1[slot - 4].rearrange("p (c t) -> p c t", t=CH), b_sl1[slot - 4]
                    S.op("pool", lambda e: e.tensor_tensor(tv, xxt[:], muf[:, i, :].unsqueeze(2).to_broadcast([128, DC, CH]), ALU.mult),
                         reads=[b_xx, b_muf], writes=[bt])
                    S.op("dve", lambda e: e.tensor_tensor(tv, tv, hb[:, :, q * CH:(q + 1) * CH], ALU.add),
                         reads=[bt, b_h], writes=[bt])
                    return tv, bt

                def proj_tm(xt_, bxt_, sl0, bk2, extra=None):
                    for s in range(4):
                        if extra == "noring":
                            wt_, wb1 = wload((sc_bwin, sl0 + s))
                            wt, wbl = wt_[:], [wb1]
                        else:
                            wt, wbl = ring[rstate["i"] % len(ring)]
                            rstate["i"] += 1
                            S.dma("sp", wt, sc_bwin[sl0 + s], writes=wbl, reads=[sbuf_of(sc_bwin, sl0 + s)])
                        k = bk2 + s // 2
                        cols = slice((s % 2) * 256, (s % 2) * 256 + 256)
                        for c in range(DC):
                            S.op("pe", lambda e: e.matmul(ps[:, k, cols], xt_[:, c, :], wt[:, c * 256:(c + 1) * 256],
                                                          start=(c == 0), stop=(c == DC - 1)),
                                 reads=wbl + [bxt_], writes=[bps[k]])

                def transpose8(src, bsrc, dst_fn, bdst):
                    bk = pb1()
                    psb = ps[:, bk, :].bitcast(BF16)
                    for c in range(8):
                        S.op("pe", lambda e: e.transpose(psb[:, c * CH:(c + 1) * CH], src[:, c * CH:(c + 1) * CH], identb[:]),
                             reads=[bsrc] + cst, writes=[bps[bk]])
                    S.op("act", lambda e: e.copy(dst_fn, psb.rearrange("p (c t) -> p c t", t=CH)), reads=[bps[bk]], writes=[bdst])

                rbf, b_rbf = sl1[2], b_sl1[2]
                gbf, b_gbf = sl1[3], b_sl1[3]

                def genA(q):
                    q0 = q * CH
                    elc, b_elc = elc2[q % 2]
                    sbon, b_sbon = sbon2[q % 2]
                    if q == 0:
                        if first_tile:
                            S.op("pool", lambda e: e.memset(carry[:], 0.0), writes=[b_carry])
                        S.op("dve", lambda e: e.tensor_tensor(xxt[:, :, 0:1], carry[:, :, 0:1], hb[:, :, 0:1], ALU.subtract),
                             reads=[b_carry, b_h], writes=[b_xx])
                        S.op("dve", lambda e: e.tensor_tensor(xxt[:, :, 1:CH], hb[:, :, 0:CH - 1], hb[:, :, 1:CH], ALU.subtract),
                             reads=[b_h], writes=[b_xx])
                    else:
                        S.op("dve", lambda e: e.tensor_tensor(xxt[:], hb[:, :, q0 - 1:q0 + CH - 1], hb[:, :, q0:q0 + CH], ALU.subtract),
                             reads=[b_h], writes=[b_xx])
                    if q == 3:
                        S.op("pool", lambda e: e.tensor_copy(carry[:, :, 0:1], hb[:, :, TT - 1:TT]), reads=[b_h, b_xx], writes=[b_carry])
                    xk, bxk = mix(2, q)
                    xa, bxa = mix(4, q)
                    b_lw, b_la, b_lg = b_lorw, b_lora, b_lorg
                    bkk = pb2()
                    proj_tm(xk, bxk, 4, bkk)
                    kps = ps[:, bkk:bkk + 2, :].rearrange("p a n -> p (a n)")
                    bkps = [bps[bkk], bps[bkk + 1]]
                    held.update((bkk, bkk + 1))
                    S.op("dve", lambda e: e.tensor_tensor(G[:], kps, rows[:, 0, :], ALU.mult), reads=bkps + [b_rows], writes=[b_G])
                    S.op("act", lambda e: e.activation(SW[:], G[:], AF.Square), reads=[b_G], writes=[b_SW])
                    S.op("dve", lambda e: e.reduce_sum(ss[:], h3(SW[:]), axis=AX.X), reads=[b_SW], writes=[b_ss])
                    S.op("act", lambda e: e.activation(rn[:], ss[:], AF.Ln, bias=lnb_rms[:, 2:3], scale=1.0), reads=[b_ss] + cst, writes=[b_rn])
                    S.op("act", lambda e: e.activation(rn[:], rn[:], AF.Exp, scale=-0.5), reads=[b_rn], writes=[b_rn])
                    S.op("dve", lambda e: e.tensor_tensor(h3(G[:]), h3(G[:]), bc16(rn[:]), ALU.mult), reads=[b_G, b_rn], writes=[b_G])
                    yield
                    xw, bxw = mix(1, q)
                    xr, bxr = mix(0, q)

                    def zproj(which, dst, bdst, l2, blor):
                        bk2 = pb2()
                        for hlf in range(2):
                            cs = slice(hlf * 512, (hlf + 1) * 512)
                            S.op("pe", lambda e: e.matmul(ps[:, bk2 + hlf, :], onesf[0:1, :], w0a0[0:1, which, cs], start=True, stop=False),
                                 reads=[b_w0a0] + cst, writes=[bps[bk2 + hlf]])
                            S.op("pe", lambda e: e.matmul(ps[:, bk2 + hlf, :], lor[0:64, which, :], l2[:, cs], start=False, stop=True),
                                 reads=[blor, b_lw2, b_la2], writes=[bps[bk2 + hlf]])
                        S.op("act", lambda e: e.activation(dst[:], ps[:, bk2:bk2 + 2, :].rearrange("p a n -> p (a n)"), AF.Sigmoid),
                             reads=[bps[bk2], bps[bk2 + 1]], writes=[bdst])
                    bkl = pb1()
                    for c in range(DC):
                        S.op("pe", lambda e: e.matmul(ps[0:64, bkl, 0:CH], la1[:, c, :], xa[:, c, :], start=(c == 0), stop=(c == DC - 1)),
                             reads=[b_la1, bxa], writes=[bps[bkl]])
                    S.op("act", lambda e: e.copy(lor[0:64, 1, :], ps[0:64, bkl, 0:CH]), reads=[bps[bkl]], writes=[b_la])
                    zproj(1, Fa, b_F, la2, b_la)
                    S.op("dve", lambda e: e.scalar_tensor_tensor(A[:], Fa[:], -1.0, rows[:, 1, :], ALU.add, ALU.mult), reads=[b_F, b_rows], writes=[b_A])
                    S.op("dve", lambda e: e.scalar_tensor_tensor(A[:], A[:], 1.0, kps, ALU.add, ALU.mult), reads=bkps + [b_A], writes=[b_A])
                    held.difference_update((bkk, bkk + 1))
                    S.op("dve", lambda e: e.tensor_tensor(Fa[:], G[:], Fa[:], ALU.mult), reads=[b_G, b_F], writes=[b_F])
                    bkl = pb1()
                    for c in range(DC):
                        S.op("pe", lambda e: e.matmul(ps[0:64, bkl, 0:CH], lw1[:, c, :], xw[:, c, :], start=(c == 0), stop=(c == DC - 1)),
                             reads=[b_lw1, bxw], writes=[bps[bkl]])
                    S.op("act", lambda e: e.activation(lor[0:64, 0, :], ps[0:64, bkl, 0:CH], AF.Tanh), reads=[bps[bkl]], writes=[b_lw])
                    zproj(0, SW, b_SW, lw2, b_lw)
                    chk('r2')
                    yield
                    def lmat(kind):
                        bk2 = pb2()
                        for hlf in range(2):
                            S.op("pe", lambda e: e.matmul(ps[:, bk2 + hlf, :], tri[:, kind, :], SW[:, hlf * 512:(hlf + 1) * 512], start=True, stop=True),
                                 reads=[b_SW] + cst, writes=[bps[bk2 + hlf]])
                        return bk2, ps[:, bk2:bk2 + 2, :].rearrange("p a n -> p (a n)")
                    bke = pb1()
                    for p_ in range(8):
                        S.op("pe", lambda e: e.matmul(ps[:, bke, 2 * p_:2 * p_ + 2], SW[:, p_ * CH:(p_ + 1) * CH], negc[:, 0:2], start=True, stop=True),
                             reads=[b_SW] + cst, writes=[bps[bke]])
                    S.op("act", lambda e: e.activation(elc[:], ps[:, bke, 0:16].rearrange("p (a two) -> p a two", two=2)[:, :, 0], AF.Exp),
                         reads=[bps[bke]], writes=[b_elc])
                    chk('r3')
                    yield
                    bkr = pb2()
                    proj_tm(xr, bxr, 0, bkr)
                    S.op("act", lambda e: e.copy(rbf, ps[:, bkr:bkr + 2, :].rearrange("p a n -> p (a n)")),
                         reads=[bps[bkr], bps[bkr + 1]], writes=[b_rbf])
                    chk('r4')
                    yield "barrier"
                    bkL, Lps = lmat(2)
                    S.op("act", lambda e: e.activation(E1[:], Lps, AF.Exp), reads=[bps[bkL], bps[bkL + 1]], writes=[b_E1])
                    S.op("dve", lambda e: e.tensor_tensor(khat[:], A[:], E1[:], ALU.mult), reads=[b_A, b_E1], writes=[b_khat])
                    S.op("pool", lambda e: e.tensor_tensor(bhat[:], Fa[:], E1[:], ALU.mult), reads=[b_F, b_E1], writes=[b_bhat])
                    bkL, Lps = lmat(0)
                    S.op("act", lambda e: e.activation(E2[:], Lps, AF.Exp, scale=-1.0), reads=[bps[bkL], bps[bkL + 1]], writes=[b_E2])
                    S.op("act", lambda e: e.activation(E1[:], Lps, AF.Exp), reads=[bps[bkL], bps[bkL + 1]], writes=[b_E1])
                    S.op("dve", lambda e: e.tensor_tensor(tA[:], A[:], E2[:], ALU.mult), reads=[b_A, b_E2], writes=[b_tA])
                    transpose8(tA, b_tA, kbfm[:, :, 0, :], b_kbfm)
                    S.op("dve", lambda e: e.tensor_tensor(tB[:], Fa[:], E2[:], ALU.mult), reads=[b_F, b_E2], writes=[b_tB])
                    transpose8(tB, b_tB, kbfm[:, :, 1, :], b_kbfm)
                    chk('r5')
                    yield
                    S.op("dve", lambda e: e.tensor_tensor(tA[:], rbf, E1[:], ALU.mult), reads=[b_rbf, b_E1], writes=[b_tA])
                    transpose8(tA, b_tA, arfm[:, :, 1, :], b_arfm)
                    S.op("dve", lambda e: e.tensor_tensor(Kp[:], rbf, A[:], ALU.mult), reads=[b_rbf, b_A], writes=[b_Kp])
                    S.op("pool", lambda e: e.tensor_tensor(Kp[:], Kp[:], rows[:, 2, :], ALU.mult), reads=[b_Kp, b_rows], writes=[b_Kp])
                    S.op("dve", lambda e: e.reduce_sum(sbon[:], h3(Kp[:]), axis=AX.X), reads=[b_Kp], writes=[b_sbon])
                    bkL, Lps = lmat(1)
                    S.op("act", lambda e: e.activation(E2[:], Lps, AF.Exp), reads=[bps[bkL], bps[bkL + 1]], writes=[b_E2])
                    S.op("dve", lambda e: e.scalar_tensor_tensor(tB[:], G[:], -1.0, E2[:], ALU.mult, ALU.mult), reads=[b_G, b_E2], writes=[b_tB])
                    transpose8(tB, b_tB, arfm[:, :, 0, :], b_arfm)
                    chk('r6')
                    yield
                    for p_ in range(8):
                        bkA = [pb1(), pb1()]
                        for hh in range(2):
                            pr = slice(hh * 64, hh * 64 + 64)
                            rhs_ar = arfm[pr, p_, :, :].rearrange("p a t -> p (a t)")
                            S.op("pe", lambda e: e.matmul(ps[:, bkA[hh], 0:256], kbfm[pr, p_, 0, :], rhs_ar, start=True, stop=True),
                                 reads=[b_kbfm, b_arfm], writes=[bps[bkA[hh]]])
                        for hh in range(2):
                            pr = slice(hh * 64, hh * 64 + 64)
                            rhs_ar = arfm[pr, p_, :, :].rearrange("p a t -> p (a t)")
                            S.op("pe", lambda e: e.matmul(ps[:, bkA[hh], 256:512], kbfm[pr, p_, 1, :], rhs_ar, start=True, stop=True),
                                 reads=[b_kbfm, b_arfm], writes=[bps[bkA[hh]]])
                        for hh in range(2):
                            h = 2 * p_ + hh
                            S.op("dve", lambda e: e.tensor_tensor(MKB[:, h], ps[:, bkA[hh], :].rearrange("p (a t) -> p a t", t=CH),
                                                                  mask4[:], ALU.mult),
                                 reads=[bps[bkA[hh]]] + cst, writes=[b_MKB])
                        yield
                    Q0v = Q0[:].rearrange("p (c two) t -> p c two t", two=2)
                    for pg in range(2):
                        bkC = [pb1(), pb1()]
                        for i4 in range(4):
                            p_ = pg * 4 + i4
                            for hh in range(2):
                                pr = slice(hh * 64, hh * 64 + 64)
                                S.op("pe", lambda e: e.matmul(ps[:, bkC[hh], i4 * CH:(i4 + 1) * CH], arfm[pr, p_, 0, :], kbfm[pr, p_, 1, :],
                                                              start=True, stop=True),
                                     reads=[b_kbfm, b_arfm], writes=[bps[bkC[hh]]])
                        for hh in range(2):
                            S.op("dve", lambda e: e.tensor_tensor(Q0v[:, pg * 4:(pg + 1) * 4, hh, :],
                                                                  ps[:, bkC[hh], :].rearrange("p (a t) -> p a t", t=CH), maskL[:], ALU.mult),
                                 reads=[bps[bkC[hh]]] + cst, writes=[b_Q0] + bQb[0])
                        yield
                    xg, bxg = mix(5, q)
                    xv_, bxv = mix(3, q)
                    bkv = pb2()
                    held.update((bkv, bkv + 1))

                    def v_slice(s_):
                        wt_, wb1 = wload((sc_bwin, 8 + s_))
                        k = bkv + s_ // 2
                        cols = slice((s_ % 2) * 256, (s_ % 2) * 256 + 256)
                        for c in range(DC):
                            S.op("pe", lambda e: e.matmul(ps[:, k, cols], xv_[:, c, :], wt_[:, c * 256:(c + 1) * 256],
                                                          start=(c == 0), stop=(c == DC - 1)),
                                 reads=[wb1, bxv], writes=[bps[k]])

                    def v_fin():
                        S.op("act", lambda e: e.copy(vbf[:], ps[:, bkv:bkv + 2, :].rearrange("p a n -> p (a n)")),
                             reads=[bps[bkv], bps[bkv + 1]], writes=[b_vbf])
                        held.difference_update((bkv, bkv + 1))

                    def g_all():
                        bkl = pb1()
                        for c in range(DC):
                            S.op("pe", lambda e: e.matmul(ps[:, bkl, 0:CH], lg1[:, c, 0:128], xg[:, c, :], start=(c == 0), stop=(c == DC - 1)),
                                 reads=[b_lg1, bxg], writes=[bps[bkl]])
                        for c in range(DC):
                            S.op("pe", lambda e: e.matmul(ps[0:32, bkl, CH:2 * CH], lg1[:, c, 128:160], xg[:, c, :], start=(c == 0), stop=(c == DC - 1)),
                                 reads=[b_lg1, bxg], writes=[bps[bkl]])
                        S.op("act", lambda e: e.activation(lor[:, 2, :], ps[:, bkl, 0:CH], AF.Sigmoid), reads=[bps[bkl]], writes=[b_lg])
                        S.op("act", lambda e: e.activation(lor[0:32, 3, :], ps[0:32, bkl, CH:2 * CH], AF.Sigmoid), reads=[bps[bkl]], writes=[b_lg])
                        bkg = pb2()
                        for hlf in range(2):
                            cs = slice(hlf * 512, (hlf + 1) * 512)
                            S.op("pe", lambda e: e.matmul(ps[:, bkg + hlf, :], lor[:, 2, :], lg2a[:, cs], start=True, stop=False),
                                 reads=[b_lg, b_lg2a], writes=[bps[bkg + hlf]])
                            S.op("pe", lambda e: e.matmul(ps[:, bkg + hlf, :], lor[0:32, 3, :], lg2b[:, cs], start=False, stop=True),
                                 reads=[b_lg, b_lg2b], writes=[bps[bkg + hlf]])
                        S.op("act", lambda e: e.copy(gbf, ps[:, bkg:bkg + 2, :].rearrange("p a n -> p (a n)")),
                             reads=[bps[bkg], bps[bkg + 1]], writes=[b_gbf])
                    chk('r7')
                    yield
                    Qbuf = [lambda h: Q0[:, h, :], lambda h: Qt_[:, h, :]]
                    QbufG = [lambda hs: Q0[:, hs, :], lambda hs: Qt_[:, hs, :]]

                    def v4(bk, n):
                        return ps[:, bk, 0:4 * n].rearrange("p (a t) -> p a t", t=n)
                    evq = 0
                    for gi in range(4):
                        hs = slice(gi * 4, gi * 4 + 4)
                        S.op("dve", lambda e: e.tensor_tensor(PX[1][:, hs, 1, :], MKB[:, hs, 2, :],
                                                               identb[:].unsqueeze(1).to_broadcast([128, 4, CH]), ALU.add),
                             reads=[b_MKB] + cst, writes=[bPX[1][gi]])
                        bkp = pb1()
                        bkq = pb1()
                        for j in range(4):
                            h = gi * 4 + j
                            S.op("pe", lambda e: e.matmul(ps[:, bkp, j * CH:(j + 1) * CH], Q0[:, h, :], MKB[:, h, 2, :], start=True, stop=True),
                                 reads=[b_MKB, b_Q0, bQb[0][gi]], writes=[bps[bkp]])
                        for j in range(4):
                            h = gi * 4 + j
                            S.op("pe", lambda e: e.matmul(ps[:, bkq, j * CH:(j + 1) * CH], MKB[:, h, 2, :], Q0[:, h, :], start=True, stop=True),
                                 reads=[b_MKB, b_Q0, bQb[0][gi]], writes=[bps[bkq]])
                        S.op("act", lambda e: e.copy(PX[1][:, hs, 0, :], v4(bkp, CH)), reads=[bps[bkp]], writes=[bPX[1][gi]])
                        S.op("dve", lambda e: e.tensor_copy(Qt_[:, hs, :], v4(bkq, CH)), reads=[bps[bkq]], writes=[bQb[1][gi]])
                        yield
                    for t_ in range(2, 8):
                        si, di = (t_ - 1) % 2, t_ % 2
                        if t_ <= 5:
                            v_slice(t_ - 2)
                        elif t_ == 6:
                            v_fin()
                            g_all()
                        for gi in range(4):
                            hs = slice(gi * 4, gi * 4 + 4)
                            rdp = [bPX[si][gi], bQb[si][gi]]
                            if t_ <= 5:
                                bk2 = pb2()
                                for j in range(4):
                                    h = gi * 4 + j
                                    S.op("pe", lambda e: e.matmul(ps[:, bk2 + j // 2, (j % 2) * 256:(j % 2) * 256 + 256], Qbuf[si](h),
                                                                  PX[si][:, h, :, :].rearrange("p a t -> p (a t)"), start=True, stop=True),
                                         reads=rdp, writes=[bps[bk2], bps[bk2 + 1]])
                                pv = ps[:, bk2:bk2 + 2, :].rearrange("p a (h k t) -> p (a h) k t", k=2, t=CH)
                                S.op("act", lambda e: e.copy(PX[di][:, hs, 0, :], pv[:, :, 0, :]), reads=[bps[bk2], bps[bk2 + 1]], writes=[bPX[di][gi]])
                                S.op("dve", lambda e: e.tensor_tensor(PX[di][:, hs, 1, :], pv[:, :, 1, :], PX[si][:, hs, 1, :], ALU.add),
                                     reads=[bps[bk2], bps[bk2 + 1], bPX[si][gi]], writes=[bPX[di][gi]])
                            else:
                                bkx = pb1()
                                for j in range(4):
                                    h = gi * 4 + j
                                    S.op("pe", lambda e: e.matmul(ps[:, bkx, j * CH:(j + 1) * CH], Qbuf[si](h), PX[si][:, h, 1, :], start=True, stop=True),
                                         reads=rdp, writes=[bps[bkx]])
                                S.op("dve", lambda e: e.tensor_tensor(PX[di][:, hs, 1, :], v4(bkx, CH), PX[si][:, hs, 1, :], ALU.add),
                                     reads=[bps[bkx], bPX[si][gi]], writes=[bPX[di][gi]])
                            if t_ <= 6:
                                bkq = pb1()
                                for j in range(4):
                                    h = gi * 4 + j
                                    S.op("pe", lambda e: e.matmul(ps[:, bkq, j * CH:(j + 1) * CH], PX[si][:, h, 0, :], Qbuf[si](h), start=True, stop=True),
                                         reads=rdp, writes=[bps[bkq]])
                                evq += 1
                                S.op("act" if evq % 2 else "dve",
                                     (lambda e: e.copy(QbufG[di](hs), v4(bkq, CH))) if evq % 2 else (lambda e: e.tensor_copy(QbufG[di](hs), v4(bkq, CH))),
                                     reads=[bps[bkq]], writes=[bQb[di][gi]])
                            yield
                    chk('r8')
                    yield

                def genB(q):
                    q0 = q * CH
                    elc, b_elc = elc2[q % 2]
                    sbon, b_sbon = sbon2[q % 2]
                    bkR = pb2()
                    for h in range(NH):
                        k = bkR + h // 8
                        cs = slice((h % 8) * 64, (h % 8) * 64 + 64)
                        S.op("pe", lambda e: e.matmul(ps[:, k, cs], MKB[:, h, 0, :], vbf[:, h * 64:(h + 1) * 64], start=True, stop=False),
                             reads=[b_MKB, b_vbf], writes=[bps[k]])
                        S.op("pe", lambda e: e.matmul(ps[:, k, cs], arfm[:, h // 2, 0, :], hbd_t[:, h // 2, (h % 2) * 64:(h % 2) * 64 + 64],
                                                      start=False, stop=True),
                             reads=[b_arfm, b_hbd], writes=[bps[k]])
                    S.op("act", lambda e: e.copy(rhsbf[:], ps[:, bkR:bkR + 2, :].rearrange("p a n -> p (a n)")),
                         reads=[bps[bkR], bps[bkR + 1]], writes=[b_rhsbf])
                    yield
                    bkU = pb2()
                    for h in range(NH):
                        k = bkU + h // 8
                        cs = slice((h % 8) * 64, (h % 8) * 64 + 64)
                        S.op("pe", lambda e: e.matmul(ps[:, k, cs], PX[1][:, h, 1, :], rhsbf[:, h * 64:(h + 1) * 64], start=True, stop=True),
                             reads=bPX[1] + [b_rhsbf], writes=[bps[k]])
                    S.op("dve", lambda e: e.tensor_copy(ubf[:], ps[:, bkU:bkU + 2, :].rearrange("p a n -> p (a n)")),
                         reads=[bps[bkU], bps[bkU + 1]], writes=[b_ubf])
                    yield
                    bkY = pb2()
                    for h in range(NH):
                        k = bkY + h // 8
                        cs = slice((h % 8) * 64, (h % 8) * 64 + 64)
                        S.op("pe", lambda e: e.matmul(ps[:, k, cs], MKB[:, h, 3, :], ubf[:, h * 64:(h + 1) * 64], start=True, stop=False),
                             reads=[b_MKB, b_ubf], writes=[bps[k]])
                        S.op("pe", lambda e: e.matmul(ps[:, k, cs], MKB[:, h, 1, :], vbf[:, h * 64:(h + 1) * 64], start=False, stop=False),
                             reads=[b_MKB, b_vbf], writes=[bps[k]])
                        S.op("pe", lambda e: e.matmul(ps[:, k, cs], arfm[:, h // 2, 1, :], hbd_t[:, h // 2, (h % 2) * 64:(h % 2) * 64 + 64],
                                                      start=False, stop=True),
                             reads=[b_arfm, b_hbd], writes=[bps[k]])
                    yps = ps[:, bkY:bkY + 2, :].rearrange("p a n -> p (a n)")
                    bY = [bps[bkY], bps[bkY + 1]]
                    held.update((bkY, bkY + 1))
                    yield
                    bkH = pb2()
                    for p_ in range(8):
                        k = bkH + p_ // 4
                        cs = slice((p_ % 4) * CH, (p_ % 4) * CH + CH)
                        S.op("pe", lambda e: e.matmul(ps[:, k, cs], khat[:, p_ * CH:(p_ + 1) * CH], vbf[:, p_ * CH:(p_ + 1) * CH], start=True, stop=False),
                             reads=[b_khat, b_vbf], writes=[bps[k]])
                        S.op("pe", lambda e: e.matmul(ps[:, k, cs], bhat[:, p_ * CH:(p_ + 1) * CH], ubf[:, p_ * CH:(p_ + 1) * CH], start=False, stop=True),
                             reads=[b_bhat, b_ubf], writes=[bps[k]])
                    S.op("dve", lambda e: e.tensor_tensor(hf_t[:], hf_t[:], elc[:].unsqueeze(2).to_broadcast([128, 8, 64]), ALU.mult),
                         reads=[b_hf, b_elc], writes=[b_hf])
                    dH = ps[:, bkH:bkH + 2, :].rearrange("p a (c n) -> p (a c) n", n=CH)
                    for hh in range(2):
                        pr = slice(hh * 64, hh * 64 + 64)
                        S.op("dve", lambda e: e.tensor_tensor(hf_t[pr], hf_t[pr], dH[pr, :, hh * 64:hh * 64 + 64], ALU.add),
                             reads=[b_hf, bps[bkH], bps[bkH + 1]], writes=[b_hf])
                        S.op("act", lambda e: e.copy(hbd_t[pr, :, hh * 64:hh * 64 + 64], hf_t[pr]), reads=[b_hf], writes=[b_hbd])
                    chk('r9')
                    yield
                    S.op("dve", lambda e: e.reduce_sum(g1s[:], h3(yps), axis=AX.X), reads=bY, writes=[b_g1s])
                    S.op("act", lambda e: e.activation(E1[:], yps, AF.Square), reads=bY, writes=[b_E1])
                    S.op("dve", lambda e: e.reduce_sum(g2s[:], h3(E1[:]), axis=AX.X), reads=[b_E1], writes=[b_g2s])
                    S.op("dve", lambda e: e.tensor_scalar(g1s[:], g1s[:], 1.0 / 64, None, ALU.mult), reads=[b_g1s], writes=[b_g1s])
                    S.op("dve", lambda e: e.tensor_tensor(g3s[:], g1s[:], g1s[:], ALU.mult), reads=[b_g1s], writes=[b_g3s])
                    S.op("dve", lambda e: e.scalar_tensor_tensor(g3s[:], g2s[:], 1.0 / 64, g3s[:], ALU.mult, ALU.subtract),
                         reads=[b_g2s, b_g3s], writes=[b_g3s])
                    S.op("act", lambda e: e.activation(g3s[:], g3s[:], AF.Ln, bias=lnb_rms[:, 3:4], scale=1.0), reads=[b_g3s] + cst, writes=[b_g3s])
                    S.op("act", lambda e: e.activation(g3s[:], g3s[:], AF.Exp, scale=-0.5), reads=[b_g3s], writes=[b_g3s])
                    S.op("dve", lambda e: e.tensor_tensor(h3(E1[:]), h3(yps), bc16(g1s[:]), ALU.subtract), reads=bY + [b_g1s], writes=[b_E1])
                    held.difference_update((bkY, bkY + 1))
                    yield
                    S.op("dve", lambda e: e.tensor_tensor(h3(E1[:]), h3(E1[:]), bc16(g3s[:]), ALU.mult), reads=[b_E1, b_g3s], writes=[b_E1])
                    S.op("dve", lambda e: e.tensor_tensor(E1[:], E1[:], rows[:, 3, :], ALU.mult), reads=[b_E1, b_rows], writes=[b_E1])
                    S.op("dve", lambda e: e.tensor_tensor(E1[:], E1[:], rows[:, 4, :], ALU.add), reads=[b_E1, b_rows], writes=[b_E1])
                    S.op("dve", lambda e: e.tensor_tensor(h3(tA[:]), h3(vbf[:]), bc16(sbon[:]), ALU.mult), reads=[b_vbf, b_sbon], writes=[b_tA])
                    S.op("dve", lambda e: e.tensor_tensor(E1[:], E1[:], tA[:], ALU.add), reads=[b_E1, b_tA], writes=[b_E1])
                    S.op("dve", lambda e: e.tensor_tensor(tA[:], E1[:], gbf, ALU.mult), reads=[b_E1, b_gbf], writes=[b_tA])
                    transpose8(tA, b_tA, yg[:, :, q0:q0 + CH], b_yg)
                def run(g):
                    for _ in g:
                        pass

                def interleave(ga, gb):
                    alive_a, alive_b = True, True
                    while alive_a or alive_b:
                        if alive_b:
                            try:
                                next(gb)
                            except StopIteration:
                                alive_b = False
                        if alive_a:
                            try:
                                if next(ga) == "barrier" and alive_b:
                                    run(gb)
                                    alive_b = False
                            except StopIteration:
                                alive_a = False
                run(genA(0))
                for q in range(4):
                    if q < 3 and os.environ.get("KNOIL") != "1":
                        interleave(genA(q + 1), genB(q))
                    else:
                        run(genB(q))
                        if q < 3:
                            run(genA(q + 1))
                proj_fm_out(sc_bwout, 4, yg, b_yg, lambda ec, bk: resid_add(ec, bk, l, 0, b))

            for b in range(NB):
                hf_t, b_hf = Hf[b]
                hbd_t, b_hbd = Hbd[b]
                S.op("pool", lambda e: e.memset(hf_t[:], 0.0), writes=[b_hf])
                S.op("pool", lambda e: e.memset(hbd_t[:], 0.0), writes=[b_hbd])
                for ti in range(NT):
                    t0 = ti * TT
                    for q in range(4):
                        xt_, b_xt = tmf[q]
                        S.dma("sp", xt_[:], x_d[b, t0 + q * CH: t0 + (q + 1) * CH, :], writes=[b_xt])
                    for half in range(2):
                        for c4 in range(4):
                            c = half * 4 + c4
                            bk = pb1()
                            for q in range(4):
                                xt_, b_xt = tmf[q]
                                S.op("pe", lambda e: e.transpose(ps[:, bk, q * CH:(q + 1) * CH], xt_[:, c * CH:(c + 1) * CH], identf[:]),
                                     reads=[b_xt] + cst, writes=[bps[bk]])
                            S.op("act" if c % 2 else "dve",
                                 (lambda e: e.copy(xres[:, c, :], ps[:, bk, :])) if c % 2 else (lambda e: e.tensor_copy(xres[:, c, :], ps[:, bk, :])),
                                 reads=[bps[bk]], writes=[b_x])
                    chk('load', (xres[:].rearrange('p c t -> p (c t)'), b_x, 4096))
                    sgu(b)
                    chk('sgu', (xres[:].rearrange('p c t -> p (c t)'), b_x, 4096))
                    mlp(0, b)
                    chk('mlp0', (xres[:].rearrange('p c t -> p (c t)'), b_x, 4096))
                    pump(1000)
                    rwkv(b, ti == 0)
                    chk('rwkv', (xres[:].rearrange('p c t -> p (c t)'), b_x, 4096))
                    mlp(1, b)
                    rms_mod(None, 0, b)
                    yf, b_yf = slab[0]
                    yff = yf[:].rearrange("p c t -> p (c t)").bitcast(F32).rearrange("p (c t) -> p c t", t=TT // 2)
                    for hlf in range(2):
                        tsl = slice(hlf * 256, (hlf + 1) * 256)
                        for c in range(DC):
                            S.op("dve", lambda e: e.scalar_tensor_tensor(yff[:, c, :], xres[:, c, tsl], fgf[:, c:c + 1], rstd[:, tsl],
                                                                         ALU.mult, ALU.mult),
                                 reads=[b_x, b_fgf, b_rstd], writes=[b_yf])
                        for q2 in range(2):
                            q = hlf * 2 + q2
                            ot, b_ot = tmf[4 + q % 2]
                            for c4 in range(2):
                                bk = pb1()
                                for cc in range(4):
                                    c = c4 * 4 + cc
                                    S.op("pe", lambda e: e.transpose(ps[:, bk, cc * CH:(cc + 1) * CH], yff[:, c, q2 * CH:(q2 + 1) * CH], identf[:]),
                                         reads=[b_yf] + cst, writes=[bps[bk]])
                                S.op("act" if c4 else "dve",
                                     (lambda e: e.copy(ot[:, c4 * 512:(c4 + 1) * 512], ps[:, bk, :])) if c4 else
                                     (lambda e: e.tensor_copy(ot[:, c4 * 512:(c4 + 1) * 512], ps[:, bk, :])),
                                     reads=[bps[bk]], writes=[b_ot])
                            S.dma("pool", out_d[b, t0 + q * CH: t0 + (q + 1) * CH, :], ot[:], reads=[b_ot])
      except _Stop:
          pass
      for e_ in ("sp", "pool"):
          S.finish(e_)
    return nc


b_outd = Buf("out_dram")

_NC_CACHE = {}


def kernel(**inputs):
    n = 8
    x = np.ascontiguousarray(inputs["x"], dtype=np.float32)
    B, T, _ = x.shape
    NB = B // n
    key = (T, NB)
    if key not in _NC_CACHE:
        _NC_CACHE[key] = build(T, NB)
    nc = _NC_CACHE[key]
    shared = {}
    for k, v in inputs.items():
        if k in ("x", "c"):
            continue
        a = np.ascontiguousarray(v, dtype=np.float32)
        if k.startswith("a_") or k.startswith("b_"):
            a = a[0]
        if k in ("a_b_s", "b_r_k"):
            a = a.reshape(-1)
        shared[k] = np.ascontiguousarray(a)
    c = np.ascontiguousarray(inputs["c"], dtype=np.float32)
    in_maps = []
    for i in range(n):
        m = dict(shared)
        m["x"] = np.ascontiguousarray(x[i * NB:(i + 1) * NB])
        m["c"] = np.ascontiguousarray(c[i * NB:(i + 1) * NB])
        in_maps.append(m)
    res = run_bass_kernel_spmd(nc, in_maps, core_ids=list(range(n)))
    return np.concatenate([r["out"] for r in res.results], axis=0).astype(np.float32)
```

```python
import os
import numpy as np
from contextlib import ExitStack
import concourse.bass as bass
import concourse.mybir as mybir
from concourse.bass_utils import run_bass_kernel_spmd

F32 = mybir.dt.float32
BF16 = mybir.dt.bfloat16
AF = mybir.ActivationFunctionType
ALU = mybir.AluOpType
AX = mybir.AxisListType

D = 1024
DC = 8
FF = 4096
TT = 512
CH = 128
NH = 16
DECAY_C = 0.6065306597126334


class Buf:
    __slots__ = ("name", "w", "r")

    def __init__(self, name):
        self.name = name
        self.w = None
        self.r = {}


class Sched:
    NDMA = 24

    def __init__(self, nc, stack):
        self.nc = nc
        self.eng = {"pe": nc.tensor, "dve": nc.vector, "act": nc.scalar,
                    "pool": nc.gpsimd, "sp": nc.sync}
        self.sem = {}
        for k in self.eng:
            self.sem[k] = stack.enter_context(nc.semaphore("s_" + k))
        self.cnt = {k: 0 for k in self.eng}
        self.waited = {k: {} for k in self.eng}
        for j in range(self.NDMA):
            key = "d%d" % j
            self.sem[key] = stack.enter_context(nc.semaphore("s_" + key))
            self.cnt[key] = 0
        self.dma_i = 0
        self.sw_i = 0

    def _wait(self, e, deps):
        eng = self.eng[e]
        wd = self.waited[e]
        for key, val in deps.items():
            if key == "pe" and e == "pe":
                continue
            if wd.get(key, 0) < val:
                eng.wait_ge(self.sem[key], val)
                wd[key] = val

    @staticmethod
    def _add(deps, tok):
        if tok is None:
            return
        k, v = tok
        if deps.get(k, 0) < v:
            deps[k] = v

    def _collect(self, reads, writes):
        deps = {}
        for b in reads:
            self._add(deps, b.w)
        for b in writes:
            self._add(deps, b.w)
            for k, v in b.r.items():
                self._add(deps, (k, v))
        return deps

    def _commit(self, tok, reads, writes):
        k, v = tok
        for b in reads:
            if b.r.get(k, 0) < v:
                b.r[k] = v
        for b in writes:
            b.w = tok
            b.r = {}

    def op(self, e, fn, reads=(), writes=()):
        deps = self._collect(reads, writes)
        self._wait(e, deps)
        ins = fn(self.eng[e])
        self.cnt[e] += 1
        ins.then_inc(self.sem[e], 1)
        self._commit((e, self.cnt[e]), reads, writes)
        return ins

    def dma(self, e, out, in_, reads=(), writes=(), **kw):
        half = self.NDMA // 2
        if e == "pool":
            key = "d%d" % (self.sw_i % half)
            self.sw_i += 1
        else:
            key = "d%d" % (half + self.dma_i % half)
            self.dma_i += 1
        deps = self._collect(reads, writes)
        if self.cnt[key] > 0:
            self._add(deps, (key, self.cnt[key]))
        self._wait(e, deps)
        ins = self.eng[e].dma_start(out=out, in_=in_, **kw)
        self.cnt[key] += 16
        ins.then_inc(self.sem[key], 16)
        self._commit((key, self.cnt[key]), reads, writes)
        return ins

    def finish(self, e):
        for j in range(self.NDMA):
            k = "d%d" % j
            if self.cnt[k]:
                self._wait(e, {k: self.cnt[k]})


class _Stop(Exception):
    pass


def build(T=4096, NB=2, stop=None):
    nc = bass.Bass("TRN2", target_bir_lowering=False)
    NT = T // TT

    def din(name, shape):
        return nc.dram_tensor(name, list(shape), F32, kind="ExternalInput").ap()

    x_d = din("x", [NB, T, D])
    c_d = din("c", [NB, D])
    ada_w = din("ada_w", [2, D, 6 * D])
    ada_b = din("ada_b", [2, 6 * D])
    mlp_w1 = din("mlp_w1", [2, D, FF])
    mlp_w2 = din("mlp_w2", [2, FF, D])
    a_w_in = din("a_w_in", [D, 2 * D])
    a_ln_g = din("a_ln_g", [D])
    a_ln_b = din("a_ln_b", [D])
    a_w_s = din("a_w_s", [8, CH, CH])
    a_b_s = din("a_b_s", [8 * CH])
    a_w_out = din("a_w_out", [D, D])
    b_mu = din("b_mu", [6, D])
    b_w_in = din("b_w_in", [D, 3 * D])
    b_w0 = din("b_w0", [D])
    b_w1 = din("b_w1", [D, 64])
    b_w2 = din("b_w2", [64, D])
    b_a0 = din("b_a0", [D])
    b_a1 = din("b_a1", [D, 64])
    b_a2 = din("b_a2", [64, D])
    b_g1 = din("b_g1", [D, 160])
    b_g2 = din("b_g2", [160, D])
    b_k_k = din("b_k_k", [D])
    b_k_a = din("b_k_a", [D])
    b_r_k = din("b_r_k", [D])
    b_ln_g = din("b_ln_g", [D])
    b_ln_b = din("b_ln_b", [D])
    b_w_out = din("b_w_out", [D, D])
    final_g = din("final_g", [D])
    out_d = nc.dram_tensor("out", [NB, T, D], F32, kind="ExternalOutput").ap()

    def scratch(name, nsl):
        return nc.dram_tensor(name, [nsl, 128, 2048], BF16, kind="Internal").ap()

    sc_w1 = [scratch("sc_w1_%d" % l, 16) for l in range(2)]
    sc_w2 = [scratch("sc_w2_%d" % l, 16) for l in range(2)]
    sc_awin = scratch("sc_awin", 8)
    sc_awout = scratch("sc_awout", 4)
    sc_bwin = scratch("sc_bwin", 12)
    sc_bwout = scratch("sc_bwout", 4)

    with ExitStack() as st:
      S = Sched(nc, st)
      dbg_d = nc.dram_tensor("dbg", [128, 4096], F32, kind="ExternalOutput").ap() if stop else None

      def chk(name, dump=None):
          if stop == name:
              if dump is not None:
                  ap, bf, ncol = dump
                  S.dma("sp", dbg_d[:, 0:ncol], ap, reads=[bf])
              raise _Stop()
      try:

            def tile(name, shape, dt):
                return st.enter_context(nc.sbuf_tensor(name, list(shape), dt)), Buf(name)

            ps = st.enter_context(nc.psum_tensor("ps", [128, 8, 512], F32))
            bps = [Buf("ps%d" % i) for i in range(8)]
            pstate = {"i": 0}

            held = set()

            def pb1():
                while True:
                    k = pstate["i"] % 8
                    pstate["i"] += 1
                    if k not in held:
                        return k

            def pb2():
                while True:
                    if pstate["i"] % 2:
                        pstate["i"] += 1
                    k = pstate["i"] % 8
                    pstate["i"] += 2
                    if k not in held and (k + 1) not in held:
                        return k

            identf, b_identf = tile("identf", [128, 128], F32)
            identb, b_identb = tile("identb", [128, 128], BF16)
            onesb, b_onesb = tile("onesb", [128, 128], BF16)
            onesf, b_onesf = tile("onesf", [128, 128], F32)
            tri, b_tri = tile("tri", [128, 3, 128], F32)
            negc, b_negc = tile("negc", [128, 2], F32)
            mask4, b_mask4 = tile("mask4", [128, 4, 128], BF16)
            maskL, b_maskL = tile("maskL", [128, 4, 128], BF16)
            lnb_rms, b_lnbr = tile("lnb_rms", [128, 4], F32)
            cst = [b_identf, b_identb, b_onesb, b_onesf, b_tri, b_negc, b_mask4, b_maskL, b_lnbr]

            def sel(out, in_, cm, pat, base):
                S.op("pool", lambda e: e.affine_select(out=out, in_=in_, compare_op=ALU.is_ge, fill=0.0,
                                                        base=base, pattern=[[pat, 128]], channel_multiplier=cm),
                     writes=cst, reads=cst)

            S.op("pool", lambda e: e.memset(identf[:], 0.0), writes=cst)
            S.op("pool", lambda e: e.affine_select(out=identf[:], in_=identf[:], compare_op=ALU.not_equal, fill=1.0,
                                                    base=0, pattern=[[-1, 128]], channel_multiplier=1), writes=cst)
            S.op("pool", lambda e: e.tensor_copy(identb[:], identf[:]), writes=cst)
            S.op("pool", lambda e: e.memset(onesb[:], 1.0), writes=cst)
            S.op("pool", lambda e: e.memset(onesf[:], 1.0), writes=cst)
            S.op("pool", lambda e: e.memset(negc[:], -DECAY_C), writes=cst)
            for i_, v_ in enumerate((D * 1e-6, 1e-5, 1e-24, 64e-5)):
                S.op("pool", lambda e: e.memset(lnb_rms[:, i_:i_ + 1], v_), writes=cst)
            S.op("pool", lambda e: e.memset(tri[:], -DECAY_C), writes=cst)
            sel(tri[:, 0, :], tri[:, 0, :], -1, 1, 0)
            sel(tri[:, 1, :], tri[:, 1, :], -1, 1, -1)
            sel(tri[:, 2, :], tri[:, 2, :], 1, -1, -1)
            S.op("pool", lambda e: e.memset(mask4[:], 1.0), writes=cst)
            S.op("pool", lambda e: e.memset(maskL[:], 1.0), writes=cst)
            for k in range(4):
                sel(mask4[:, k, :], mask4[:, k, :], -1, 1, -1 if k % 2 == 0 else 0)
            for k in range(4):
                sel(maskL[:, k, :], maskL[:, k, :], 1, -1, -1)

            xres, b_x = tile("xres", [128, DC, TT], F32)
            hb, b_h = tile("hb", [128, DC, TT], BF16)
            carry, b_carry = tile("carry", [128, DC, 2], BF16)
            rstd, b_rstd = tile("rstd", [128, TT], F32)
            ntmp = [tile("ntmp%d" % i, [128, TT], F32) for i in range(2)]
            slab = [tile("slab%d" % i, [128, DC, TT], BF16) for i in range(2)]
            sl1f = slab[1][0][:].rearrange("p c t -> p (c t)")
            sl1 = [sl1f[:, k * 1024:(k + 1) * 1024] for k in range(4)]
            b_sl1 = [Buf("sl1_%d" % k) for k in range(4)]
            NTMF = 6
            tmf = [tile("tmf%d" % i, [128, D], F32) for i in range(NTMF)]
            NTMB = 5
            tmb = [tile("tmb%d" % i, [128, D], BF16) for i in range(NTMB)]
            NWS = 4
            wslot = [tile("wslot%d" % i, [128, 2048], BF16) for i in range(NWS)]
            wstate = {"i": 0}
            rstate = {"i": 0}

            scrb = {}

            def sbuf_of(scr, s):
                key = (scr.name if hasattr(scr, "name") else id(scr), s)
                if key not in scrb:
                    scrb[key] = Buf("scr")
                return scrb[key]

            def wload(src):
                scr, s = src
                k = wstate["i"] % NWS
                wstate["i"] += 1
                t, b = wslot[k]
                S.dma("sp", t[:], scr[s], writes=[b], reads=[sbuf_of(scr, s)])
                pump(2)
                return t, b

            st16 = [tile("st16_%d" % i, [128, 16], F32) for i in range(6)]
            modf, b_mod = tile("modf", [128, 2, 6, DC, NB], F32)
            scf, b_scf = tile("scf", [128, 2, 2, DC, NB], F32)
            adab, b_adab = tile("adab", [128, 2, 6, DC], F32)
            condT, b_cond = tile("condT", [128, DC, NB], BF16)
            cfm, b_cfm = tile("cfm", [128, DC, NB], F32)
            muf, b_muf = tile("muf", [128, 6, DC], F32)
            fgf, b_fgf = tile("fgf", [128, DC], F32)
            alng, b_alng = tile("alng", [128, DC], F32)
            alnb, b_alnb = tile("alnb", [128, DC], F32)
            wsT, b_wsT = tile("wsT", [128, 8, CH], BF16)
            csgu, b_csgu = tile("csgu", [128, 8, CH], F32)
            rows, b_rows = tile("rows", [128, 5, D], BF16)
            w0a0, b_w0a0 = tile("w0a0", [1, 2, D], F32)
            lw1, b_lw1 = tile("lw1", [128, DC, 64], BF16)
            la1, b_la1 = tile("la1", [128, DC, 64], BF16)
            lg1, b_lg1 = tile("lg1", [128, DC, 160], BF16)
            lw2, b_lw2 = tile("lw2", [64, D], BF16)
            la2, b_la2 = tile("la2", [64, D], BF16)
            lg2a, b_lg2a = tile("lg2a", [128, D], BF16)
            lg2b, b_lg2b = tile("lg2b", [32, D], BF16)
            xi = [tile("xi%d" % i, [128, DC, CH], BF16) for i in range(4)]
            xstate = {"i": 0}
            xxt, b_xx = tile("xxt", [128, DC, CH], BF16)
            lor, b_lor = tile("lor", [128, 4, CH], BF16)
            b_lorw, b_lora, b_lorg = Buf("lorw"), Buf("lora"), Buf("lorg")
            arfm, b_arfm = tile("arfm", [128, 8, 2, CH], BF16)
            kbfm, b_kbfm = tile("kbfm", [128, 8, 2, CH], BF16)
            MKB, b_MKB = tile("MKB", [128, NH, 4, CH], BF16)
            Q0, b_Q0 = tile("Q0", [128, NH, CH], BF16)
            Qt_, _ = tile("Qt", [128, NH, CH], BF16)
            PX = [tile("PX%d" % i, [128, NH, 2, CH], BF16)[0] for i in range(2)]
            bPX = [[Buf("PX%d_%d" % (i, g)) for g in range(4)] for i in range(2)]
            bQb = [[Buf("Q%d_%d" % (i, g)) for g in range(4)] for i in range(2)]
            Hf = [tile("Hf%d" % i, [128, 8, 64], F32) for i in range(NB)]
            Hbd = [tile("Hbd%d" % i, [128, 8, 128], BF16) for i in range(NB)]
            elc2 = [tile("elc%d" % i, [128, 8], F32) for i in range(2)]
            sbon2 = [tile("sbon%d" % i, [128, 16], F32) for i in range(2)]

            if stop:
                print('SBUF remaining', nc.sbuf_bytes_remaining)
            def fmvec(dst, bdst, src_ap):
                with nc.allow_non_contiguous_dma(reason="small per-feature vector to FM"):
                    S.dma("sp", dst, src_ap, writes=[bdst])

            for b in range(NB):
                fmvec(cfm[:, :, b], b_cfm, c_d[b].rearrange("(c p) -> p c", p=128))
            for m in range(6):
                fmvec(muf[:, m, :], b_muf, b_mu[m].rearrange("(c p) -> p c", p=128))
                for l in range(2):
                    fmvec(adab[:, l, m, :], b_adab, ada_b[l, m * D:(m + 1) * D].rearrange("(c p) -> p c", p=128))
            fmvec(fgf[:], b_fgf, final_g.rearrange("(c p) -> p c", p=128))
            fmvec(alng[:], b_alng, a_ln_g.rearrange("(c p) -> p c", p=128))
            fmvec(alnb[:], b_alnb, a_ln_b.rearrange("(c p) -> p c", p=128))
            S.op("dve", lambda e: e.tensor_scalar(fgf[:], fgf[:], 32.0, None, ALU.mult), reads=[b_fgf], writes=[b_fgf])
            S.dma("sp", w0a0[:, 0, :], b_w0.rearrange("(o n) -> o n", o=1), writes=[b_w0a0])
            S.dma("sp", w0a0[:, 1, :], b_a0.rearrange("(o n) -> o n", o=1), writes=[b_w0a0])
            for i, src in enumerate([b_k_k, b_k_a, b_r_k, b_ln_g, b_ln_b]):
                t, b = tmf[i % 2]
                S.dma("sp", t[:], src.partition_broadcast(128), writes=[b])
                S.op("act", lambda e: e.copy(rows[:, i, :], t[:]), reads=[b], writes=[b_rows])
            S.dma("pool", lw1[:], b_w1.rearrange("(c p) n -> p c n", p=128), writes=[b_lw1])
            S.dma("pool", la1[:], b_a1.rearrange("(c p) n -> p c n", p=128), writes=[b_la1])
            S.dma("pool", lg1[:], b_g1.rearrange("(c p) n -> p c n", p=128), writes=[b_lg1])
            S.dma("pool", lw2[:], b_w2[:, :], writes=[b_lw2])
            S.dma("pool", la2[:], b_a2[:, :], writes=[b_la2])
            S.dma("pool", lg2a[:], b_g2[0:128, :], writes=[b_lg2a])
            S.dma("pool", lg2b[:], b_g2[128:160, :], writes=[b_lg2b])

            chk('const', (rows[:, 0:2, :].rearrange('p a n -> p (a n)'), b_rows, 2048)) if False else chk('const')
            def prep_kmajor(dst, src, ncols):
                v = src.rearrange("(c p) (s n) -> s p c n", p=128, n=256)
                for s in range(ncols // 256):
                    k = wstate["i"] % NWS
                    wstate["i"] += 1
                    wt, wb_ = wslot[k]
                    S.dma("pool", wt[:].rearrange("p (c n) -> p c n", n=256), v[s], writes=[wb_])
                    S.dma("sp", dst[s], wt[:], reads=[wb_], writes=[sbuf_of(dst, s)])
                    yield

            def prep_w2(dst, src):
                v = src.rearrange("(pa s j p) (eh n) -> pa eh s p j n", pa=2, s=4, j=4, p=128, n=512)
                for pa in range(2):
                    for eh in range(2):
                        for s4 in range(4):
                            k = wstate["i"] % NWS
                            wstate["i"] += 1
                            wt, wb_ = wslot[k]
                            S.dma("pool", wt[:].rearrange("p (j n) -> p j n", n=512), v[pa, eh, s4], writes=[wb_])
                            S.dma("sp", dst[(pa * 2 + eh) * 4 + s4], wt[:], reads=[wb_],
                                  writes=[sbuf_of(dst, (pa * 2 + eh) * 4 + s4)])
                            yield

            for _ in prep_kmajor(sc_awin, a_w_in, 2048):
                pass
            for _ in prep_kmajor(sc_awout, a_w_out, 1024):
                pass

            def _pending():
                yield from prep_kmajor(sc_w1[0], mlp_w1[0], FF)
                yield from prep_w2(sc_w2[0], mlp_w2[0])
                yield from prep_kmajor(sc_bwin, b_w_in, 3072)
                yield from prep_kmajor(sc_bwout, b_w_out, 1024)
                yield from prep_kmajor(sc_w1[1], mlp_w1[1], FF)
                yield from prep_w2(sc_w2[1], mlp_w2[1])
            pend = {"g": _pending(), "alive": True}

            def pump(n):
                for _ in range(n):
                    if not pend["alive"]:
                        return
                    try:
                        next(pend["g"])
                    except StopIteration:
                        pend["alive"] = False

            chk('prologue')
            S.op("act", lambda e: e.activation(condT[:], cfm[:], AF.Silu), reads=[b_cfm], writes=[b_cond])
            for l in range(2):
                av = ada_w[l].rearrange("(c p) (s n) -> s p c n", p=128, n=256)
                for s in range(24):
                    k = wstate["i"] % NWS
                    wstate["i"] += 1
                    wt, wb_ = wslot[k]
                    S.dma("pool", wt[:].rearrange("p (c n) -> p c n", n=256), av[s], writes=[wb_])
                    bk = pb1()
                    for j in range(2):
                        for c in range(DC):
                            S.op("pe", lambda e: e.matmul(ps[:, bk, j * NB:(j + 1) * NB],
                                                          wt[:, c * 256 + j * 128: c * 256 + (j + 1) * 128],
                                                          condT[:, c, :], start=(c == 0), stop=(c == DC - 1)),
                                 reads=[wb_, b_cond], writes=[bps[bk]])
                    ec = s * 2
                    m, cc = ec // 8, ec % 8
                    S.op("dve", lambda e: e.tensor_tensor(
                        modf[:, l, m, cc:cc + 2, :], ps[:, bk, 0:2 * NB].rearrange("p (j b) -> p j b", b=NB),
                        adab[:, l, m, cc:cc + 2].unsqueeze(2).to_broadcast([128, 2, NB]), ALU.add),
                        reads=[bps[bk], b_adab], writes=[b_mod])
            for l in range(2):
                for k2 in range(2):
                    S.op("dve", lambda e: e.tensor_scalar(scf[:, l, k2], modf[:, l, 1 + 3 * k2], 1.0, 32.0, ALU.add, ALU.mult),
                         reads=[b_mod], writes=[b_scf])

            chk('ada', (modf[:].rearrange('p l m c b -> p (l m c b)'), b_mod, 2 * 6 * DC * NB))
            lnbs_t, b_lnbs = tmf[2]
            S.dma("sp", lnbs_t[:], a_b_s.partition_broadcast(128), writes=[b_lnbs])
            wtmp, b_wtmp = tmf[3]
            S.dma("sp", wtmp[:].rearrange("p (g s) -> p g s", s=CH), a_w_s.rearrange("g t s -> t g s"), writes=[b_wtmp])
            wm32, b_wm32 = tmf[4]
            for g in range(8):
                bk = pb1()
                S.op("pe", lambda e: e.transpose(ps[:, bk, 0:CH], wtmp[:, g * CH:(g + 1) * CH], identf[:]),
                     reads=[b_wtmp] + cst, writes=[bps[bk]])
                S.op("dve", lambda e: e.tensor_tensor(wm32[:, g * CH:(g + 1) * CH], ps[:, bk, 0:CH], mask4[:, 1, :], ALU.mult),
                     reads=[bps[bk]] + cst, writes=[b_wm32])
                S.op("act", lambda e: e.copy(wsT[:, g, :], wm32[:, g * CH:(g + 1) * CH]), reads=[b_wm32], writes=[b_wsT])
                bk2 = pb1()
                S.op("pe", lambda e: e.matmul(ps[:, bk2, 0:CH], onesf[:], wm32[:, g * CH:(g + 1) * CH], start=True, stop=True),
                     reads=[b_wm32] + cst, writes=[bps[bk2]])
                S.op("dve", lambda e: e.scalar_tensor_tensor(csgu[:, g, :], ps[:, bk2, 0:CH], alnb[:, g:g + 1],
                                                             lnbs_t[:, g * CH:(g + 1) * CH], ALU.mult, ALU.add),
                     reads=[bps[bk2], b_alnb, b_lnbs], writes=[b_csgu])

            chk('sguconst', (csgu[:].rearrange('p g t -> p (g t)'), b_csgu, 1024))
            def rms_mod(l, k2, b):
                sq, b_sq = slab[1]
                S.op("act", lambda e: e.activation(sq[:], xres[:], AF.Square), reads=[b_x], writes=[b_sq] + b_sl1)
                bk = pb1()
                for c in range(DC):
                    S.op("pe", lambda e: e.matmul(ps[:, bk, :], onesb[:], sq[:, c, :], start=(c == 0), stop=(c == DC - 1)),
                         reads=[b_sq] + b_sl1 + cst, writes=[bps[bk]])
                S.op("act", lambda e: e.activation(rstd[:], ps[:, bk, :], AF.Ln, bias=lnb_rms[:, 0:1], scale=1.0),
                     reads=[bps[bk]] + cst, writes=[b_rstd])
                S.op("act", lambda e: e.activation(rstd[:], rstd[:], AF.Exp, scale=-0.5), reads=[b_rstd], writes=[b_rstd])
                if l is None:
                    return
                for c in range(DC):
                    nt, b_nt = ntmp[c % 2]
                    S.op("dve", lambda e: e.scalar_tensor_tensor(nt[:], xres[:, c, :], scf[:, l, k2, c, b:b + 1], rstd[:],
                                                                 ALU.mult, ALU.mult),
                         reads=[b_x, b_scf, b_rstd], writes=[b_nt])
                    S.op("act", lambda e: e.activation(hb[:, c, :], nt[:], AF.Identity,
                                                       bias=modf[:, l, 3 * k2, c, b:b + 1], scale=1.0),
                         reads=[b_nt, b_mod], writes=[b_h])

            def resid_add(e_chunk, bk, l, k2, b, cols=slice(0, TT)):
                gate = modf[:, l, 2 + 3 * k2, e_chunk, b:b + 1]
                S.op("dve", lambda e: e.scalar_tensor_tensor(xres[:, e_chunk, cols], ps[:, bk, cols], gate,
                                                             xres[:, e_chunk, cols], ALU.mult, ALU.add),
                     reads=[bps[bk], b_mod, b_x], writes=[b_x])

            def proj_fm_out(scr, nsl, rhs_tile, b_rhs, sink):
                for s in range(nsl):
                    wt, wb_ = wload((scr, s))
                    for j in range(2):
                        bk = pb1()
                        for c in range(DC):
                            S.op("pe", lambda e: e.matmul(ps[:, bk, :], wt[:, c * 256 + j * 128: c * 256 + (j + 1) * 128],
                                                          rhs_tile[:, c, :], start=(c == 0), stop=(c == DC - 1)),
                                 reads=[wb_, b_rhs], writes=[bps[bk]])
                        sink(s * 2 + j, bk)

            def mlp(l, b):
                rms_mod(l, 1, b)
                hid, b_hid = slab[0]
                hid2, b_hid2 = slab[1]

                def hv(f):
                    return (hid if f < 8 else hid2)[:, f % 8, :], (b_hid if f < 8 else b_hid2)
                for pa in range(2):
                    for s in range(8):
                        wt, wb_ = wload((sc_w1[l], pa * 8 + s))
                        for j in range(2):
                            f = s * 2 + j
                            bk = pb1()
                            for c in range(DC):
                                S.op("pe", lambda e: e.matmul(ps[:, bk, :], wt[:, c * 256 + j * 128: c * 256 + (j + 1) * 128],
                                                              hb[:, c, :], start=(c == 0), stop=(c == DC - 1)),
                                     reads=[wb_, b_h], writes=[bps[bk]])
                            hap, hbuf = hv(f)
                            S.op("act", lambda e: e.activation(hap, ps[:, bk, :], AF.Relu), reads=[bps[bk]], writes=[hbuf])
                            S.op("pool", lambda e: e.tensor_tensor(hap, hap, hap, ALU.mult), reads=[hbuf], writes=[hbuf])
                    for eh in range(2):
                        bks = [pb1() for _ in range(4)]
                        for s4 in range(4):
                            wt, wb_ = wload((sc_w2[l], (pa * 2 + eh) * 4 + s4))
                            for j in range(4):
                                f = s4 * 4 + j
                                hap, hbuf = hv(f)
                                for e4 in range(4):
                                    S.op("pe", lambda e: e.matmul(ps[:, bks[e4], :],
                                                                  wt[:, j * 512 + e4 * 128: j * 512 + (e4 + 1) * 128],
                                                                  hap, start=(f == 0), stop=(f == 15)),
                                         reads=[wb_, hbuf], writes=[bps[bks[e4]]])
                        for e4 in range(4):
                            resid_add(eh * 4 + e4, bks[e4], l, 1, b)

            def sgu(b):
                l = 0
                rms_mod(l, 0, b)
                u, b_u = slab[0]
                def sink_u(ec, bk):
                    S.op("act", lambda e: e.activation(u[:, ec, :], ps[:, bk, :], AF.Gelu), reads=[bps[bk]], writes=[b_u])
                proj_fm_out(sc_awin, 4, hb, b_h, sink_u)
                s1, b_s1 = st16[0]
                s2, b_s2 = st16[1]
                S.op("pool", lambda e: e.memset(s1[:], 0.0), writes=[b_s1])
                S.op("pool", lambda e: e.memset(s2[:], 0.0), writes=[b_s2])
                for s in range(4):
                    wt, wb_ = wload((sc_awin, 4 + s))
                    for q in range(4):
                        bk = pb1()
                        for c in range(DC):
                            S.op("pe", lambda e: e.matmul(ps[:, bk, 0:256], hb[:, c, q * CH:(q + 1) * CH],
                                                          wt[:, c * 256:(c + 1) * 256], start=(c == 0), stop=(c == DC - 1)),
                                 reads=[wb_, b_h], writes=[bps[bk]])
                        vt, b_vt = tmf[q]
                        S.op("act", lambda e: e.activation(vt[:, s * 256:(s + 1) * 256], ps[:, bk, 0:256], AF.Gelu,
                                                           accum_out=s1[:, q * 4 + s: q * 4 + s + 1]),
                             reads=[bps[bk]], writes=[b_vt, b_s1])
                        jt, b_jt = ntmp[0]
                        S.op("act", lambda e: e.activation(jt[:, 0:256], vt[:, s * 256:(s + 1) * 256], AF.Square,
                                                           accum_out=s2[:, q * 4 + s: q * 4 + s + 1]),
                             reads=[b_vt], writes=[b_jt, b_s2])
                mean, b_mean = st16[2]
                ex2, b_ex2 = st16[3]
                rs, b_rs = st16[4]
                nb_, b_nb = st16[5]
                S.op("dve", lambda e: e.reduce_sum(mean[:, 0:4], s1[:].rearrange("p (q s) -> p q s", s=4), axis=AX.X),
                     reads=[b_s1], writes=[b_mean])
                S.op("dve", lambda e: e.reduce_sum(ex2[:, 0:4], s2[:].rearrange("p (q s) -> p q s", s=4), axis=AX.X),
                     reads=[b_s2], writes=[b_ex2])
                S.op("dve", lambda e: e.tensor_scalar(mean[:, 0:4], mean[:, 0:4], 1.0 / D, None, ALU.mult), reads=[b_mean], writes=[b_mean])
                S.op("dve", lambda e: e.tensor_tensor(rs[:, 0:4], mean[:, 0:4], mean[:, 0:4], ALU.mult), reads=[b_mean], writes=[b_rs])
                S.op("dve", lambda e: e.scalar_tensor_tensor(rs[:, 0:4], ex2[:, 0:4], 1.0 / D, rs[:, 0:4], ALU.mult, ALU.subtract),
                     reads=[b_ex2, b_rs], writes=[b_rs])
                S.op("act", lambda e: e.activation(rs[:, 0:4], rs[:, 0:4], AF.Ln, bias=lnb_rms[:, 1:2], scale=1.0), reads=[b_rs] + cst, writes=[b_rs])
                S.op("act", lambda e: e.activation(rs[:, 0:4], rs[:, 0:4], AF.Exp, scale=-0.5), reads=[b_rs], writes=[b_rs])
                S.op("dve", lambda e: e.scalar_tensor_tensor(nb_[:, 0:4], mean[:, 0:4], -1.0, rs[:, 0:4], ALU.mult, ALU.mult),
                     reads=[b_mean, b_rs], writes=[b_nb])
                for q in range(4):
                    vt, b_vt = tmf[q]
                    vn, b_vn = tmb[q % 2]
                    S.op("act", lambda e: e.activation(vn[:], vt[:], AF.Identity, bias=nb_[:, q:q + 1], scale=rs[:, q:q + 1]),
                         reads=[b_vt, b_rs, b_nb], writes=[b_vn])
                    for half in range(2):
                        bk = pb1()
                        for g4 in range(4):
                            g = half * 4 + g4
                            S.op("pe", lambda e: e.matmul(ps[:, bk, g4 * CH:(g4 + 1) * CH], vn[:, g * CH:(g + 1) * CH],
                                                          wsT[:, g, :], start=True, stop=True),
                                 reads=[b_vn, b_wsT], writes=[bps[bk]])
                        tt_, b_tt = ntmp[half]
                        for g4 in range(4):
                            g = half * 4 + g4
                            S.op("dve", lambda e: e.scalar_tensor_tensor(tt_[:, g4 * CH:(g4 + 1) * CH], ps[:, bk, g4 * CH:(g4 + 1) * CH],
                                                                         alng[:, g:g + 1], csgu[:, g, :], ALU.mult, ALU.add),
                                 reads=[bps[bk], b_alng, b_csgu], writes=[b_tt])
                        uv = u[:, half * 4:(half + 1) * 4, q * CH:(q + 1) * CH]
                        S.op("dve", lambda e: e.tensor_tensor(uv, tt_[:].rearrange("p (g t) -> p g t", t=CH), uv, ALU.mult),
                             reads=[b_tt, b_u], writes=[b_u])
                proj_fm_out(sc_awout, 4, u, b_u, lambda ec, bk: resid_add(ec, bk, l, 0, b))

            def rwkv(b, first_tile):
                l = 1
                rms_mod(l, 0, b)
                yg, b_yg = slab[0]
                hf_t, b_hf = Hf[b]
                hbd_t, b_hbd = Hbd[b]
                A, b_A = tmf[0]
                Fa, b_F = tmf[1]
                G, b_G = tmf[2]
                E1, b_E1 = tmf[3]
                E2, b_E2 = tmf[4]
                SW, b_SW = tmf[5]
                Kp, b_Kp = E2, b_E2
                khat, b_khat = tmb[0]
                bhat, b_bhat = tmb[1]
                vbf, b_vbf = tmb[2]
                tA, b_tA = tmb[3]
                tB, b_tB = tmb[4]
                ubf, b_ubf = tB, b_tB
                rhsbf, b_rhsbf = tA, b_tA
                ss, b_ss = st16[0]
                rn, b_rn = st16[1]
                g1s, b_g1s = st16[3]
                g2s, b_g2s = st16[4]
                g3s, b_g3s = st16[5]

                px0 = PX[0][:].rearrange("p h k t -> p (h k t)")
                px1 = PX[1][:].rearrange("p h k t -> p (h k t)")
                ring = [(wslot[i][0][:], [wslot[i][1]]) for i in range(NWS)] + [
                    (px0[:, 0:2048], [bPX[0][0], bPX[0][1]]), (px0[:, 2048:4096], [bPX[0][2], bPX[0][3]]),
                    (Q0[:].rearrange("p h t -> p (h t)"), [b_Q0] + bQb[0]),
                    (Qt_[:].rearrange("p h t -> p (h t)"), list(bQb[1]))]

                def h3(ap):
                    return ap.rearrange("p (h j) -> p h j", j=64)

                def bc16(ap16):
                    return ap16.unsqueeze(2).to_broadcast([128, NH, 64])

                def mix(i, q):
                    slot = {1: 0, 4: 1, 5: 2, 2: 3, 0: 4, 3: 5}[i]
                    if slot < 4:
                        t, bt = xi[slot]
                        tv = t[:]
                    else:
                        tv, bt = sl1[slot - 4].rearrange("p (c t) -> p c t", t=CH), b_sl1[slot - 4]
                    S.op("pool", lambda e: e.tensor_tensor(tv, xxt[:], muf[:, i, :].unsqueeze(2).to_broadcast([128, DC, CH]), ALU.mult),
                         reads=[b_xx, b_muf], writes=[bt])
                    S.op("dve", lambda e: e.tensor_tensor(tv, tv, hb[:, :, q * CH:(q + 1) * CH], ALU.add),
                         reads=[bt, b_h], writes=[bt])
                    return tv, bt

                def proj_tm(xt_, bxt_, sl0, bk2, extra=None):
                    for s in range(4):
                        if extra == "noring":
                            wt_, wb1 = wload((sc_bwin, sl0 + s))
                            wt, wbl = wt_[:], [wb1]
                        else:
                            wt, wbl = ring[rstate["i"] % len(ring)]
                            rstate["i"] += 1
                            S.dma("sp", wt, sc_bwin[sl0 + s], writes=wbl, reads=[sbuf_of(sc_bwin, sl0 + s)])
                        k = bk2 + s // 2
                        cols = slice((s % 2) * 256, (s % 2) * 256 + 256)
                        for c in range(DC):
                            S.op("pe", lambda e: e.matmul(ps[:, k, cols], xt_[:, c, :], wt[:, c * 256:(c + 1) * 256],
                                                          start=(c == 0), stop=(c == DC - 1)),
                                 reads=wbl + [bxt_], writes=[bps[k]])

                def transpose8(src, bsrc, dst_fn, bdst):
                    bk = pb1()
                    psb = ps[:, bk, :].bitcast(BF16)
                    for c in range(8):
                        S.op("pe", lambda e: e.transpose(psb[:, c * CH:(c + 1) * CH], src[:, c * CH:(c + 1) * CH], identb[:]),
                             reads=[bsrc] + cst, writes=[bps[bk]])
                    S.op("act", lambda e: e.copy(dst_fn, psb.rearrange("p (c t) -> p c t", t=CH)), reads=[bps[bk]], writes=[bdst])

                rbf, b_rbf = sl1[2], b_sl1[2]
                gbf, b_gbf = sl1[3], b_sl1[3]

                def genA(q):
                    q0 = q * CH
                    elc, b_elc = elc2[q % 2]
                    sbon, b_sbon = sbon2[q % 2]
                    if q == 0:
                        if first_tile:
                            S.op("pool", lambda e: e.memset(carry[:], 0.0), writes=[b_carry])
                        S.op("dve", lambda e: e.tensor_tensor(xxt[:, :, 0:1], carry[:, :, 0:1], hb[:, :, 0:1], ALU.subtract),
                             reads=[b_carry, b_h], writes=[b_xx])
                        S.op("dve", lambda e: e.tensor_tensor(xxt[:, :, 1:CH], hb[:, :, 0:CH - 1], hb[:, :, 1:CH], ALU.subtract),
                             reads=[b_h], writes=[b_xx])
                    else:
                        S.op("dve", lambda e: e.tensor_tensor(xxt[:], hb[:, :, q0 - 1:q0 + CH - 1], hb[:, :, q0:q0 + CH], ALU.subtract),
                             reads=[b_h], writes=[b_xx])
                    if q == 3:
                        S.op("pool", lambda e: e.tensor_copy(carry[:, :, 0:1], hb[:, :, TT - 1:TT]), reads=[b_h, b_xx], writes=[b_carry])
                    xk, bxk = mix(2, q)
                    xa, bxa = mix(4, q)
                    b_lw, b_la, b_lg = b_lorw, b_lora, b_lorg
                    bkk = pb2()
                    proj_tm(xk, bxk, 4, bkk)
                    kps = ps[:, bkk:bkk + 2, :].rearrange("p a n -> p (a n)")
                    bkps = [bps[bkk], bps[bkk + 1]]
                    held.update((bkk, bkk + 1))
                    S.op("dve", lambda e: e.tensor_tensor(G[:], kps, rows[:, 0, :], ALU.mult), reads=bkps + [b_rows], writes=[b_G])
                    S.op("act", lambda e: e.activation(SW[:], G[:], AF.Square), reads=[b_G], writes=[b_SW])
                    S.op("dve", lambda e: e.reduce_sum(ss[:], h3(SW[:]), axis=AX.X), reads=[b_SW], writes=[b_ss])
                    S.op("act", lambda e: e.activation(rn[:], ss[:], AF.Ln, bias=lnb_rms[:, 2:3], scale=1.0), reads=[b_ss] + cst, writes=[b_rn])
                    S.op("act", lambda e: e.activation(rn[:], rn[:], AF.Exp, scale=-0.5), reads=[b_rn], writes=[b_rn])
                    S.op("dve", lambda e: e.tensor_tensor(h3(G[:]), h3(G[:]), bc16(rn[:]), ALU.mult), reads=[b_G, b_rn], writes=[b_G])
                    yield
                    xw, bxw = mix(1, q)
                    xr, bxr = mix(0, q)

                    def zproj(which, dst, bdst, l2, blor):
                        bk2 = pb2()
                        for hlf in range(2):
                            cs = slice(hlf * 512, (hlf + 1) * 512)
                            S.op("pe", lambda e: e.matmul(ps[:, bk2 + hlf, :], onesf[0:1, :], w0a0[0:1, which, cs], start=True, stop=False),
                                 reads=[b_w0a0] + cst, writes=[bps[bk2 + hlf]])
                            S.op("pe", lambda e: e.matmul(ps[:, bk2 + hlf, :], lor[0:64, which, :], l2[:, cs], start=False, stop=True),
                                 reads=[blor, b_lw2, b_la2], writes=[bps[bk2 + hlf]])
                        S.op("act", lambda e: e.activation(dst[:], ps[:, bk2:bk2 + 2, :].rearrange("p a n -> p (a n)"), AF.Sigmoid),
                             reads=[bps[bk2], bps[bk2 + 1]], writes=[bdst])
                    bkl = pb1()
                    for c in range(DC):
                        S.op("pe", lambda e: e.matmul(ps[0:64, bkl, 0:CH], la1[:, c, :], xa[:, c, :], start=(c == 0), stop=(c == DC - 1)),
                             reads=[b_la1, bxa], writes=[bps[bkl]])
                    S.op("act", lambda e: e.copy(lor[0:64, 1, :], ps[0:64, bkl, 0:CH]), reads=[bps[bkl]], writes=[b_la])
                    zproj(1, Fa, b_F, la2, b_la)
                    S.op("dve", lambda e: e.scalar_tensor_tensor(A[:], Fa[:], -1.0, rows[:, 1, :], ALU.add, ALU.mult), reads=[b_F, b_rows], writes=[b_A])
                    S.op("dve", lambda e: e.scalar_tensor_tensor(A[:], A[:], 1.0, kps, ALU.add, ALU.mult), reads=bkps + [b_A], writes=[b_A])
                    held.difference_update((bkk, bkk + 1))
                    S.op("dve", lambda e: e.tensor_tensor(Fa[:], G[:], Fa[:], ALU.mult), reads=[b_G, b_F], writes=[b_F])
                    bkl = pb1()
                    for c in range(DC):
                        S.op("pe", lambda e: e.matmul(ps[0:64, bkl, 0:CH], lw1[:, c, :], xw[:, c, :], start=(c == 0), stop=(c == DC - 1)),
                             reads=[b_lw1, bxw], writes=[bps[bkl]])
                    S.op("act", lambda e: e.activation(lor[0:64, 0, :], ps[0:64, bkl, 0:CH], AF.Tanh), reads=[bps[bkl]], writes=[b_lw])
                    zproj(0, SW, b_SW, lw2, b_lw)
                    chk('r2')
                    yield
                    def lmat(kind):
                        bk2 = pb2()
                        for hlf in range(2):
                            S.op("pe", lambda e: e.matmul(ps[:, bk2 + hlf, :], tri[:, kind, :], SW[:, hlf * 512:(hlf + 1) * 512], start=True, stop=True),
                                 reads=[b_SW] + cst, writes=[bps[bk2 + hlf]])
                        return bk2, ps[:, bk2:bk2 + 2, :].rearrange("p a n -> p (a n)")
                    bke = pb1()
                    for p_ in range(8):
                        S.op("pe", lambda e: e.matmul(ps[:, bke, 2 * p_:2 * p_ + 2], SW[:, p_ * CH:(p_ + 1) * CH], negc[:, 0:2], start=True, stop=True),
                             reads=[b_SW] + cst, writes=[bps[bke]])
                    S.op("act", lambda e: e.activation(elc[:], ps[:, bke, 0:16].rearrange("p (a two) -> p a two", two=2)[:, :, 0], AF.Exp),
                         reads=[bps[bke]], writes=[b_elc])
                    chk('r3')
                    yield
                    bkr = pb2()
                    proj_tm(xr, bxr, 0, bkr)
                    S.op("act", lambda e: e.copy(rbf, ps[:, bkr:bkr + 2, :].rearrange("p a n -> p (a n)")),
                         reads=[bps[bkr], bps[bkr + 1]], writes=[b_rbf])
                    chk('r4')
                    yield "barrier"
                    bkL, Lps = lmat(0)
                    S.op("act", lambda e: e.activation(E2[:], Lps, AF.Exp, scale=-1.0), reads=[bps[bkL], bps[bkL + 1]], writes=[b_E2])
                    S.op("act", lambda e: e.activation(E1[:], Lps, AF.Exp), reads=[bps[bkL], bps[bkL + 1]], writes=[b_E1])
                    S.op("dve", lambda e: e.tensor_tensor(khat[:], A[:], E2[:], ALU.mult), reads=[b_A, b_E2], writes=[b_khat])
                    transpose8(khat, b_khat, kbfm[:, :, 0, :], b_kbfm)
                    S.op("dve", lambda e: e.tensor_tensor(bhat[:], Fa[:], E2[:], ALU.mult), reads=[b_F, b_E2], writes=[b_bhat])
                    transpose8(bhat, b_bhat, kbfm[:, :, 1, :], b_kbfm)
                    chk('r5')
                    yield
                    S.op("dve", lambda e: e.tensor_tensor(tA[:], rbf, E1[:], ALU.mult), reads=[b_rbf, b_E1], writes=[b_tA])
                    transpose8(tA, b_tA, arfm[:, :, 1, :], b_arfm)
                    S.op("dve", lambda e: e.tensor_tensor(Kp[:], rbf, A[:], ALU.mult), reads=[b_rbf, b_A], writes=[b_Kp])
                    S.op("pool", lambda e: e.tensor_tensor(Kp[:], Kp[:], rows[:, 2, :], ALU.mult), reads=[b_Kp, b_rows], writes=[b_Kp])
                    S.op("dve", lambda e: e.reduce_sum(sbon[:], h3(Kp[:]), axis=AX.X), reads=[b_Kp], writes=[b_sbon])
                    bkL, Lps = lmat(1)
                    S.op("act", lambda e: e.activation(E2[:], Lps, AF.Exp), reads=[bps[bkL], bps[bkL + 1]], writes=[b_E2])
                    S.op("dve", lambda e: e.scalar_tensor_tensor(tB[:], G[:], -1.0, E2[:], ALU.mult, ALU.mult), reads=[b_G, b_E2], writes=[b_tB])
                    transpose8(tB, b_tB, arfm[:, :, 0, :], b_arfm)
                    chk('r6')
                    yield
                    for p_ in range(8):
                        bkA = [pb1(), pb1()]
                        for hh in range(2):
                            pr = slice(hh * 64, hh * 64 + 64)
                            rhs_ar = arfm[pr, p_, :, :].rearrange("p a t -> p (a t)")
                            S.op("pe", lambda e: e.matmul(ps[:, bkA[hh], 0:256], kbfm[pr, p_, 0, :], rhs_ar, start=True, stop=True),
                                 reads=[b_kbfm, b_arfm], writes=[bps[bkA[hh]]])
                        for hh in range(2):
                            pr = slice(hh * 64, hh * 64 + 64)
                            rhs_ar = arfm[pr, p_, :, :].rearrange("p a t -> p (a t)")
                            S.op("pe", lambda e: e.matmul(ps[:, bkA[hh], 256:512], kbfm[pr, p_, 1, :], rhs_ar, start=True, stop=True),
                                 reads=[b_kbfm, b_arfm], writes=[bps[bkA[hh]]])
                        for hh in range(2):
                            h = 2 * p_ + hh
                            S.op("dve", lambda e: e.tensor_tensor(MKB[:, h], ps[:, bkA[hh], :].rearrange("p (a t) -> p a t", t=CH),
                                                                  mask4[:], ALU.mult),
                                 reads=[bps[bkA[hh]]] + cst, writes=[b_MKB])
                        yield
                    Q0v = Q0[:].rearrange("p (c two) t -> p c two t", two=2)
                    for pg in range(2):
                        bkC = [pb1(), pb1()]
                        for i4 in range(4):
                            p_ = pg * 4 + i4
                            for hh in range(2):
                                pr = slice(hh * 64, hh * 64 + 64)
                                S.op("pe", lambda e: e.matmul(ps[:, bkC[hh], i4 * CH:(i4 + 1) * CH], arfm[pr, p_, 0, :], kbfm[pr, p_, 1, :],
                                                              start=True, stop=True),
                                     reads=[b_kbfm, b_arfm], writes=[bps[bkC[hh]]])
                        for hh in range(2):
                            S.op("dve", lambda e: e.tensor_tensor(Q0v[:, pg * 4:(pg + 1) * 4, hh, :],
                                                                  ps[:, bkC[hh], :].rearrange("p (a t) -> p a t", t=CH), maskL[:], ALU.mult),
                                 reads=[bps[bkC[hh]]] + cst, writes=[b_Q0] + bQb[0])
                        yield
                    xg, bxg = mix(5, q)
                    xv_, bxv = mix(3, q)
                    bkv = pb2()
                    held.update((bkv, bkv + 1))

                    def v_slice(s_):
                        wt_, wb1 = wload((sc_bwin, 8 + s_))
                        k = bkv + s_ // 2
                        cols = slice((s_ % 2) * 256, (s_ % 2) * 256 + 256)
                        for c in range(DC):
                            S.op("pe", lambda e: e.matmul(ps[:, k, cols], xv_[:, c, :], wt_[:, c * 256:(c + 1) * 256],
                                                          start=(c == 0), stop=(c == DC - 1)),
                                 reads=[wb1, bxv], writes=[bps[k]])

                    def v_fin():
                        S.op("act", lambda e: e.copy(vbf[:], ps[:, bkv:bkv + 2, :].rearrange("p a n -> p (a n)")),
                             reads=[bps[bkv], bps[bkv + 1]], writes=[b_vbf])
                        held.difference_update((bkv, bkv + 1))

                    def g_all():
                        bkl = pb1()
                        for c in range(DC):
                            S.op("pe", lambda e: e.matmul(ps[:, bkl, 0:CH], lg1[:, c, 0:128], xg[:, c, :], start=(c == 0), stop=(c == DC - 1)),
                                 reads=[b_lg1, bxg], writes=[bps[bkl]])
                        for c in range(DC):
                            S.op("pe", lambda e: e.matmul(ps[0:32, bkl, CH:2 * CH], lg1[:, c, 128:160], xg[:, c, :], start=(c == 0), stop=(c == DC - 1)),
                                 reads=[b_lg1, bxg], writes=[bps[bkl]])
                        S.op("act", lambda e: e.activation(lor[:, 2, :], ps[:, bkl, 0:CH], AF.Sigmoid), reads=[bps[bkl]], writes=[b_lg])
                        S.op("act", lambda e: e.activation(lor[0:32, 3, :], ps[0:32, bkl, CH:2 * CH], AF.Sigmoid), reads=[bps[bkl]], writes=[b_lg])
                        bkg = pb2()
                        for hlf in range(2):
                            cs = slice(hlf * 512, (hlf + 1) * 512)
                            S.op("pe", lambda e: e.matmul(ps[:, bkg + hlf, :], lor[:, 2, :], lg2a[:, cs], start=True, stop=False),
                                 reads=[b_lg, b_lg2a], writes=[bps[bkg + hlf]])
                            S.op("pe", lambda e: e.matmul(ps[:, bkg + hlf, :], lor[0:32, 3, :], lg2b[:, cs], start=False, stop=True),
                                 reads=[b_lg, b_lg2b], writes=[bps[bkg + hlf]])
                        S.op("act", lambda e: e.copy(gbf, ps[:, bkg:bkg + 2, :].rearrange("p a n -> p (a n)")),
                             reads=[bps[bkg], bps[bkg + 1]], writes=[b_gbf])
                    chk('r7')
                    yield
                    Qbuf = [lambda h: Q0[:, h, :], lambda h: Qt_[:, h, :]]
                    QbufG = [lambda hs: Q0[:, hs, :], lambda hs: Qt_[:, hs, :]]

                    def v4(bk, n):
                        return ps[:, bk, 0:4 * n].rearrange("p (a t) -> p a t", t=n)
                    evq = 0
                    for gi in range(4):
                        hs = slice(gi * 4, gi * 4 + 4)
                        S.op("dve", lambda e: e.tensor_tensor(PX[1][:, hs, 1, :], MKB[:, hs, 2, :],
                                                               identb[:].unsqueeze(1).to_broadcast([128, 4, CH]), ALU.add),
                             reads=[b_MKB] + cst, writes=[bPX[1][gi]])
                        bkp = pb1()
                        bkq = pb1()
                        for j in range(4):
                            h = gi * 4 + j
                            S.op("pe", lambda e: e.matmul(ps[:, bkp, j * CH:(j + 1) * CH], Q0[:, h, :], MKB[:, h, 2, :], start=True, stop=True),
                                 reads=[b_MKB, b_Q0, bQb[0][gi]], writes=[bps[bkp]])
                        for j in range(4):
                            h = gi * 4 + j
                            S.op("pe", lambda e: e.matmul(ps[:, bkq, j * CH:(j + 1) * CH], MKB[:, h, 2, :], Q0[:, h, :], start=True, stop=True),
                                 reads=[b_MKB, b_Q0, bQb[0][gi]], writes=[bps[bkq]])
                        S.op("act", lambda e: e.copy(PX[1][:, hs, 0, :], v4(bkp, CH)), reads=[bps[bkp]], writes=[bPX[1][gi]])
                        S.op("dve", lambda e: e.tensor_copy(Qt_[:, hs, :], v4(bkq, CH)), reads=[bps[bkq]], writes=[bQb[1][gi]])
                        yield
                    for t_ in range(2, 8):
                        si, di = (t_ - 1) % 2, t_ % 2
                        if t_ <= 5:
                            v_slice(t_ - 2)
                        elif t_ == 6:
                            v_fin()
                            g_all()
                        for gi in range(4):
                            hs = slice(gi * 4, gi * 4 + 4)
                            rdp = [bPX[si][gi], bQb[si][gi]]
                            if t_ <= 5:
                                bk2 = pb2()
                                for j in range(4):
                                    h = gi * 4 + j
                                    S.op("pe", lambda e: e.matmul(ps[:, bk2 + j // 2, (j % 2) * 256:(j % 2) * 256 + 256], Qbuf[si](h),
                                                                  PX[si][:, h, :, :].rearrange("p a t -> p (a t)"), start=True, stop=True),
                                         reads=rdp, writes=[bps[bk2], bps[bk2 + 1]])
                                pv = ps[:, bk2:bk2 + 2, :].rearrange("p a (h k t) -> p (a h) k t", k=2, t=CH)
                                S.op("act", lambda e: e.copy(PX[di][:, hs, 0, :], pv[:, :, 0, :]), reads=[bps[bk2], bps[bk2 + 1]], writes=[bPX[di][gi]])
                                S.op("dve", lambda e: e.tensor_tensor(PX[di][:, hs, 1, :], pv[:, :, 1, :], PX[si][:, hs, 1, :], ALU.add),
                                     reads=[bps[bk2], bps[bk2 + 1], bPX[si][gi]], writes=[bPX[di][gi]])
                            else:
                                bkx = pb1()
                                for j in range(4):
                                    h = gi * 4 + j
                                    S.op("pe", lambda e: e.matmul(ps[:, bkx, j * CH:(j + 1) * CH], Qbuf[si](h), PX[si][:, h, 1, :], start=True, stop=True),
                                         reads=rdp, writes=[bps[bkx]])
                                S.op("dve", lambda e: e.tensor_tensor(PX[di][:, hs, 1, :], v4(bkx, CH), PX[si][:, hs, 1, :], ALU.add),
                                     reads=[bps[bkx], bPX[si][gi]], writes=[bPX[di][gi]])
                            if t_ <= 6:
                                bkq = pb1()
                                for j in range(4):
                                    h = gi * 4 + j
                                    S.op("pe", lambda e: e.matmul(ps[:, bkq, j * CH:(j + 1) * CH], PX[si][:, h, 0, :], Qbuf[si](h), start=True, stop=True),
                                         reads=rdp, writes=[bps[bkq]])
                                evq += 1
                                S.op("act" if evq % 2 else "dve",
                                     (lambda e: e.copy(QbufG[di](hs), v4(bkq, CH))) if evq % 2 else (lambda e: e.tensor_copy(QbufG[di](hs), v4(bkq, CH))),
                                     reads=[bps[bkq]], writes=[bQb[di][gi]])
                            yield
                    chk('r8')
                    yield

                def genB(q):
                    q0 = q * CH
                    elc, b_elc = elc2[q % 2]
                    sbon, b_sbon = sbon2[q % 2]
                    bkR = pb2()
                    for h in range(NH):
                        k = bkR + h // 8
                        cs = slice((h % 8) * 64, (h % 8) * 64 + 64)
                        S.op("pe", lambda e: e.matmul(ps[:, k, cs], MKB[:, h, 0, :], vbf[:, h * 64:(h + 1) * 64], start=True, stop=False),
                             reads=[b_MKB, b_vbf], writes=[bps[k]])
                        S.op("pe", lambda e: e.matmul(ps[:, k, cs], arfm[:, h // 2, 0, :], hbd_t[:, h // 2, (h % 2) * 64:(h % 2) * 64 + 64],
                                                      start=False, stop=True),
                             reads=[b_arfm, b_hbd], writes=[bps[k]])
                    S.op("act", lambda e: e.copy(rhsbf[:], ps[:, bkR:bkR + 2, :].rearrange("p a n -> p (a n)")),
                         reads=[bps[bkR], bps[bkR + 1]], writes=[b_rhsbf])
                    yield
                    bkU = pb2()
                    for h in range(NH):
                        k = bkU + h // 8
                        cs = slice((h % 8) * 64, (h % 8) * 64 + 64)
                        S.op("pe", lambda e: e.matmul(ps[:, k, cs], PX[1][:, h, 1, :], rhsbf[:, h * 64:(h + 1) * 64], start=True, stop=True),
                             reads=bPX[1] + [b_rhsbf], writes=[bps[k]])
                    S.op("dve", lambda e: e.tensor_copy(ubf[:], ps[:, bkU:bkU + 2, :].rearrange("p a n -> p (a n)")),
                         reads=[bps[bkU], bps[bkU + 1]], writes=[b_ubf])
                    yield
                    bkY = pb2()
                    for h in range(NH):
                        k = bkY + h // 8
                        cs = slice((h % 8) * 64, (h % 8) * 64 + 64)
                        S.op("pe", lambda e: e.matmul(ps[:, k, cs], MKB[:, h, 3, :], ubf[:, h * 64:(h + 1) * 64], start=True, stop=False),
                             reads=[b_MKB, b_ubf], writes=[bps[k]])
                        S.op("pe", lambda e: e.matmul(ps[:, k, cs], MKB[:, h, 1, :], vbf[:, h * 64:(h + 1) * 64], start=False, stop=False),
                             reads=[b_MKB, b_vbf], writes=[bps[k]])
                        S.op("pe", lambda e: e.matmul(ps[:, k, cs], arfm[:, h // 2, 1, :], hbd_t[:, h // 2, (h % 2) * 64:(h % 2) * 64 + 64],
                                                      start=False, stop=True),
                             reads=[b_arfm, b_hbd], writes=[bps[k]])
                    yps = ps[:, bkY:bkY + 2, :].rearrange("p a n -> p (a n)")
                    bY = [bps[bkY], bps[bkY + 1]]
                    held.update((bkY, bkY + 1))
                    yield
                    bkH = pb2()
                    for p_ in range(8):
                        k = bkH + p_ // 4
                        cs = slice((p_ % 4) * CH, (p_ % 4) * CH + CH)
                        S.op("pe", lambda e: e.matmul(ps[:, k, cs], khat[:, p_ * CH:(p_ + 1) * CH], vbf[:, p_ * CH:(p_ + 1) * CH], start=True, stop=False),
                             reads=[b_khat, b_vbf], writes=[bps[k]])
                        S.op("pe", lambda e: e.matmul(ps[:, k, cs], bhat[:, p_ * CH:(p_ + 1) * CH], ubf[:, p_ * CH:(p_ + 1) * CH], start=False, stop=True),
                             reads=[b_bhat, b_ubf], writes=[bps[k]])
                    dH = ps[:, bkH:bkH + 2, :].rearrange("p a (c n) -> p (a c) n", n=CH)
                    for hh in range(2):
                        pr = slice(hh * 64, hh * 64 + 64)
                        S.op("dve", lambda e: e.tensor_tensor(hf_t[pr], hf_t[pr], dH[pr, :, hh * 64:hh * 64 + 64], ALU.add),
                             reads=[b_hf, bps[bkH], bps[bkH + 1]], writes=[b_hf])
                    S.op("dve", lambda e: e.tensor_tensor(hf_t[:], hf_t[:], elc[:].unsqueeze(2).to_broadcast([128, 8, 64]), ALU.mult),
                         reads=[b_hf, b_elc], writes=[b_hf])
                    for hh in range(2):
                        pr = slice(hh * 64, hh * 64 + 64)
                        S.op("act", lambda e: e.copy(hbd_t[pr, :, hh * 64:hh * 64 + 64], hf_t[pr]), reads=[b_hf], writes=[b_hbd])
                    chk('r9')
                    yield
                    S.op("dve", lambda e: e.reduce_sum(g1s[:], h3(yps), axis=AX.X), reads=bY, writes=[b_g1s])
                    S.op("act", lambda e: e.activation(E1[:], yps, AF.Square), reads=bY, writes=[b_E1])
                    S.op("dve", lambda e: e.reduce_sum(g2s[:], h3(E1[:]), axis=AX.X), reads=[b_E1], writes=[b_g2s])
                    S.op("dve", lambda e: e.tensor_scalar(g1s[:], g1s[:], 1.0 / 64, None, ALU.mult), reads=[b_g1s], writes=[b_g1s])
                    S.op("dve", lambda e: e.tensor_tensor(g3s[:], g1s[:], g1s[:], ALU.mult), reads=[b_g1s], writes=[b_g3s])
                    S.op("dve", lambda e: e.scalar_tensor_tensor(g3s[:], g2s[:], 1.0 / 64, g3s[:], ALU.mult, ALU.subtract),
                         reads=[b_g2s, b_g3s], writes=[b_g3s])
                    S.op("act", lambda e: e.activation(g3s[:], g3s[:], AF.Ln, bias=lnb_rms[:, 3:4], scale=1.0), reads=[b_g3s] + cst, writes=[b_g3s])
                    S.op("act", lambda e: e.activation(g3s[:], g3s[:], AF.Exp, scale=-0.5), reads=[b_g3s], writes=[b_g3s])
                    S.op("dve", lambda e: e.tensor_tensor(h3(E1[:]), h3(yps), bc16(g1s[:]), ALU.subtract), reads=bY + [b_g1s], writes=[b_E1])
                    held.difference_update((bkY, bkY + 1))
                    yield
                    S.op("dve", lambda e: e.tensor_tensor(h3(E1[:]), h3(E1[:]), bc16(g3s[:]), ALU.mult), reads=[b_E1, b_g3s], writes=[b_E1])
                    S.op("dve", lambda e: e.tensor_tensor(E1[:], E1[:], rows[:, 3, :], ALU.mult), reads=[b_E1, b_rows], writes=[b_E1])
                    S.op("dve", lambda e: e.tensor_tensor(E1[:], E1[:], rows[:, 4, :], ALU.add), reads=[b_E1, b_rows], writes=[b_E1])
                    S.op("dve", lambda e: e.tensor_tensor(h3(tA[:]), h3(vbf[:]), bc16(sbon[:]), ALU.mult), reads=[b_vbf, b_sbon], writes=[b_tA])
                    S.op("dve", lambda e: e.tensor_tensor(E1[:], E1[:], tA[:], ALU.add), reads=[b_E1, b_tA], writes=[b_E1])
                    S.op("dve", lambda e: e.tensor_tensor(tA[:], E1[:], gbf, ALU.mult), reads=[b_E1, b_gbf], writes=[b_tA])
                    transpose8(tA, b_tA, yg[:, :, q0:q0 + CH], b_yg)
                def run(g):
                    for _ in g:
                        pass

                def interleave(ga, gb):
                    alive_a, alive_b = True, True
                    while alive_a or alive_b:
                        if alive_b:
                            try:
                                next(gb)
                            except StopIteration:
                                alive_b = False
                        if alive_a:
                            try:
                                if next(ga) == "barrier" and alive_b:
                                    run(gb)
                                    alive_b = False
                            except StopIteration:
                                alive_a = False
                run(genA(0))
                for q in range(4):
                    if q < 3 and os.environ.get("KNOIL") != "1":
                        interleave(genA(q + 1), genB(q))
                    else:
                        run(genB(q))
                        if q < 3:
                            run(genA(q + 1))
                proj_fm_out(sc_bwout, 4, yg, b_yg, lambda ec, bk: resid_add(ec, bk, l, 0, b))

            for b in range(NB):
                hf_t, b_hf = Hf[b]
                hbd_t, b_hbd = Hbd[b]
                S.op("pool", lambda e: e.memset(hf_t[:], 0.0), writes=[b_hf])
                S.op("pool", lambda e: e.memset(hbd_t[:], 0.0), writes=[b_hbd])
                for ti in range(NT):
                    t0 = ti * TT
                    for q in range(4):
                        xt_, b_xt = tmf[q]
                        S.dma("sp", xt_[:], x_d[b, t0 + q * CH: t0 + (q + 1) * CH, :], writes=[b_xt])
                    for half in range(2):
                        for c4 in range(4):
                            c = half * 4 + c4
                            bk = pb1()
                            for q in range(4):
                                xt_, b_xt = tmf[q]
                                S.op("pe", lambda e: e.transpose(ps[:, bk, q * CH:(q + 1) * CH], xt_[:, c * CH:(c + 1) * CH], identf[:]),
                                     reads=[b_xt] + cst, writes=[bps[bk]])
                            S.op("act" if c % 2 else "dve",
                                 (lambda e: e.copy(xres[:, c, :], ps[:, bk, :])) if c % 2 else (lambda e: e.tensor_copy(xres[:, c, :], ps[:, bk, :])),
                                 reads=[bps[bk]], writes=[b_x])
                    chk('load', (xres[:].rearrange('p c t -> p (c t)'), b_x, 4096))
                    sgu(b)
                    chk('sgu', (xres[:].rearrange('p c t -> p (c t)'), b_x, 4096))
                    mlp(0, b)
                    chk('mlp0', (xres[:].rearrange('p c t -> p (c t)'), b_x, 4096))
                    pump(1000)
                    rwkv(b, ti == 0)
                    chk('rwkv', (xres[:].rearrange('p c t -> p (c t)'), b_x, 4096))
                    mlp(1, b)
                    rms_mod(None, 0, b)
                    yf, b_yf = slab[0]
                    yff = yf[:].rearrange("p c t -> p (c t)").bitcast(F32).rearrange("p (c t) -> p c t", t=TT // 2)
                    for hlf in range(2):
                        tsl = slice(hlf * 256, (hlf + 1) * 256)
                        for c in range(DC):
                            S.op("dve", lambda e: e.scalar_tensor_tensor(yff[:, c, :], xres[:, c, tsl], fgf[:, c:c + 1], rstd[:, tsl],
                                                                         ALU.mult, ALU.mult),
                                 reads=[b_x, b_fgf, b_rstd], writes=[b_yf])
                        for q2 in range(2):
                            q = hlf * 2 + q2
                            ot, b_ot = tmf[4 + q % 2]
                            for c4 in range(2):
                                bk = pb1()
                                for cc in range(4):
                                    c = c4 * 4 + cc
                                    S.op("pe", lambda e: e.transpose(ps[:, bk, cc * CH:(cc + 1) * CH], yff[:, c, q2 * CH:(q2 + 1) * CH], identf[:]),
                                         reads=[b_yf] + cst, writes=[bps[bk]])
                                S.op("act" if c4 else "dve",
                                     (lambda e: e.copy(ot[:, c4 * 512:(c4 + 1) * 512], ps[:, bk, :])) if c4 else
                                     (lambda e: e.tensor_copy(ot[:, c4 * 512:(c4 + 1) * 512], ps[:, bk, :])),
                                     reads=[bps[bk]], writes=[b_ot])
                            S.dma("pool", out_d[b, t0 + q * CH: t0 + (q + 1) * CH, :], ot[:], reads=[b_ot])
      except _Stop:
          pass
      for e_ in ("sp", "pool"):
          S.finish(e_)
    return nc


b_outd = Buf("out_dram")

_NC_CACHE = {}


def kernel(**inputs):
    n = 8
    x = np.ascontiguousarray(inputs["x"], dtype=np.float32)
    B, T, _ = x.shape
    NB = B // n
    key = (T, NB)
    if key not in _NC_CACHE:
        _NC_CACHE[key] = build(T, NB)
    nc = _NC_CACHE[key]
    shared = {}
    for k, v in inputs.items():
        if k in ("x", "c"):
            continue
        a = np.ascontiguousarray(v, dtype=np.float32)
        if k.startswith("a_") or k.startswith("b_"):
            a = a[0]
        if k in ("a_b_s", "b_r_k"):
            a = a.reshape(-1)
        shared[k] = np.ascontiguousarray(a)
    c = np.ascontiguousarray(inputs["c"], dtype=np.float32)
    in_maps = []
    for i in range(n):
        m = dict(shared)
        m["x"] = np.ascontiguousarray(x[i * NB:(i + 1) * NB])
        m["c"] = np.ascontiguousarray(c[i * NB:(i + 1) * NB])
        in_maps.append(m)
    res = run_bass_kernel_spmd(nc, in_maps, core_ids=list(range(n)))
    return np.concatenate([r["out"] for r in res.results], axis=0).astype(np.float32)
```
